# Optimizing a Trainium2 kernel written in Bass

```python
import jax
import jax.numpy as jnp
from jax import lax
import numpy as np

D_MODEL = 2048
BATCH = 4
SEQ = 4096
DEPTH = 4

GRID_W = 64
CTX_LEN = 256
N_MIXERS = 3
RET_HEADS = 8
RET_QK_DIM = D_MODEL // RET_HEADS
RET_V_DIM = 2 * RET_QK_DIM
RET_CHUNK = 128
ATT_HEAD_DIM = 128
ATT_HEADS = D_MODEL // ATT_HEAD_DIM
ATT_KV_HEADS = 4
ATT_GROUPS = ATT_HEADS // ATT_KV_HEADS
Q_BLOCK = 128
ROPE_BASE = 10000.0
CONV_WIDTH = 31
FFN_DIM = ((8 * D_MODEL // 3 + 255) // 256) * 256
FFN_CONV_WIDTH = 3
N_MOD = 6
RMS_EPS = 1e-6
LN_EPS = 1e-5

kernel_name = 'hybrid_retention_gqa_conformer_dit'


def rmsnorm(x, w):
    xf = x.astype(jnp.float32)
    y = xf * lax.rsqrt(jnp.mean(xf * xf, axis=-1, keepdims=True) + RMS_EPS)
    return (y * w.astype(jnp.float32)).astype(x.dtype)


def layernorm(x, w, b):
    xf = x.astype(jnp.float32)
    mu = jnp.mean(xf, axis=-1, keepdims=True)
    var = jnp.mean(jnp.square(xf - mu), axis=-1, keepdims=True)
    y = (xf - mu) * lax.rsqrt(var + LN_EPS)
    return (y * w.astype(jnp.float32) + b.astype(jnp.float32)).astype(x.dtype)


def depthwise_conv(x, w, b):
    y = lax.conv_general_dilated(x, w[:, None, :].astype(x.dtype), window_strides=(1,), padding='SAME',
                                 dimension_numbers=('NWC', 'WIO', 'NWC'), feature_group_count=x.shape[-1])
    return y + b.astype(x.dtype)


def axial_rope(x, row, col):
    n, hd = x.shape[1], x.shape[-1]
    half = hd // 2
    nf = half // 2
    inv_freq = ROPE_BASE ** (-jnp.arange(nf, dtype=jnp.float32) / nf)
    shape = (1, n) + (1,) * (x.ndim - 3) + (nf,)

    def rot(xa, pos):
        ang = pos.astype(jnp.float32)[:, None] * inv_freq[None, :]
        cos = jnp.cos(ang).reshape(shape)
        sin = jnp.sin(ang).reshape(shape)
        x1 = xa[..., :nf].astype(jnp.float32)
        x2 = xa[..., nf:].astype(jnp.float32)
        return jnp.concatenate([x1 * cos - x2 * sin, x2 * cos + x1 * sin], axis=-1)

    out = jnp.concatenate([rot(x[..., :half], row), rot(x[..., half:], col)], axis=-1)
    return out.astype(x.dtype)


def sdpa_block(q, k, v):
    s = jnp.einsum('bqkgd,bnkd->bkgqn', q, k).astype(jnp.float32) * (q.shape[-1] ** -0.5)
    p = jax.nn.softmax(s, axis=-1).astype(v.dtype)
    return jnp.einsum('bkgqn,bnkd->bqkgd', p, v)


def attention_mixer(hl, hc, wq, wkv, q_gain, k_gain, wo, with_ctx):
    B, S, _ = hl.shape
    rows = S // GRID_W
    row = jnp.repeat(jnp.arange(rows), GRID_W)
    col = jnp.tile(jnp.arange(GRID_W), rows)

    def project(h):
        n = h.shape[1]
        q = rmsnorm((h @ wq).reshape(B, n, ATT_KV_HEADS, ATT_GROUPS, ATT_HEAD_DIM), q_gain)
        kv = (h @ wkv).reshape(B, n, 2, ATT_KV_HEADS, ATT_HEAD_DIM)
        return q, rmsnorm(kv[:, :, 0], k_gain), kv[:, :, 1]

    ql, kl, vl = project(hl)
    qc, kc, vc = project(hc)
    ql = axial_rope(ql, row, col)
    kl = axial_rope(kl, row, col)
    k_all = jnp.concatenate([kc, kl], axis=1)
    v_all = jnp.concatenate([vc, vl], axis=1)
    nb = S // Q_BLOCK
    qb = ql.reshape(B, nb, Q_BLOCK, ATT_KV_HEADS, ATT_GROUPS, ATT_HEAD_DIM).swapaxes(0, 1)
    yl = lax.map(lambda qi: sdpa_block(qi, k_all, v_all), qb)
    yl = yl.swapaxes(0, 1).reshape(B, S, D_MODEL) @ wo
    yc = sdpa_block(qc, kc, vc).reshape(B, hc.shape[1], D_MODEL) @ wo if with_ctx else None
    return yl, yc


def retention_chunkwise(q, k, v, log_gamma, state0):
    B, H, N, dk = q.shape
    dv = v.shape[-1]
    C = RET_CHUNK
    nc = N // C
    pos = jnp.arange(C, dtype=jnp.float32)
    diff = pos[:, None] - pos[None, :]
    lg = log_gamma[:, None]
    intra = jnp.where(diff >= 0, jnp.exp(log_gamma[:, None, None] * jnp.maximum(diff, 0.0)), 0.0)
    q_dec = jnp.exp(lg * (pos + 1.0))[None, :, :, None]
    k_dec = jnp.exp(lg * (C - 1.0 - pos))[None, :, :, None]
    chunk_dec = jnp.exp(log_gamma * C)[None, :, None, None]

    def split(t):
        return t.reshape(B, H, nc, C, t.shape[-1]).transpose(2, 0, 1, 3, 4)

    def step(state, inp):
        qc, kc, vc = inp
        scores = jnp.einsum('bhnk,bhmk->bhnm', qc, kc) * intra
        out = jnp.einsum('bhnm,bhmv->bhnv', scores, vc) + jnp.einsum('bhnk,bhkv->bhnv', qc, state) * q_dec
        state = state * chunk_dec + jnp.einsum('bhmk,bhmv->bhkv', kc * k_dec, vc)
        return state, out

    _, outs = lax.scan(step, state0, (split(q), split(k), split(v)))
    return outs.transpose(1, 2, 0, 3, 4).reshape(B, H, N, dv)


def retention_bidirectional(q, k, v, log_gamma, state_f, state_b):
    fwd = retention_chunkwise(q, k, v, log_gamma[0], state_f)
    rev = lambda t: jnp.flip(t, axis=2)
    bwd = rev(retention_chunkwise(rev(q), rev(k), rev(v), log_gamma[1], state_b))
    return fwd + bwd


def retention_output(y, h, wg, wo, gn_w):
    mu = jnp.mean(y, axis=-1, keepdims=True)
    var = jnp.mean(jnp.square(y - mu), axis=-1, keepdims=True)
    y = (y - mu) * lax.rsqrt(var + LN_EPS)
    B, H, N, dv = y.shape
    y = y.transpose(0, 2, 1, 3).reshape(B, N, H * dv) * gn_w.astype(jnp.float32)
    return (jax.nn.silu(h @ wg) * y.astype(h.dtype)) @ wo


def retention_mixer(hl, hc, wq, wk, wv, wg, wo, decay, gn_w, with_ctx):
    log_gamma = -jnp.exp(decay.astype(jnp.float32))
    k_scale = RET_QK_DIM ** -0.5

    def heads(h, w, dh):
        B, n, _ = h.shape
        return (h @ w).reshape(B, n, RET_HEADS, dh).transpose(0, 2, 1, 3).astype(jnp.float32)

    kc = heads(hc, wk, RET_QK_DIM) * k_scale
    vc = heads(hc, wv, RET_V_DIM)
    lc = hc.shape[1]
    m = jnp.arange(lc, dtype=jnp.float32)
    w_f = jnp.exp(log_gamma[0][:, None] * (lc - 1.0 - m))[None, :, :, None]
    w_b = jnp.exp(log_gamma[1][:, None] * m)[None, :, :, None]
    state_f = jnp.einsum('bhmk,bhmv->bhkv', kc * w_f, vc)
    state_b = jnp.einsum('bhmk,bhmv->bhkv', kc * w_b, vc)
    ql = heads(hl, wq, RET_QK_DIM)
    kl = heads(hl, wk, RET_QK_DIM) * k_scale
    vl = heads(hl, wv, RET_V_DIM)
    yl = retention_output(retention_bidirectional(ql, kl, vl, log_gamma, state_f, state_b), hl, wg, wo, gn_w)
    yc = None
    if with_ctx:
        zero = jnp.zeros_like(state_f)
        qc = heads(hc, wq, RET_QK_DIM)
        yc = retention_output(retention_bidirectional(qc, kc, vc, log_gamma, zero, zero), hc, wg, wo, gn_w)
    return yl, yc


def conformer_conv(h, w1, b1, dw, dw_b, ln_w, ln_b, w2, b2):
    a = h @ w1 + b1
    a = a[..., :D_MODEL] * jax.nn.sigmoid(a[..., D_MODEL:])
    a = layernorm(depthwise_conv(a, dw, dw_b), ln_w, ln_b)
    return jax.nn.silu(a) @ w2 + b2


def conv_ffn(h, w_gate, w_up, dw, dw_b, w_down):
    g = depthwise_conv(h @ w_gate, dw, dw_b)
    return (jax.nn.silu(g) * (h @ w_up)) @ w_down


def setup_inputs(seed: int = 0) -> dict:
    key = jax.random.key(seed)
    ks = iter(jax.random.split(key, 48))
    n_a = (DEPTH + N_MIXERS - 1) // N_MIXERS
    n_b = (DEPTH + N_MIXERS - 2) // N_MIXERS
    n_c = DEPTH // N_MIXERS
    D = D_MODEL
    HV = RET_HEADS * RET_V_DIM

    def normal(shape):
        return jax.random.normal(next(ks), shape, jnp.float32)

    def dense(shape, fan_in, g=1.0):
        return normal(shape) * (g * fan_in ** -0.5)

    def gain(shape):
        return 1.0 + 0.02 * normal(shape)

    def bias(shape):
        return 0.02 * normal(shape)

    base_decay = jnp.log(-jnp.log1p(-jnp.exp2(-5.0 - jnp.arange(RET_HEADS, dtype=jnp.float32))))
    return {
        'x': normal((BATCH, SEQ, D)),
        'c': normal((BATCH, D)),
        'ctx': normal((BATCH, CTX_LEN, D)),
        'c_ctx': normal((D,)),
        'mod_w': dense((DEPTH, D, N_MOD * D), D, 0.5),
        'mod_b': bias((DEPTH, N_MOD * D)),
        'norm1_w': gain((DEPTH, D)),
        'norm2_w': gain((DEPTH, D)),
        'ffn_w_gate': dense((DEPTH, D, FFN_DIM), D),
        'ffn_w_up': dense((DEPTH, D, FFN_DIM), D),
        'ffn_dw': dense((DEPTH, FFN_CONV_WIDTH, FFN_DIM), FFN_CONV_WIDTH),
        'ffn_dw_b': bias((DEPTH, FFN_DIM)),
        'ffn_w_down': dense((DEPTH, FFN_DIM, D), FFN_DIM),
        'ret_wq': dense((n_a, D, RET_HEADS * RET_QK_DIM), D),
        'ret_wk': dense((n_a, D, RET_HEADS * RET_QK_DIM), D),
        'ret_wv': dense((n_a, D, HV), D),
        'ret_wg': dense((n_a, D, HV), D),
        'ret_wo': dense((n_a, HV, D), HV),
        'ret_decay': base_decay + 0.1 * normal((n_a, 2, RET_HEADS)),
        'ret_gn_w': gain((n_a, HV)),
        'att_wq': dense((n_b, D, ATT_HEADS * ATT_HEAD_DIM), D),
        'att_wkv': dense((n_b, D, 2 * ATT_KV_HEADS * ATT_HEAD_DIM), D),
        'att_q_gain': gain((n_b, ATT_HEAD_DIM)),
        'att_k_gain': gain((n_b, ATT_HEAD_DIM)),
        'att_wo': dense((n_b, ATT_HEADS * ATT_HEAD_DIM, D), D),
        'cnv_w1': dense((n_c, D, 2 * D), D),
        'cnv_b1': bias((n_c, 2 * D)),
        'cnv_dw': dense((n_c, CONV_WIDTH, D), CONV_WIDTH),
        'cnv_dw_b': bias((n_c, D)),
        'cnv_ln_w': gain((n_c, D)),
        'cnv_ln_b': bias((n_c, D)),
        'cnv_w2': dense((n_c, D, D), D),
        'cnv_b2': bias((n_c, D)),
    }


def reference(x, c, ctx, c_ctx, mod_w, mod_b, norm1_w, norm2_w, ffn_w_gate, ffn_w_up, ffn_dw, ffn_dw_b,
              ffn_w_down, ret_wq, ret_wk, ret_wv, ret_wg, ret_wo, ret_decay, ret_gn_w, att_wq, att_wkv,
              att_q_gain, att_k_gain, att_wo, cnv_w1, cnv_b1, cnv_dw, cnv_dw_b, cnv_ln_w, cnv_ln_b, cnv_w2,
              cnv_b2):
    xl, xc = x, ctx
    silu_c = jax.nn.silu(c)
    silu_cc = jax.nn.silu(c_ctx)
    for i in range(DEPTH):
        with_ctx = i < DEPTH - 1
        mod_l = (silu_c @ mod_w[i] + mod_b[i])[:, None, :]
        mod_c = silu_cc @ mod_w[i] + mod_b[i]
        sh1, sc1, g1, sh2, sc2, g2 = jnp.split(mod_l, N_MOD, axis=-1)
        csh1, csc1, cg1, csh2, csc2, cg2 = jnp.split(mod_c, N_MOD, axis=-1)
        hl = rmsnorm(xl, norm1_w[i]) * (1.0 + sc1) + sh1
        hc = rmsnorm(xc, norm1_w[i]) * (1.0 + csc1) + csh1
        kind, j = i % N_MIXERS, i // N_MIXERS
        if kind == 0:
            yl, yc = retention_mixer(hl, hc, ret_wq[j], ret_wk[j], ret_wv[j], ret_wg[j], ret_wo[j],
                                     ret_decay[j], ret_gn_w[j], with_ctx)
        elif kind == 1:
            yl, yc = attention_mixer(hl, hc, att_wq[j], att_wkv[j], att_q_gain[j], att_k_gain[j], att_wo[j],
                                     with_ctx)
        else:
            cp = (cnv_w1[j], cnv_b1[j], cnv_dw[j], cnv_dw_b[j], cnv_ln_w[j], cnv_ln_b[j], cnv_w2[j], cnv_b2[j])
            yl = conformer_conv(hl, *cp)
            yc = conformer_conv(hc, *cp) if with_ctx else None
        fp = (ffn_w_gate[i], ffn_w_up[i], ffn_dw[i], ffn_dw_b[i], ffn_w_down[i])
        xl = xl + g1 * yl
        xl = xl + g2 * conv_ffn(rmsnorm(xl, norm2_w[i]) * (1.0 + sc2) + sh2, *fp)
        if with_ctx:
            xc = xc + cg1 * yc
            xc = xc + cg2 * conv_ffn(rmsnorm(xc, norm2_w[i]) * (1.0 + csc2) + csh2, *fp)
    return xl
```

```python
import contextlib
import numpy as np
import concourse.bass as bass
import concourse.mybir as mybir
from concourse.bass_utils import run_bass_kernel_spmd

F32 = mybir.dt.float32
BF16 = mybir.dt.bfloat16
AF = mybir.ActivationFunctionType
ALU = mybir.AluOpType

ENGS = ('sync', 'act', 'dve', 'pool', 'pe')
BLOCK_NAME = {'sync': 'sync', 'act': 'scalar', 'dve': 'vector', 'pool': 'gpsimd', 'pe': 'tensor'}


class Tile:
    __slots__ = ('name', 'ap', 'w', 'r')

    def __init__(self, name, ap=None):
        self.name = name
        self.ap = ap
        self.w = None
        self.r = {}


class Sched:
    def __init__(self, nc, ndma=(('sync', 30), ('pool', 30), ('act', 8))):
        self.nc = nc
        self.stack = contextlib.ExitStack()
        self.q = {e: [] for e in ENGS}
        self.cnt = {e: 0 for e in ENGS}
        self.waited = {e: {} for e in ENGS}
        self.csem = {}
        self.dsems, self.dval, self.dn = {}, {}, {}
        self.ndma = ndma
        self.finals = []
        self.n_ops = 0
        self.cc_list = []

    def __enter__(self):
        self.stack.__enter__()
        nc = self.nc
        for e in ('act', 'dve', 'pool', 'pe'):
            self.csem[e] = self.stack.enter_context(nc.semaphore("c_" + e))
        for e, n in self.ndma:
            self.dsems[e] = [self.stack.enter_context(nc.semaphore("d_%s_%d" % (e, i))) for i in range(n)]
            self.dval[e] = [0] * n
            self.dn[e] = 0
        return self

    def __exit__(self, *a):
        return self.stack.__exit__(*a)

    def sbuf(self, name, shape, dtype):
        t = self.stack.enter_context(self.nc.sbuf_tensor(name, list(shape), dtype))
        return Tile(name, t[:])

    def psum(self, name, shape, dtype):
        t = self.stack.enter_context(self.nc.psum_tensor(name, list(shape), dtype))
        return Tile(name, t[:])

    def region(self, name):
        return Tile(name, None)

    def _deps(self, reads, writes):
        evs = []
        for t in reads:
            evs.append(t.w)
        for t in writes:
            evs.append(t.w)
            evs.extend(t.r.values())
        return evs

    def _need(self, eng, evs):
        w = self.waited[eng]
        best = {}
        pe_sem = self.csem['pe'].num
        for ev in evs:
            if ev is None:
                continue
            sem, val = ev
            k = sem.num
            if eng == 'pe' and k == pe_sem:
                continue
            if w.get(k, 0) >= val:
                continue
            if k not in best or best[k][1] < val:
                best[k] = (sem, val)
        out = []
        for k, (sem, val) in best.items():
            w[k] = val
            out.append((sem, val))
        return out

    def _record(self, ev, reads, writes):
        k = ev[0].num
        for t in reads:
            t.r[k] = ev
        for t in writes:
            t.w = ev
            t.r = {}

    def op(self, eng, fn, reads=(), writes=()):
        waits = self._need(eng, self._deps(reads, writes))
        self.cnt[eng] += 1
        sem = self.csem[eng]
        ev = (sem, self.cnt[eng])
        self.q[eng].append((waits, fn, sem, 1))
        self._record(ev, reads, writes)
        self.n_ops += 1
        return ev

    def dma(self, eng, out_ap, in_ap, reads=(), writes=(), final=False, **kw):
        evs = self._deps(reads, writes)
        n = len(self.dsems[eng])
        k = self.dn[eng] % n
        self.dn[eng] += 1
        sem = self.dsems[eng][k]
        prev = self.dval[eng][k]
        if prev:
            evs.append((sem, prev))
        waits = self._need(eng, evs)
        val = prev + 16
        self.dval[eng][k] = val
        ev = (sem, val)
        self.q[eng].append((waits, lambda e: e.dma_start(out=out_ap, in_=in_ap, **kw), sem, 16))
        self._record(ev, reads, writes)
        if final:
            self.finals.append(ev)
        self.n_ops += 1
        return ev


    def collective(self, groups, in_ap, out_ap, reads=(), writes=()):
        evs = self._deps(reads, writes)
        sem = self.stack.enter_context(self.nc.semaphore("cc_sem%d" % len(self.cc_list)))
        waits = self._need('pool', evs)
        ev = (sem, 1)
        self.cc_list.append(ev)
        self.q['pool'].append((waits, lambda e: e.collective_compute(
            "AllGather", ALU.bypass, replica_groups=groups, ins=[in_ap.opt()], outs=[out_ap.opt()]), sem, 1))
        self._record(ev, reads, writes)
        return ev

    def barrier(self):
        evs = []
        for e in ('act', 'dve', 'pool', 'pe'):
            if self.cnt[e]:
                evs.append((self.csem[e], self.cnt[e]))
        for e, _ in self.ndma:
            for s, v in zip(self.dsems[e], self.dval[e]):
                if v:
                    evs.append((s, v))
        evs.extend(self.cc_list)
        for e in ENGS:
            waits = self._need(e, evs)
            if waits:
                self.q[e].append((waits, None, None, 0))

    @staticmethod
    def _t(ts):
        return [t for t in ts if t is not None]

    def mm(self, pt, pap, lt, lap, rt, rap, start=True, stop=True):
        self.op('pe', lambda e: e.matmul(pap, lap, rap, start=start, stop=stop),
                reads=self._t((lt, rt)), writes=(pt,))

    def tr(self, pt, pap, it, iap, idap):
        self.op('pe', lambda e: e.transpose(pap, iap, idap), reads=self._t((it,)), writes=(pt,))

    def act(self, ot, oap, it, iap, func, bias=None, scale=None, rd=()):
        kw = {}
        if bias is not None:
            kw['bias'] = bias
        if scale is not None:
            kw['scale'] = scale
        self.op('act', lambda e: e.activation(oap, iap, func, **kw),
                reads=self._t((it,) + tuple(rd)), writes=(ot,))

    def tt(self, eng, ot, oap, at, aap, bt, bap, op):
        self.op(eng, lambda e: e.tensor_tensor(oap, aap, bap, op), reads=self._t((at, bt)), writes=(ot,))

    def ts(self, eng, ot, oap, it, iap, s1, s2, op0, op1=None, rd=()):
        if op1 is None:
            self.op(eng, lambda e: e.tensor_scalar(oap, iap, s1, s2, op0),
                    reads=self._t((it,) + tuple(rd)), writes=(ot,))
        else:
            self.op(eng, lambda e: e.tensor_scalar(oap, iap, s1, s2, op0, op1),
                    reads=self._t((it,) + tuple(rd)), writes=(ot,))

    def stt(self, ot, oap, at, aap, sc, bt, bap, op0, op1, rd=()):
        self.op('dve', lambda e: e.scalar_tensor_tensor(oap, aap, sc, bap, op0, op1),
                reads=self._t((at, bt) + tuple(rd)), writes=(ot,))

    def cp(self, eng, ot, oap, it, iap):
        if eng == 'act':
            self.op('act', lambda e: e.copy(oap, iap), reads=self._t((it,)), writes=(ot,))
        else:
            self.op(eng, lambda e: e.tensor_copy(oap, iap), reads=self._t((it,)), writes=(ot,))

    def ms(self, eng, t, ap, val):
        self.op(eng, lambda e: e.memset(ap, val), writes=(t,))

    def recip(self, ot, oap, it, iap):
        self.op('dve', lambda e: e.reciprocal(oap, iap), reads=self._t((it,)), writes=(ot,))

    def finish(self):
        evs = list(self.finals)
        for e in ('act', 'dve', 'pool', 'pe'):
            if self.cnt[e]:
                evs.append((self.csem[e], self.cnt[e]))
        for e, _ in self.ndma:
            for s, v in zip(self.dsems[e], self.dval[e]):
                if v:
                    evs.append((s, v))
        evs.extend(self.cc_list)
        waits = self._need('sync', evs)
        q = self.q
        q['sync'].append((waits, None, None, 0))
        with self.nc.Block() as block:
            for e in ENGS:
                lst = q[e]
                if not lst:
                    continue

                def body(eng, lst=lst):
                    for waits, fn, sem, inc in lst:
                        for s, v in waits:
                            eng.wait_ge(s, v)
                        if fn is not None:
                            fn(eng).then_inc(sem, inc)
                getattr(block, BLOCK_NAME[e])(body)


D = 2048
KC = 16
CTX = 256
FFN = 5632
FC = 44
HV = 4096
DEPTH = 4
RMS_EPS = 1e-6
LN_EPS = 1e-5
MIXER = {0: 'ret', 1: 'att', 2: 'cnv', 3: 'ret'}
ARENA = 47000


def _sv_layout():
    lay, n = {}, 0
    for name, w in (('c', 16), ('cc', 16), ('mod_b', 4 * 96), ('n1w', 64), ('n2w', 64),
                    ('fdw', 4 * 3 * FC), ('fdwb', 4 * FC), ('qg', 1), ('kg', 1), ('cb1', 32),
                    ('cdw', 31 * 16), ('cdwb', 16), ('clnw', 16), ('clnb', 16), ('cb2', 16), ('dec', 32), ('msk', 2), ('gnw', 64)):
        lay[name] = n
        n += w
    return lay, n


SVL, NSV = _sv_layout()
CSTL = {'ident': 0, 'ones': 128, 'perm': 256, 'd1': 384, 'd2': 512, 'u': 640, 'l': 768,
        'p1': 896, 'cmp': 897, 'c1p': 898, 'p0': 899}
NCST = 900


def _chunked(v):
    v = np.asarray(v, np.float32).reshape(-1, 128)
    return np.ascontiguousarray(v.T)


def pack_sv(inp, b, r=0):
    sv = np.zeros((128, NSV), np.float32)

    def put(name, arr):
        o = SVL[name]
        sv[:, o:o + arr.shape[1]] = arr
    ft = (2, 1, 0) if r else (0, 1, 2)
    ct = tuple(range(30, -1, -1)) if r else tuple(range(31))
    put('c', _chunked(inp['c'][b]))
    put('cc', _chunked(inp['c_ctx']))
    put('mod_b', np.concatenate([_chunked(inp['mod_b'][l]) for l in range(4)], axis=1))
    put('n1w', np.concatenate([_chunked(inp['norm1_w'][l]) for l in range(4)], axis=1))
    put('n2w', np.concatenate([_chunked(inp['norm2_w'][l]) for l in range(4)], axis=1))
    put('fdw', np.concatenate([_chunked(inp['ffn_dw'][l][t]) for l in range(4) for t in ft], axis=1))
    put('fdwb', np.concatenate([_chunked(inp['ffn_dw_b'][l]) for l in range(4)], axis=1))
    put('qg', _chunked(inp['att_q_gain'][0]))
    put('kg', _chunked(inp['att_k_gain'][0]))
    put('cb1', _chunked(inp['cnv_b1'][0]))
    put('cdw', np.concatenate([_chunked(inp['cnv_dw'][0][t]) for t in ct], axis=1))
    put('cdwb', _chunked(inp['cnv_dw_b'][0]))
    put('clnw', _chunked(inp['cnv_ln_w'][0]))
    put('clnb', _chunked(inp['cnv_ln_b'][0]))
    put('cb2', _chunked(inp['cnv_b2'][0]))
    dec = np.asarray(inp['ret_decay'], np.float32)
    if r:
        dec = dec[:, ::-1, :]
    put('dec', np.broadcast_to(np.ascontiguousarray(dec).reshape(1, 32), (128, 32)))
    put('gnw', np.concatenate([_chunked(inp['ret_gn_w'][j]) for j in range(2)], axis=1))
    msk = np.zeros((128, 2), np.float32)
    msk[:, 1 - r] = 1.0
    put('msk', msk)
    return sv


def make_consts():
    cst = np.zeros((128, NCST), np.float32)
    p = np.arange(128)
    cst[:, 0:128] = np.eye(128)
    cst[:, 128:256] = 1.0
    perm = np.where((p % 64) < 32, p + 32, p - 32)
    pm = np.zeros((128, 128), np.float32)
    pm[perm, p] = 1.0
    cst[:, 256:384] = pm
    m = p[:, None].astype(np.float32)
    n = p[None, :].astype(np.float32)
    cst[:, 384:512] = np.maximum(n - m, 0)
    cst[:, 512:640] = np.maximum(m - n, 0)
    cst[:, 640:768] = (n >= m)
    cst[:, 768:896] = (m >= n)
    cst[:, 896] = p + 1
    cst[:, 897] = 128 - p
    cst[:, 898] = 127 - p
    cst[:, 899] = p
    return cst


def make_rope(tpos):
    p = np.arange(128)
    t = np.asarray(tpos)
    row, col = t // 64, t % 64
    ii = p % 64
    fi = ii % 32
    inv = (10000.0 ** (-(np.arange(32, dtype=np.float32)) / 32.0)).astype(np.float32)
    pos = np.where((p < 64)[:, None], row[None, :], col[None, :]).astype(np.float32)
    ang = pos * inv[fi][:, None]
    cos = np.cos(ang).astype(np.float32)
    sin = np.sin(ang).astype(np.float32)
    sin = np.where((ii < 32)[:, None], -sin, sin).astype(np.float32)
    return np.ascontiguousarray(cos), np.ascontiguousarray(sin)


def _split(n, m):
    k = -(-n // m)
    base, rem = divmod(n, k)
    return [base + (1 if i < rem else 0) for i in range(k)]


class Arena:
    def __init__(self, tile):
        self.t = tile
        self.off = 0

    def reset(self):
        self.off = 0

    def alloc(self, name, fshape, dtype):
        n = 1
        for s in fshape:
            n *= s
        n32 = n if dtype == F32 else (n + 1) // 2
        assert self.off + n32 <= ARENA, (name, self.off, n32)
        ap = self.t.ap[:, self.off:self.off + n32]
        self.off += n32
        if dtype != F32:
            ap = ap.bitcast(dtype)[:, 0:n]
        if len(fshape) == 2:
            ap = ap.rearrange("p (a b) -> p a b", a=fshape[0])
        elif len(fshape) == 3:
            ap = ap.rearrange("p (a b c) -> p a b c", a=fshape[0], b=fshape[1])
        return Tile(name, ap)


def build(nlat=4, layers=(0, 1, 2, 3), do_mix=True, do_ffn=True):
    L = nlat * 512
    NT = CTX + L
    NCH = NT // 128
    nc = bass.Bass("TRN2", target_bir_lowering=False)

    def din(name, shape):
        return nc.dram_tensor(name, list(shape), F32, kind="ExternalInput").ap()

    def dscr(name, shape, dt):
        return nc.dram_tensor(name, list(shape), dt).ap()

    xT_in = din("xT", [D, NT])
    sv_d = din("sv", [128, NSV])
    cst_d = din("cst", [128, NCST])
    rcos = din("rope_cos", [128, L])
    rsin = din("rope_sin", [128, L])
    Wt = {}
    for l in layers:
        Wt['mw', l] = din("mw%d" % l, [D, 6 * D])
        if do_ffn:
            Wt['wg', l] = din("wg%d" % l, [D, FFN])
            Wt['wu', l] = din("wu%d" % l, [D, FFN])
            Wt['wd', l] = din("wd%d" % l, [FFN, D])
        if not do_mix:
            continue
        if MIXER[l] == 'ret':
            Wt['rq', l] = din("rq%d" % l, [D, D])
            Wt['rk', l] = din("rk%d" % l, [D, D])
            Wt['rv', l] = din("rv%d" % l, [D, HV])
            Wt['rg', l] = din("rg%d" % l, [D, HV])
            Wt['ro', l] = din("ro%d" % l, [HV, D])
            Wt['gn', l] = din("gn%d" % l, [1, HV])
        elif MIXER[l] == 'att':
            Wt['aq', l] = din("aq%d" % l, [D, D])
            Wt['akv', l] = din("akv%d" % l, [D, 1024])
            Wt['ao', l] = din("ao%d" % l, [D, D])
        else:
            Wt['c1', l] = din("c1_%d" % l, [D, 2 * D])
            Wt['c2', l] = din("c2_%d" % l, [D, D])
    outT = nc.dram_tensor("outT", [D, L], F32, kind="ExternalOutput").ap()
    XA = dscr("XA", [D, NT], F32)
    XB = dscr("XB", [D, NT], F32)
    QT = dscr("QT", [16, 128, NT], BF16)
    KT = dscr("KT", [16, 128, NT], BF16)
    KTOK = dscr("KTOK", [NCH, 128, D], BF16)
    VTOK = dscr("VTOK", [NCH, 128, HV], BF16)
    GTOK = dscr("GTOK", [NCH, 128, HV], BF16)
    ATS = dscr("ATS", [32, 128, NT], BF16)
    AFF = dscr("AFF", [FC, 128, NT], BF16)
    GT = dscr("GT", [16, 128, NT], F32)
    SBD = dscr("SBD", [8, NCH, 128, 1024], BF16)
    NLC = L // 128
    GROUPS = [[0, 1], [2, 3], [4, 5], [6, 7]]
    XH_in = dscr("XH_in", [D], F32)
    XH_out = dscr("XH_out", [2, D], F32)
    CH_in = dscr("CH_in", [16, 128, 15], F32)
    CH_out = dscr("CH_out", [2, 16, 128, 15], F32)
    KX_in = dscr("KX_in", [4, 128, L], BF16)
    KX_out = dscr("KX_out", [2, 4, 128, L], BF16)
    VX_in = dscr("VX_in", [NLC, 128, 512], BF16)
    VX_out = dscr("VX_out", [2, NLC, 128, 512], BF16)
    E_in = [dscr("E_in%d" % i, [4, 128, 1024], F32) for i in range(2)]
    E_out = [dscr("E_out%d" % i, [2, 4, 128, 1024], F32) for i in range(2)]

    def fm(ap2d):
        return ap2d.rearrange("(c p) n -> p c n", p=128)

    def hm(ap3d):
        return ap3d.rearrange("r p n -> p r n")

    XA3, XB3, OUT3 = fm(XA), fm(XB), fm(outT)

    mixer_tiles = [(0, 256, 1)] + [(CTX + 512 * i, 512, 0) for i in range(nlat)]
    ffn_tiles = [(0, 256, True, True, 1)]
    t0 = CTX
    sizes = _split(L, 510)
    for i, T in enumerate(sizes):
        ffn_tiles.append((t0, T, i == 0, 'partner' if i == len(sizes) - 1 else False, 0))
        t0 += T

    S = Sched(nc)
    with S:
        svt = S.sbuf("svt", [128, NSV], F32)
        cstt = S.sbuf("cstt", [128, NCST], F32)
        cb = S.sbuf("cb", [128, 384], BF16)
        modv = S.sbuf("modv", [128, 4, 96, 2], F32)
        AB = S.sbuf("AB", [128, 4, 2, 16, 2], F32)
        b2g = S.sbuf("b2g", [128, 16, 2], F32)
        sbf = S.sbuf("sbf", [128, 16, 2], BF16)
        epsT = S.sbuf("epsT", [128, 2], F32)
        lgT = S.sbuf("lgT", [128, 32], F32)
        hx = S.sbuf("hx", [128, 16], F32)
        hx2 = S.sbuf("hx2", [128, 2, 16], F32)
        arena_t = S.sbuf("arena", [128, ARENA], F32)
        A = Arena(arena_t)
        PS = [S.psum("ps%d" % i, [128, 512], F32) for i in range(8)]

        def sv(name, idx=0, n=1):
            o = SVL[name] + idx
            return svt.ap[:, o:o + n]

        def cst(name, n=128):
            o = CSTL[name]
            return cstt.ap[:, o:o + n]

        msk0, msk1 = sv('msk', 0), sv('msk', 1)

        def exchange(in_t, out_t):
            r_in, r_out = S.region('xin'), S.region('xout')
            S.collective(GROUPS, in_t, out_t, reads=[r_in], writes=[r_out])
            S.barrier()

        def halo_exchange(X3):
            S.dma('sync', XH_in.rearrange("(c p) -> p c", p=128), X3[:, :, NT - 1], allow_slow_non_contiguous=True)
            S.barrier()
            exchange(XH_in, XH_out)
            S.dma('sync', hx2.ap, XH_out.rearrange("r (c p) -> p r c", p=128), writes=[hx2], allow_slow_non_contiguous=True)
            S.ts('dve', hx, hx.ap, hx2, hx2.ap[:, 0, :], msk0, None, ALU.mult)
            S.stt(hx, hx.ap, hx2, hx2.ap[:, 1, :], msk1, hx, hx.ap, ALU.mult, ALU.add)
            S.barrier()

        ident_bf, ones_bf, perm_bf = cb.ap[:, 0:128], cb.ap[:, 128:256], cb.ap[:, 256:384]
        eps_rms, eps_ln = epsT.ap[:, 0:1], epsT.ap[:, 1:2]

        def mA(l, w, k, j):
            return AB.ap[:, l, w, k, j:j + 1]

        def mB(l, w, k, j):
            return modv.ap[:, l, (0 if w == 0 else 48) + k, j:j + 1]

        def mG(l, w, k, j):
            return modv.ap[:, l, (32 if w == 0 else 80) + k, j:j + 1]

        def run_stream(jobs, bufs):
            nb = len(bufs)
            n = len(jobs)

            def load(s):
                buf = bufs[s % nb]
                for (W2, kc, c0, ncols, off) in jobs[s]['loads']:
                    dst = buf.ap[:, off:off + kc * ncols].rearrange("p (c n) -> p c n", c=kc)
                    S.dma('pool', dst, W2[:, c0:c0 + ncols].rearrange("(c p) n -> p c n", p=128), writes=[buf])
            for s in range(min(nb - 1, n)):
                load(s)
            for s in range(n):
                if s + nb - 1 < n:
                    load(s + nb - 1)
                if jobs[s].get('pre') is not None:
                    jobs[s]['pre']()
                jobs[s]['fn'](bufs[s % nb])

        def wview(buf, off, kc, ncols):
            return buf.ap[:, off:off + kc * ncols].rearrange("p (c n) -> p c n", c=kc)

        def norm_mod(xt, W, l, w, j, hT, sq, sd, rs, tmp, hoff=0, bank=6):
            pst = PS[bank]
            for k in range(KC):
                q = sq[k % 2]
                S.act(q, q.ap[:, :W], xt, xt.ap[:, k, :W], AF.Square)
                S.mm(pst, pst.ap[:, :W], None, ones_bf, q, q.ap[:, :W], k == 0, k == KC - 1)
            S.act(sd, sd.ap[:, :W], pst, pst.ap[:, :W], AF.Sqrt, bias=eps_rms, scale=1.0 / D)
            S.recip(rs, rs.ap[:, :W], sd, sd.ap[:, :W])
            for k in range(KC):
                t = tmp[k % 2]
                S.tt('dve', t, t.ap[:, :W], xt, xt.ap[:, k, :W], rs, rs.ap[:, :W], ALU.mult)
                S.act(hT, hT.ap[:, k, hoff:hoff + W], t, t.ap[:, :W], AF.Identity, bias=mB(l, w, k, j), scale=mA(l, w, k, j))

        S.dma('sync', svt.ap, sv_d, writes=[svt])
        S.dma('sync', cstt.ap, cst_d, writes=[cstt])
        S.dma('sync', XA, xT_in)
        S.cp('dve', cb, cb.ap, cstt, cstt.ap[:, 0:384])
        S.ms('pool', epsT, epsT.ap[:, 0:1], RMS_EPS)
        S.ms('pool', epsT, epsT.ap[:, 1:2], LN_EPS)
        S.act(sbf, sbf.ap[:, :, 0], svt, sv('c', 0, 16), AF.Silu)
        S.act(sbf, sbf.ap[:, :, 1], svt, sv('cc', 0, 16), AF.Silu)
        S.act(lgT, lgT.ap, svt, sv('dec', 0, 32), AF.Exp)
        S.ts('dve', lgT, lgT.ap, lgT, lgT.ap, -1.0, None, ALU.mult)
        A.reset()
        wb = [A.alloc('wb%d' % i, (8192,), BF16) for i in range(3)]
        for l in layers:
            pm = PS[l % 2]
            jobs = []
            for s in range(24):
                def fn(buf, s=s, l=l, pm=pm):
                    wv = wview(buf, 0, KC, 512)
                    for f4 in range(4):
                        f = s * 4 + f4
                        for k in range(KC):
                            S.mm(pm, pm.ap[:, 2 * f:2 * f + 2], buf, wv[:, k, f4 * 128:(f4 + 1) * 128],
                                 sbf, sbf.ap[:, k, :], k == 0, k == KC - 1)
                jobs.append(dict(loads=[(Wt['mw', l], KC, s * 512, 512, 0)], fn=fn))
            run_stream(jobs, wb)
            pv = pm.ap[:, 0:192].rearrange("p (f j) -> p f j", j=2)
            for j in range(2):
                S.tt('dve', modv, modv.ap[:, l, :, j], pm, pv[:, :, j], svt, sv('mod_b', l * 96, 96), ALU.add)
            for w in range(2):
                base = 16 if w == 0 else 64
                nw = sv('n1w' if w == 0 else 'n2w', l * 16, 16)
                for j in range(2):
                    S.ts('dve', AB, AB.ap[:, l, w, :, j], modv, modv.ap[:, l, base:base + 16, j], 1.0, None, ALU.add)
                    S.tt('dve', AB, AB.ap[:, l, w, :, j], AB, AB.ap[:, l, w, :, j], svt, nw, ALU.mult)
            if MIXER[l] == 'cnv':
                for j in range(2):
                    S.tt('dve', b2g, b2g.ap[:, :, j], svt, sv('cb2', 0, 16), modv, modv.ap[:, l, 32:48, j], ALU.mult)
        S.barrier()

        LB = CTX + 2
        NE = LB + L + 2

        def ffn_phase(l, XI3, XO3, final):
            skip_ctx = (l == DEPTH - 1)
            wins = ([] if skip_ctx else [(0, CTX, 1, 1)]) + [(CTX + 512 * i, 512, 0, LB + 1 + 512 * i) for i in range(nlat)]
            A.reset()
            hT = A.alloc('hT', (KC, NE), BF16)
            xt = A.alloc('xt', (KC, 256), F32)
            wb = [A.alloc('wb%d' % i, (8192,), BF16) for i in range(2)]
            gseg = [A.alloc('gseg%d' % i, (NE,), F32) for i in range(2)]
            acc = A.alloc('acc', (NT,), F32)
            ast = [A.alloc('ast%d' % i, (NT,), BF16) for i in range(2)]
            sq = [A.alloc('sq%d' % i, (256,), BF16) for i in range(2)]
            tmp = [A.alloc('tmp%d' % i, (256,), F32) for i in range(2)]
            sd = A.alloc('sd', (256,), F32)
            rs = A.alloc('rs', (256,), F32)
            for zc in (0, CTX + 1, LB):
                S.ms('pool', hT, hT.ap[:, :, zc:zc + 1], 0.0)
                for g_ in gseg:
                    S.ms('pool', g_, g_.ap[:, zc:zc + 1], 0.0)
            pieces = ([] if skip_ctx else [(0, 256, 1, 1)]) + [(CTX + 256 * i, 256, 0, LB + 1 + 256 * i) for i in range(L // 256)]
            for (c0, Wp, j, e0) in pieces:
                S.dma('sync', xt.ap[:, :, :Wp], XI3[:, :, c0:c0 + Wp], writes=[xt])
                norm_mod(xt, Wp, l, 1, j, hT, sq, sd, rs, tmp, hoff=e0, bank=7)
            S.cp('pool', xt, xt.ap[:, :, 0], hx, hx.ap)
            norm_mod(xt, 1, l, 1, 0, hT, sq, sd, rs, tmp, hoff=LB + L + 1, bank=7)
            jobs = []
            cnt = [0, 0]
            for j2 in range(FC // 2):
                def fn(buf, j2=j2):
                    wgv = wview(buf, 0, KC, 256)
                    wuv = wview(buf, 4096, KC, 256)
                    for jj in range(2):
                        jh = 2 * j2 + jj
                        gs = gseg[jh % 2]
                        pus = []
                        for (c0, T, j, e0) in wins:
                            pg = PS[cnt[0] % 3]
                            cnt[0] += 1
                            for k in range(KC):
                                S.mm(pg, pg.ap[:, :T], buf, wgv[:, k, jj * 128:(jj + 1) * 128],
                                     hT, hT.ap[:, k, e0:e0 + T], k == 0, k == KC - 1)
                            S.cp('act', gs, gs.ap[:, e0:e0 + T], pg, pg.ap[:, :T])
                        pg = PS[3]
                        eh = LB + L + 1
                        for k in range(KC):
                            S.mm(pg, pg.ap[:, 0:1], buf, wgv[:, k, jj * 128:(jj + 1) * 128], hT, hT.ap[:, k, eh:eh + 1], k == 0, k == KC - 1)
                        S.cp('act', gs, gs.ap[:, eh:eh + 1], pg, pg.ap[:, 0:1])
                        segs = ([] if skip_ctx else [(0, CTX, 0)]) + [(CTX, L, LB)]
                        for (a0, n, e) in segs:
                            S.act(acc, acc.ap[:, a0:a0 + n], gs, gs.ap[:, e + 1:e + 1 + n], AF.Identity,
                                  bias=sv('fdwb', l * FC + jh), scale=sv('fdw', (l * 3 + 1) * FC + jh))
                            S.stt(acc, acc.ap[:, a0:a0 + n], gs, gs.ap[:, e:e + n], sv('fdw', (l * 3 + 0) * FC + jh),
                                  acc, acc.ap[:, a0:a0 + n], ALU.mult, ALU.add)
                            S.stt(acc, acc.ap[:, a0:a0 + n], gs, gs.ap[:, e + 2:e + 2 + n], sv('fdw', (l * 3 + 2) * FC + jh),
                                  acc, acc.ap[:, a0:a0 + n], ALU.mult, ALU.add)
                        lo = CTX if skip_ctx else 0
                        S.act(acc, acc.ap[:, lo:NT], acc, acc.ap[:, lo:NT], AF.Silu)
                        a_ = ast[jh % 2]
                        for (c0, T, j, e0) in wins:
                            pu = PS[4 + cnt[1] % 3]
                            cnt[1] += 1
                            for k in range(KC):
                                S.mm(pu, pu.ap[:, :T], buf, wuv[:, k, jj * 128:(jj + 1) * 128],
                                     hT, hT.ap[:, k, e0:e0 + T], k == 0, k == KC - 1)
                            S.tt('dve', a_, a_.ap[:, c0:c0 + T], acc, acc.ap[:, c0:c0 + T], pu, pu.ap[:, :T], ALU.mult)
                        S.dma('sync', AFF[jh, :, lo:NT], a_.ap[:, lo:NT], reads=[a_])
                jobs.append(dict(loads=[(Wt['wg', l], KC, j2 * 256, 256, 0), (Wt['wu', l], KC, j2 * 256, 256, 4096)], fn=fn))
            run_stream(jobs, wb)
            S.barrier()
            A.reset()
            wd = A.alloc('wd', (FC, 512), BF16)
            at = [A.alloc('at%d' % i, (FC, 512), BF16) for i in range(2)]
            xo = [A.alloc('xo%d' % i, (4, 512), F32) for i in range(2)]
            AFF3 = hm(AFF)
            it = 0
            seq = [(fs, w) for fs in range(4) for w in wins]

            def load_at(i):
                (c0, T, j, e0) = seq[i][1]
                a_ = at[i % 2]
                S.dma('sync', a_.ap[:, :, :T], AFF3[:, :, c0:c0 + T], writes=[a_])
            load_at(0)
            for i, (fs, (c0, T, j, e0)) in enumerate(seq):
                if i % len(wins) == 0:
                    S.dma('pool', wd.ap, Wt['wd', l][:, fs * 512:(fs + 1) * 512].rearrange("(c p) n -> p c n", p=128), writes=[wd])
                if i + 1 < len(seq):
                    load_at(i + 1)
                a_, x_ = at[i % 2], xo[i % 2]
                S.dma('sync', x_.ap[:, :, :T], XI3[:, 4 * fs:4 * fs + 4, c0:c0 + T], writes=[x_])
                for ff in range(4):
                    f = 4 * fs + ff
                    po = PS[(4 * i + ff) % 4]
                    for jh in range(FC):
                        S.mm(po, po.ap[:, :T], wd, wd.ap[:, jh, ff * 128:(ff + 1) * 128], a_, a_.ap[:, jh, :T], jh == 0, jh == FC - 1)
                    S.stt(x_, x_.ap[:, ff, :T], po, po.ap[:, :T], mG(l, 1, f, j), x_, x_.ap[:, ff, :T], ALU.mult, ALU.add)
                if final and j == 1:
                    pass
                elif final:
                    S.dma('sync', OUT3[:, 4 * fs:4 * fs + 4, c0 - CTX:c0 - CTX + T], x_.ap[:, :, :T], reads=[x_], final=True)
                else:
                    S.dma('sync', XO3[:, 4 * fs:4 * fs + 4, c0:c0 + T], x_.ap[:, :, :T], reads=[x_])
            S.barrier()

        def copy_phase(XI, XO):
            S.dma('sync', XO, XI)
            S.barrier()

        def cnv_phase(l, XI3, XO3):
            A.reset()
            xt = A.alloc('xt', (KC, 512), F32)
            hT = A.alloc('hT', (KC, 512), BF16)
            gl = A.alloc('gl', (KC, 512), F32)
            wb = [A.alloc('wb%d' % i, (8192,), BF16) for i in range(3)]
            sgm = [A.alloc('sgm%d' % i, (512,), F32) for i in range(2)]
            sq = [A.alloc('sq%d' % i, (512,), BF16) for i in range(2)]
            tmp = [A.alloc('tmp%d' % i, (512,), F32) for i in range(2)]
            sd = A.alloc('sd', (512,), F32)
            rs = A.alloc('rs', (512,), F32)
            GT3 = hm(GT)
            jobs = []
            for (t0, T, j) in mixer_tiles:
                def pre(t0=t0, T=T, j=j):
                    S.dma('sync', xt.ap[:, :, :T], XI3[:, :, t0:t0 + T], writes=[xt])
                    norm_mod(xt, T, l, 0, j, hT, sq, sd, rs, tmp)
                for c in range(KC):
                    def fn(buf, c=c, t0=t0, T=T):
                        wa = wview(buf, 0, KC, 128)
                        wg_ = wview(buf, 2048, KC, 128)
                        pa, pb = PS[c % 2], PS[2 + c % 2]
                        for k in range(KC):
                            S.mm(pa, pa.ap[:, :T], buf, wa[:, k, :], hT, hT.ap[:, k, :T], k == 0, k == KC - 1)
                        for k in range(KC):
                            S.mm(pb, pb.ap[:, :T], buf, wg_[:, k, :], hT, hT.ap[:, k, :T], k == 0, k == KC - 1)
                        g_ = sgm[c % 2]
                        S.act(g_, g_.ap[:, :T], pb, pb.ap[:, :T], AF.Sigmoid, bias=sv('cb1', 16 + c))
                        S.stt(gl, gl.ap[:, c, :T], pa, pa.ap[:, :T], sv('cb1', c), g_, g_.ap[:, :T], ALU.add, ALU.mult)
                        if c == KC - 1:
                            S.dma('sync', GT3[:, :, t0:t0 + T], gl.ap[:, :, :T], reads=[gl])
                    jobs.append(dict(pre=pre if c == 0 else None,
                                     loads=[(Wt['c1', l], KC, c * 128, 128, 0), (Wt['c1', l], KC, D + c * 128, 128, 2048)],
                                     fn=fn))
            run_stream(jobs, wb)
            S.barrier()
            S.dma('sync', CH_in.rearrange("c p d -> p c d"), GT3[:, :, NT - 15:NT])
            S.barrier()
            exchange(CH_in, CH_out)
            A.reset()
            ch2 = A.alloc('ch2', (2, KC, 15), F32)
            gx = A.alloc('gx', (KC, 542), F32)
            acs = [A.alloc('ac%d' % k, (512,), F32) for k in range(KC)]
            a2 = A.alloc('a2', (KC, 512), BF16)
            xt = A.alloc('xt', (KC, 512), F32)
            wb = [A.alloc('wb%d' % i, (2048,), BF16) for i in range(3)]
            xb = [A.alloc('xb%d' % i, (512,), BF16) for i in range(2)]
            sq = [A.alloc('sq%d' % i, (512,), BF16) for i in range(2)]
            tmp = [A.alloc('tmp%d' % i, (512,), F32) for i in range(2)]
            mean = A.alloc('mean', (512,), F32)
            msq = A.alloc('msq', (512,), F32)
            var = A.alloc('var', (512,), F32)
            rs = A.alloc('rs', (512,), F32)
            jobs = []
            for (t0, T, j) in mixer_tiles:
                if l == DEPTH - 1 and j == 1:
                    continue
                s0, s1 = (0, CTX) if j == 1 else (CTX, NT)

                def pre(t0=t0, T=T, j=j, s0=s0, s1=s1):
                    a0, a1 = max(t0 - 15, s0), min(t0 + T + 15, s1)
                    if a0 > t0 - 15:
                        S.ms('pool', gx, gx.ap[:, :, 0:15], 0.0)
                    if a1 < t0 + T + 15 and j == 1:
                        S.ms('pool', gx, gx.ap[:, :, T + 15:T + 30], 0.0)
                    S.dma('sync', gx.ap[:, :, a0 - (t0 - 15):a1 - (t0 - 15)], GT3[:, :, a0:a1], writes=[gx])
                    if a1 < t0 + T + 15 and j == 0:
                        S.dma('sync', ch2.ap, CH_out.rearrange("r c p d -> p r c d"), writes=[ch2])
                        for d in range(15):
                            S.ts('dve', gx, gx.ap[:, :, T + 15 + d], ch2, ch2.ap[:, 0, :, 14 - d], msk0, None, ALU.mult)
                            S.stt(gx, gx.ap[:, :, T + 15 + d], ch2, ch2.ap[:, 1, :, 14 - d], msk1,
                                  gx, gx.ap[:, :, T + 15 + d], ALU.mult, ALU.add)
                    S.dma('sync', xt.ap[:, :, :T], XI3[:, :, t0:t0 + T], writes=[xt])
                    pm_, pq_ = PS[4], PS[5]
                    for kg in range(0, 16, 4):
                        for tp in range(31):
                            for k in range(kg, kg + 4):
                                a_ = acs[k]
                                if tp == 0:
                                    S.ts('dve', a_, a_.ap[:, :T], gx, gx.ap[:, k, 0:T], sv('cdw', k), sv('cdwb', k), ALU.mult, ALU.add)
                                else:
                                    S.stt(a_, a_.ap[:, :T], gx, gx.ap[:, k, tp:tp + T], sv('cdw', tp * 16 + k),
                                          a_, a_.ap[:, :T], ALU.mult, ALU.add)
                    for k in range(KC):
                        b_ = xb[k % 2]
                        q_ = sq[k % 2]
                        S.cp('act', b_, b_.ap[:, :T], acs[k], acs[k].ap[:, :T])
                        S.act(q_, q_.ap[:, :T], acs[k], acs[k].ap[:, :T], AF.Square)
                        S.mm(pm_, pm_.ap[:, :T], None, ones_bf, b_, b_.ap[:, :T], k == 0, k == KC - 1)
                        S.mm(pq_, pq_.ap[:, :T], None, ones_bf, q_, q_.ap[:, :T], k == 0, k == KC - 1)
                    S.act(mean, mean.ap[:, :T], pm_, pm_.ap[:, :T], AF.Copy, scale=1.0 / D)
                    S.tt('dve', msq, msq.ap[:, :T], mean, mean.ap[:, :T], mean, mean.ap[:, :T], ALU.mult)
                    S.stt(var, var.ap[:, :T], pq_, pq_.ap[:, :T], 1.0 / D, msq, msq.ap[:, :T], ALU.mult, ALU.subtract)
                    S.act(var, var.ap[:, :T], var, var.ap[:, :T], AF.Sqrt, bias=eps_ln, scale=1.0)
                    S.recip(rs, rs.ap[:, :T], var, var.ap[:, :T])
                    for k in range(KC):
                        t_ = tmp[k % 2]
                        S.tt('dve', t_, t_.ap[:, :T], acs[k], acs[k].ap[:, :T], mean, mean.ap[:, :T], ALU.subtract)
                        S.tt('dve', t_, t_.ap[:, :T], t_, t_.ap[:, :T], rs, rs.ap[:, :T], ALU.mult)
                        S.act(a2, a2.ap[:, k, :T], t_, t_.ap[:, :T], AF.Silu, bias=sv('clnb', k), scale=sv('clnw', k))
                for f in range(KC):
                    def fn(buf, f=f, t0=t0, T=T, j=j):
                        wv = wview(buf, 0, KC, 128)
                        pw = PS[f % 2]
                        for k in range(KC):
                            S.mm(pw, pw.ap[:, :T], buf, wv[:, k, :], a2, a2.ap[:, k, :T], k == 0, k == KC - 1)
                        S.stt(xt, xt.ap[:, f, :T], pw, pw.ap[:, :T], mG(l, 0, f, j), xt, xt.ap[:, f, :T], ALU.mult, ALU.add)
                        S.ts('dve', xt, xt.ap[:, f, :T], xt, xt.ap[:, f, :T], b2g.ap[:, f, j:j + 1], None, ALU.add)
                        if f == KC - 1:
                            S.dma('sync', XO3[:, :, t0:t0 + T], xt.ap[:, :, :T], reads=[xt])
                    jobs.append(dict(pre=pre if f == 0 else None, loads=[(Wt['c2', l], KC, f * 128, 128, 0)], fn=fn))
            run_stream(jobs, wb)
            S.barrier()


        def att_phase(l, XI3, XO3):
            A.reset()
            xt = A.alloc('xt', (KC, 512), F32)
            hT = A.alloc('hT', (KC, 512), BF16)
            wb = [A.alloc('wb%d' % i, (8192,), BF16) for i in range(3)]
            sq = [A.alloc('sq%d' % i, (512,), BF16) for i in range(2)]
            tmp = [A.alloc('tmp%d' % i, (512,), F32) for i in range(2)]
            sd = A.alloc('sd', (512,), F32)
            rs = A.alloc('rs', (512,), F32)
            cs = A.alloc('cs', (512,), F32)
            sn = A.alloc('sn', (512,), F32)
            qn = [A.alloc('qn%d' % i, (512,), BF16) for i in range(2)]
            qr = [A.alloc('qr%d' % i, (512,), BF16) for i in range(2)]
            hsd = [A.alloc('hsd%d' % i, (512,), F32) for i in range(2)]
            hrs = [A.alloc('hrs%d' % i, (512,), F32) for i in range(2)]
            t1 = [A.alloc('t1%d' % i, (512,), F32) for i in range(2)]
            t2 = [A.alloc('t2%d' % i, (512,), F32) for i in range(2)]
            vt = [A.alloc('vt%d' % i, (512,), BF16) for i in range(2)]
            cnt = [0]
            jobs = []
            for (t0, T, j) in mixer_tiles:
                def pre(t0=t0, T=T, j=j):
                    S.dma('sync', xt.ap[:, :, :T], XI3[:, :, t0:t0 + T], writes=[xt])
                    if j == 0:
                        S.dma('sync', cs.ap[:, :T], rcos[:, t0 - CTX:t0 - CTX + T], writes=[cs])
                        S.dma('sync', sn.ap[:, :T], rsin[:, t0 - CTX:t0 - CTX + T], writes=[sn])
                    norm_mod(xt, T, l, 0, j, hT, sq, sd, rs, tmp)

                def head(buf, wv, hh, gain, dst, t0, T, j):
                    i = cnt[0]
                    cnt[0] += 1
                    pq, pss, pr = PS[i % 2], PS[2 + i % 2], PS[4 + i % 2]
                    for k in range(KC):
                        S.mm(pq, pq.ap[:, :T], buf, wv[:, k, hh * 128:(hh + 1) * 128], hT, hT.ap[:, k, :T], k == 0, k == KC - 1)
                    q_ = sq[i % 2]
                    S.act(q_, q_.ap[:, :T], pq, pq.ap[:, :T], AF.Square)
                    S.mm(pss, pss.ap[:, :T], None, ones_bf, q_, q_.ap[:, :T], True, True)
                    d_, r_ = hsd[i % 2], hrs[i % 2]
                    S.act(d_, d_.ap[:, :T], pss, pss.ap[:, :T], AF.Sqrt, bias=eps_rms, scale=1.0 / 128)
                    S.recip(r_, r_.ap[:, :T], d_, d_.ap[:, :T])
                    n_ = qn[i % 2]
                    S.stt(n_, n_.ap[:, :T], pq, pq.ap[:, :T], gain, r_, r_.ap[:, :T], ALU.mult, ALU.mult)
                    if j == 0:
                        S.mm(pr, pr.ap[:, :T], None, perm_bf, n_, n_.ap[:, :T], True, True)
                        a_, b_, o_ = t1[i % 2], t2[i % 2], qr[i % 2]
                        S.tt('dve', a_, a_.ap[:, :T], n_, n_.ap[:, :T], cs, cs.ap[:, :T], ALU.mult)
                        S.tt('dve', b_, b_.ap[:, :T], pr, pr.ap[:, :T], sn, sn.ap[:, :T], ALU.mult)
                        S.tt('dve', o_, o_.ap[:, :T], a_, a_.ap[:, :T], b_, b_.ap[:, :T], ALU.add)
                        S.dma('sync', dst[:, t0:t0 + T], o_.ap[:, :T], reads=[o_])
                    else:
                        S.dma('sync', dst[:, t0:t0 + T], n_.ap[:, :T], reads=[n_])

                for s in range(8):
                    def fn(buf, s=s, t0=t0, T=T, j=j):
                        wv = wview(buf, 0, KC, 256)
                        for hh in range(2):
                            head(buf, wv, hh, sv('qg'), QT[2 * s + hh], t0, T, j)
                    jobs.append(dict(pre=pre if s == 0 else None, loads=[(Wt['aq', l], KC, s * 256, 256, 0)], fn=fn))

                def fnk(buf, t0=t0, T=T, j=j):
                    wv = wview(buf, 0, KC, 512)
                    for hh in range(4):
                        head(buf, wv, hh, sv('kg'), KT[hh], t0, T, j)
                jobs.append(dict(loads=[(Wt['akv', l], KC, 0, 512, 0)], fn=fnk))

                def fnv(buf, t0=t0, T=T):
                    wv = wview(buf, 0, KC, 512)
                    for c in range(T // 128):
                        pv = PS[6 + c % 2]
                        for k in range(KC):
                            S.mm(pv, pv.ap, hT, hT.ap[:, k, c * 128:(c + 1) * 128], buf, wv[:, k, :], k == 0, k == KC - 1)
                        v_ = vt[c % 2]
                        S.cp('act', v_, v_.ap, pv, pv.ap)
                        S.dma('sync', VTOK[t0 // 128 + c, :, 0:512], v_.ap, reads=[v_])
                jobs.append(dict(loads=[(Wt['akv', l], KC, 512, 512, 0)], fn=fnv))
            run_stream(jobs, wb)
            S.barrier()
            S.dma('sync', KX_in, KT[0:4, :, CTX:NT])
            S.dma('sync', VX_in, VTOK[2:NCH, :, 0:512])
            S.barrier()
            exchange(KX_in, KX_out)
            exchange(VX_in, VX_out)
            A.reset()
            NK = CTX + 2 * L
            NKC = NK // 128
            Ks = A.alloc('Ks', (4, NK), BF16)
            Vs = A.alloc('Vs', (NKC, 512), BF16)
            qs = A.alloc('qs', (KC, 512), BF16)
            at = A.alloc('at', (KC, 512), BF16)
            xt = A.alloc('xt', (KC, 512), F32)
            wb = [A.alloc('wb%d' % i, (4096,), BF16) for i in range(2)]
            Eb = [A.alloc('E%d' % i, (512,), BF16) for i in range(3)]
            rec = [A.alloc('rec%d' % i, (512,), F32) for i in range(2)]
            S.dma('sync', Ks.ap[:, :, 0:CTX], hm(KT)[:, 0:4, 0:CTX], writes=[Ks])
            S.dma('sync', Vs.ap[:, 0:2, :], hm(VTOK)[:, 0:2, 0:512], writes=[Vs])
            for r_ in range(2):
                S.dma('sync', Ks.ap[:, :, CTX + r_ * L:CTX + (r_ + 1) * L], hm(KX_out[r_]), writes=[Ks])
                S.dma('sync', Vs.ap[:, 2 + r_ * NLC:2 + (r_ + 1) * NLC, :], hm(VX_out[r_]), writes=[Vs])
            QT3 = hm(QT)
            scale = 128.0 ** -0.5
            jobs = []
            for (t0, T, j) in mixer_tiles:
                if l == DEPTH - 1 and j == 1:
                    continue
                chunks = [0, 1] if j == 1 else list(range(NKC))

                def pre(t0=t0, T=T, j=j, chunks=chunks):
                    S.dma('sync', qs.ap[:, :, :T], QT3[:, :, t0:t0 + T], writes=[qs])
                    S.dma('sync', xt.ap[:, :, :T], XI3[:, :, t0:t0 + T], writes=[xt])
                    n = len(chunks)
                    for h in range(16):
                        kvh = h // 4
                        po, pd = PS[2 + (h % 2) * 2], PS[3 + (h % 2) * 2]

                        def score(ci):
                            c = chunks[ci]
                            ps = PS[ci % 2]
                            S.mm(ps, ps.ap[:, :T], Ks, Ks.ap[:, kvh, c * 128:(c + 1) * 128], qs, qs.ap[:, h, :T], True, True)
                        score(0)
                        for ci in range(n):
                            if ci + 1 < n:
                                score(ci + 1)
                            c = chunks[ci]
                            ps = PS[ci % 2]
                            E = Eb[ci % 3]
                            S.act(E, E.ap[:, :T], ps, ps.ap[:, :T], AF.Exp, scale=scale)
                            S.mm(po, po.ap[:, :T], Vs, Vs.ap[:, c, kvh * 128:(kvh + 1) * 128], E, E.ap[:, :T], ci == 0, ci == n - 1)
                            S.mm(pd, pd.ap[:, :T], None, ones_bf, E, E.ap[:, :T], ci == 0, ci == n - 1)
                        r_ = rec[h % 2]
                        S.recip(r_, r_.ap[:, :T], pd, pd.ap[:, :T])
                        S.tt('dve', at, at.ap[:, h, :T], po, po.ap[:, :T], r_, r_.ap[:, :T], ALU.mult)

                for s in range(8):
                    def fn(buf, s=s, t0=t0, T=T, j=j):
                        wv = wview(buf, 0, KC, 256)
                        for ff in range(2):
                            f = 2 * s + ff
                            pw = PS[6 + f % 2]
                            for k in range(KC):
                                S.mm(pw, pw.ap[:, :T], buf, wv[:, k, ff * 128:(ff + 1) * 128], at, at.ap[:, k, :T], k == 0, k == KC - 1)
                            S.stt(xt, xt.ap[:, f, :T], pw, pw.ap[:, :T], mG(l, 0, f, j), xt, xt.ap[:, f, :T], ALU.mult, ALU.add)
                            if f == KC - 1:
                                S.dma('sync', XO3[:, :, t0:t0 + T], xt.ap[:, :, :T], reads=[xt])
                    jobs.append(dict(pre=pre if s == 0 else None, loads=[(Wt['ao', l], KC, s * 256, 256, 0)], fn=fn))
            run_stream(jobs, wb)
            S.barrier()


        def ret_phase(l, XI3, XO3):
            jl = l // 3
            A.reset()
            xt = A.alloc('xt', (KC, 512), F32)
            hT = A.alloc('hT', (KC, 512), BF16)
            wb = [A.alloc('wb%d' % i, (8192,), BF16) for i in range(3)]
            sq = [A.alloc('sq%d' % i, (512,), BF16) for i in range(2)]
            tmp = [A.alloc('tmp%d' % i, (512,), F32) for i in range(2)]
            sd = A.alloc('sd', (512,), F32)
            rs = A.alloc('rs', (512,), F32)
            ob = [A.alloc('ob%d' % i, (512,), BF16) for i in range(4)]
            cnt = [0]
            jobs = []
            for (t0, T, j) in mixer_tiles:
                def pre(t0=t0, T=T, j=j):
                    S.dma('sync', xt.ap[:, :, :T], XI3[:, :, t0:t0 + T], writes=[xt])
                    norm_mod(xt, T, l, 0, j, hT, sq, sd, rs, tmp)
                for (wname, dst, scl) in (('rq', QT, 1.0), ('rk', KT, 1.0 / 16)):
                    for s in range(8):
                        def fn(buf, s=s, t0=t0, T=T, dst=dst, scl=scl):
                            wv = wview(buf, 0, KC, 256)
                            for hh in range(2):
                                i = cnt[0]
                                cnt[0] += 1
                                pq = PS[i % 2]
                                for k in range(KC):
                                    S.mm(pq, pq.ap[:, :T], buf, wv[:, k, hh * 128:(hh + 1) * 128], hT, hT.ap[:, k, :T], k == 0, k == KC - 1)
                                o_ = ob[i % 4]
                                S.act(o_, o_.ap[:, :T], pq, pq.ap[:, :T], AF.Copy, scale=scl)
                                S.dma('sync', dst[2 * s + hh, :, t0:t0 + T], o_.ap[:, :T], reads=[o_])
                        jobs.append(dict(pre=pre if (wname == 'rq' and s == 0) else None,
                                         loads=[(Wt[wname, l], KC, s * 256, 256, 0)], fn=fn))
                for (wname, dst, nsl, func, scl) in (('rk', KTOK, 4, AF.Copy, 1.0 / 16), ('rv', VTOK, 8, AF.Copy, 1.0),
                                                     ('rg', GTOK, 8, AF.Silu, 1.0)):
                    for s in range(nsl):
                        def fn(buf, s=s, t0=t0, T=T, dst=dst, func=func, scl=scl):
                            wv = wview(buf, 0, KC, 512)
                            for c in range(T // 128):
                                i = cnt[0]
                                cnt[0] += 1
                                pv = PS[2 + i % 2]
                                for k in range(KC):
                                    S.mm(pv, pv.ap, hT, hT.ap[:, k, c * 128:(c + 1) * 128], buf, wv[:, k, :], k == 0, k == KC - 1)
                                o_ = ob[i % 4]
                                S.act(o_, o_.ap, pv, pv.ap, func, scale=scl)
                                S.dma('sync', dst[t0 // 128 + c, :, s * 512:(s + 1) * 512], o_.ap, reads=[o_])
                        jobs.append(dict(loads=[(Wt[wname, l], KC, s * 512, 512, 0)], fn=fn))
            run_stream(jobs, wb)
            S.barrier()
            QT3, KT3, ATS3 = hm(QT), hm(KT), hm(ATS)
            KTOK3, VTOK3 = hm(KTOK), hm(VTOK)

            def head_consts(hc, h):
                lgf = lgT.ap[:, jl * 16 + h:jl * 16 + h + 1]
                lgb = lgT.ap[:, jl * 16 + 8 + h:jl * 16 + 8 + h + 1]
                S.act(hc, hc.ap[:, 0:1], None, cst('p1', 1), AF.Exp, scale=lgf)
                S.act(hc, hc.ap[:, 1:2], None, cst('cmp', 1), AF.Exp, scale=lgb)
                S.act(hc, hc.ap[:, 2:3], None, cst('c1p', 1), AF.Exp, scale=lgf)
                S.act(hc, hc.ap[:, 3:4], None, cst('p0', 1), AF.Exp, scale=lgb)
                S.act(hc, hc.ap[:, 4:5], None, lgf, AF.Exp, scale=128.0)
                S.act(hc, hc.ap[:, 5:6], None, lgb, AF.Exp, scale=128.0)
                return lgf, lgb

            A.reset()
            Kfs = [A.alloc('Kf%d' % i, (NCH, 256), BF16) for i in range(2)]
            Vs_ = [A.alloc('V%d' % i, (NCH, 512), BF16) for i in range(2)]
            stall = [A.alloc('stall%d' % i, (NCH, 2, 512), BF16) for i in range(2)]
            Sm = A.alloc('Sm', (2, 512), F32)
            hcs = [A.alloc('hc%d' % i, (8,), F32) for i in range(2)]
            for h in range(8):
                Kf, V, sta, hc = Kfs[h % 2], Vs_[h % 2], stall[h % 2], hcs[h % 2]
                S.dma('sync', Kf.ap, KTOK3[:, :, h * 256:(h + 1) * 256], writes=[Kf])
                S.dma('sync', V.ap, VTOK3[:, :, h * 512:(h + 1) * 512], writes=[V])
                head_consts(hc, h)
                kdf, cdf = hc.ap[:, 2:3], hc.ap[:, 4:5]
                S.act(Kf, Kf.ap, Kf, Kf.ap, AF.Copy, scale=kdf, rd=(hc,))
                S.ms('pool', Sm, Sm.ap, 0.0)
                for c in range(NCH):
                    S.cp('act', sta, sta.ap[:, c], Sm, Sm.ap)
                    for kc in range(2):
                        pb_ = PS[(2 * c + kc) % 4]
                        S.mm(pb_, pb_.ap, Kf, Kf.ap[:, c, kc * 128:(kc + 1) * 128], V, V.ap[:, c, :], True, True)
                    for kc in range(2):
                        pb_ = PS[(2 * c + kc) % 4]
                        S.stt(Sm, Sm.ap[:, kc, :], Sm, Sm.ap[:, kc, :], cdf, pb_, pb_.ap, ALU.mult, ALU.add, rd=(hc,))
                S.dma('sync', hm(SBD[h]).rearrange("p c (a b) -> p c a b", a=2), sta.ap, reads=[sta])
                S.dma('sync', E_in[h // 4][h % 4].rearrange("p (a b) -> p a b", a=2), Sm.ap, reads=[Sm])
            S.barrier()
            exchange(E_in[0], E_out[0])
            exchange(E_in[1], E_out[1])
            A.reset()
            q = A.alloc('q', (2, NT), BF16)
            k_ = A.alloc('k', (2, NT), BF16)
            Kb = A.alloc('Kb', (NCH, 256), BF16)
            V = A.alloc('V', (NCH, 512), BF16)
            SB = A.alloc('SB', (NCH, 2, 512), BF16)
            oo = A.alloc('oo', (NCH, 512), F32)
            atb = A.alloc('atb', (4, NT), BF16)
            Sm = A.alloc('Sm', (2, 512), F32)
            Scs = [A.alloc('Sc%d' % i, (2, 512), BF16) for i in range(2)]
            e2 = A.alloc('e2', (2, 2, 512), F32)
            gb = [A.alloc('gb%d' % i, (512,), BF16) for i in range(4)]
            m1 = A.alloc('m1', (128,), F32)
            m2 = A.alloc('m2', (128,), F32)
            Mc = A.alloc('Mc', (128,), F32)
            hc = A.alloc('hc', (8,), F32)
            scb = [A.alloc('scb%d' % i, (128,), BF16) for i in range(3)]
            yb = [A.alloc('yb%d' % i, (512,), BF16) for i in range(4)]
            st6 = A.alloc('st6', (NCH, 6), F32)
            mv = A.alloc('mv', (NCH, 2), F32)
            rsd = A.alloc('rsd', (NCH,), F32)
            nmr = A.alloc('nmr', (NCH,), F32)
            order_b = list(range(NCH - 1, 1, -1)) + [1, 0]
            cmin = 2 if l == DEPTH - 1 else 0
            outs = [c for c in order_b if c >= cmin]
            for h in range(8):
                S.dma('sync', q.ap, QT3[:, 2 * h:2 * h + 2, :], writes=[q])
                S.dma('sync', k_.ap, KT3[:, 2 * h:2 * h + 2, :], writes=[k_])
                S.dma('sync', Kb.ap, KTOK3[:, :, h * 256:(h + 1) * 256], writes=[Kb])
                S.dma('sync', V.ap, VTOK3[:, :, h * 512:(h + 1) * 512], writes=[V])
                S.dma('sync', SB.ap, hm(SBD[h]).rearrange("p c (a b) -> p c a b", a=2), writes=[SB])
                S.dma('sync', e2.ap, E_out[h // 4][:, h % 4].rearrange("r p (a b) -> p r a b", a=2), writes=[e2])
                lgf, lgb = head_consts(hc, h)
                S.act(m1, m1.ap, None, cst('d1'), AF.Exp, scale=lgf)
                S.tt('dve', m1, m1.ap, m1, m1.ap, None, cst('u'), ALU.mult)
                S.act(m2, m2.ap, None, cst('d2'), AF.Exp, scale=lgb)
                S.tt('dve', m2, m2.ap, m2, m2.ap, None, cst('l'), ALU.mult)
                S.tt('dve', Mc, Mc.ap, m1, m1.ap, m2, m2.ap, ALU.add)
                qdf, qdb, kdf, kdb, cdf, cdb = (hc.ap[:, i:i + 1] for i in range(6))
                S.act(Kb, Kb.ap, Kb, Kb.ap, AF.Copy, scale=kdb, rd=(hc,))
                S.ts('dve', Sm, Sm.ap, e2, e2.ap[:, 0], msk0, None, ALU.mult)
                S.stt(Sm, Sm.ap, e2, e2.ap[:, 1], msk1, Sm, Sm.ap, ALU.mult, ALU.add)
                si = 0
                S.cp('act', Scs[0], Scs[0].ap, Sm, Sm.ap)
                for i, c in enumerate(order_b):
                    if c == 1:
                        S.ms('pool', Sm, Sm.ap, 0.0)
                        si += 1
                        S.ms('pool', Scs[si % 2], Scs[si % 2].ap, 0.0)
                    Sc = Scs[si % 2]
                    csl = slice(c * 128, (c + 1) * 128)
                    want = c >= cmin
                    upd = c not in (2, 0)
                    pss = PS[0]
                    pi, pf, pb = (PS[3 + (3 * i + t) % 5] for t in range(3))
                    if want:
                        for kc in range(2):
                            S.mm(pss, pss.ap[:, :128], k_, k_.ap[:, kc, csl], q, q.ap[:, kc, csl], kc == 0, kc == 1)
                        sc_ = scb[i % 3]
                        S.tt('dve', sc_, sc_.ap, pss, pss.ap[:, :128], Mc, Mc.ap, ALU.mult)
                    if upd:
                        for kc in range(2):
                            S.mm(PS[1 + kc], PS[1 + kc].ap, Kb, Kb.ap[:, c, kc * 128:(kc + 1) * 128], V, V.ap[:, c, :], True, True)
                    if want:
                        for kc in range(2):
                            S.mm(pf, pf.ap, q, q.ap[:, kc, csl], SB, SB.ap[:, c, kc, :], kc == 0, kc == 1)
                        S.mm(pi, pi.ap, sc_, sc_.ap, V, V.ap[:, c, :], True, True)
                        for kc in range(2):
                            S.mm(pb, pb.ap, q, q.ap[:, kc, csl], Sc, Sc.ap[:, kc, :], kc == 0, kc == 1)
                        S.cp('act', oo, oo.ap[:, c, :], pi, pi.ap)
                        S.stt(oo, oo.ap[:, c, :], pf, pf.ap, qdf, oo, oo.ap[:, c, :], ALU.mult, ALU.add, rd=(hc,))
                    if upd:
                        for kc in range(2):
                            S.stt(Sm, Sm.ap[:, kc, :], Sm, Sm.ap[:, kc, :], cdb, PS[1 + kc], PS[1 + kc].ap, ALU.mult, ALU.add, rd=(hc,))
                        si += 1
                        S.cp('act', Scs[si % 2], Scs[si % 2].ap, Sm, Sm.ap)
                    if want:
                        S.stt(oo, oo.ap[:, c, :], pb, pb.ap, qdb, oo, oo.ap[:, c, :], ALU.mult, ALU.add, rd=(hc,))
                for c in outs:
                    S.op('dve', lambda e, a=st6.ap[:, c, :], b=oo.ap[:, c, :]: e.bn_stats(a, b), reads=[oo], writes=[st6])
                    S.op('dve', lambda e, a=mv.ap[:, c, :], b=st6.ap[:, c, :]: e.bn_aggr(a, b), reads=[st6], writes=[mv])
                S.act(rsd, rsd.ap[:, cmin:NCH], mv, mv.ap[:, cmin:NCH, 1], AF.Sqrt, bias=eps_ln, scale=1.0)
                S.recip(rsd, rsd.ap[:, cmin:NCH], rsd, rsd.ap[:, cmin:NCH])
                S.stt(nmr, nmr.ap[:, cmin:NCH], mv, mv.ap[:, cmin:NCH, 0], -1.0, rsd, rsd.ap[:, cmin:NCH], ALU.mult, ALU.mult)

                def s3(i):
                    c = outs[i]
                    g_ = gb[i % 4]
                    S.dma('sync', g_.ap, GTOK[c, :, h * 512:(h + 1) * 512], writes=[g_])
                    S.act(oo, oo.ap[:, c, :], oo, oo.ap[:, c, :], AF.Identity, bias=nmr.ap[:, c:c + 1], scale=rsd.ap[:, c:c + 1], rd=(nmr, rsd))
                    yb_ = yb[i % 4]
                    S.tt('dve', yb_, yb_.ap, oo, oo.ap[:, c, :], g_, g_.ap, ALU.mult)

                def s4(i):
                    c = outs[i]
                    yb_ = yb[i % 4]
                    pt = PS[1 + i % 2]
                    ptv = pt.ap[:, 0:256].bitcast(BF16)
                    for vc in range(4):
                        S.tr(pt, ptv[:, vc * 128:(vc + 1) * 128], yb_, yb_.ap[:, vc * 128:(vc + 1) * 128], ident_bf)
                    S.tt('dve', atb, atb.ap[:, :, c * 128:(c + 1) * 128], pt, ptv.rearrange("p (a b) -> p a b", a=4),
                         None, sv('gnw', jl * 32 + h * 4, 4).unsqueeze(2).to_broadcast([128, 4, 128]), ALU.mult)
                s3(0)
                for i in range(len(outs)):
                    if i + 1 < len(outs):
                        s3(i + 1)
                    s4(i)
                S.dma('sync', ATS3[:, 4 * h:4 * h + 4, cmin * 128:NT], atb.ap[:, :, cmin * 128:NT], reads=[atb])
            S.barrier()
            A.reset()
            at = A.alloc('at', (32, 512), BF16)
            xt = A.alloc('xt', (KC, 512), F32)
            wb = [A.alloc('wb%d' % i, (8192,), BF16) for i in range(3)]
            jobs = []
            for (t0, T, j) in mixer_tiles:
                if l == DEPTH - 1 and j == 1:
                    continue

                def pre(t0=t0, T=T):
                    S.dma('sync', at.ap[:, :, :T], ATS3[:, :, t0:t0 + T], writes=[at])
                    S.dma('sync', xt.ap[:, :, :T], XI3[:, :, t0:t0 + T], writes=[xt])
                for s in range(8):
                    def fn(buf, s=s, t0=t0, T=T, j=j):
                        wv = wview(buf, 0, 32, 256)
                        for ff in range(2):
                            f = 2 * s + ff
                            pw = PS[f % 2]
                            for k in range(32):
                                S.mm(pw, pw.ap[:, :T], buf, wv[:, k, ff * 128:(ff + 1) * 128], at, at.ap[:, k, :T], k == 0, k == 31)
                            S.stt(xt, xt.ap[:, f, :T], pw, pw.ap[:, :T], mG(l, 0, f, j), xt, xt.ap[:, f, :T], ALU.mult, ALU.add)
                            if f == KC - 1:
                                S.dma('sync', XO3[:, :, t0:t0 + T], xt.ap[:, :, :T], reads=[xt])
                    jobs.append(dict(pre=pre if s == 0 else None, loads=[(Wt['ro', l], 32, s * 256, 256, 0)], fn=fn))
            run_stream(jobs, wb)
            S.barrier()

        PHASES = {'cnv': cnv_phase, 'att': att_phase, 'ret': ret_phase}
        cur, nxt = (XA, XA3), (XB, XB3)
        for li, l in enumerate(layers):
            if do_mix and MIXER[l] in PHASES:
                PHASES[MIXER[l]](l, cur[1], nxt[1])
            else:
                copy_phase(cur[0], nxt[0])
            cur, nxt = nxt, cur
            lastl = (li == len(layers) - 1)
            if do_ffn:
                halo_exchange(cur[1])
                ffn_phase(l, cur[1], nxt[1], lastl)
                cur, nxt = nxt, cur
            elif lastl:
                S.dma('sync', outT, cur[0][:, CTX:NT], final=True)
        S.finish()
    return nc


def _weights_for(inp, layers, do_mix=True, do_ffn=True):
    m = {}
    for l in layers:
        m["mw%d" % l] = inp['mod_w'][l]
        if do_ffn:
            m["wg%d" % l] = inp['ffn_w_gate'][l]
            m["wu%d" % l] = inp['ffn_w_up'][l]
            m["wd%d" % l] = inp['ffn_w_down'][l]
        if not do_mix:
            continue
        if MIXER[l] == 'ret':
            j = l // 3
            m["rq%d" % l] = inp['ret_wq'][j]
            m["rk%d" % l] = inp['ret_wk'][j]
            m["rv%d" % l] = inp['ret_wv'][j]
            m["rg%d" % l] = inp['ret_wg'][j]
            m["ro%d" % l] = inp['ret_wo'][j]
            m["gn%d" % l] = inp['ret_gn_w'][j].reshape(1, HV)
        elif MIXER[l] == 'att':
            m["aq%d" % l] = inp['att_wq'][0]
            m["akv%d" % l] = inp['att_wkv'][0]
            m["ao%d" % l] = inp['att_wo'][0]
        else:
            m["c1_%d" % l] = inp['cnv_w1'][0]
            m["c2_%d" % l] = inp['cnv_w2'][0]
    return {k: np.ascontiguousarray(np.asarray(v, np.float32)) for k, v in m.items()}


def run(inp, nlat=4, layers=(0, 1, 2, 3), do_mix=True, do_ffn=True, nb=4):
    inp = {k: np.asarray(v) for k, v in inp.items()}
    L = nlat * 512
    prog = build(nlat, layers, do_mix, do_ffn)
    cst = make_consts()
    wts = _weights_for(inp, layers, do_mix, do_ffn)
    in_maps = []
    for b in range(nb):
        for r in range(2):
            if r == 0:
                tpos = np.arange(L)
                xc, xl = inp['ctx'][b], inp['x'][b][:L]
            else:
                tpos = np.arange(2 * L - 1, L - 1, -1)
                xc, xl = inp['ctx'][b][::-1], inp['x'][b][L:2 * L][::-1]
            rc, rs_ = make_rope(tpos)
            xT = np.concatenate([xc.T, xl.T], axis=1)
            m = {'xT': np.ascontiguousarray(xT, dtype=np.float32), 'sv': pack_sv(inp, b, r), 'cst': cst,
                 'rope_cos': rc, 'rope_sin': rs_}
            m.update(wts)
            in_maps.append(m)
    res = run_bass_kernel_spmd(prog, in_maps, core_ids=list(range(2 * nb)))
    outs = []
    for b in range(nb):
        o0 = res.results[2 * b]['outT'].T
        o1 = res.results[2 * b + 1]['outT'].T[::-1]
        outs.append(np.concatenate([o0, o1], axis=0))
    return np.ascontiguousarray(np.stack(outs))


def kernel(**inputs):
    return run(inputs).astype(np.float32)
```

```python
import contextlib
import numpy as np
import concourse.bass as bass
import concourse.mybir as mybir
from concourse.bass_utils import run_bass_kernel_spmd

F32 = mybir.dt.float32
BF16 = mybir.dt.bfloat16
AF = mybir.ActivationFunctionType
ALU = mybir.AluOpType

ENGS = ('sync', 'act', 'dve', 'pool', 'pe')
BLOCK_NAME = {'sync': 'sync', 'act': 'scalar', 'dve': 'vector', 'pool': 'gpsimd', 'pe': 'tensor'}


class Tile:
    __slots__ = ('name', 'ap', 'w', 'r')

    def __init__(self, name, ap=None):
        self.name = name
        self.ap = ap
        self.w = None
        self.r = {}


class Sched:
    def __init__(self, nc, ndma=(('sync', 30), ('pool', 30), ('act', 8))):
        self.nc = nc
        self.stack = contextlib.ExitStack()
        self.q = {e: [] for e in ENGS}
        self.cnt = {e: 0 for e in ENGS}
        self.waited = {e: {} for e in ENGS}
        self.csem = {}
        self.dsems, self.dval, self.dn = {}, {}, {}
        self.ndma = ndma
        self.finals = []
        self.n_ops = 0
        self.cc_list = []

    def __enter__(self):
        self.stack.__enter__()
        nc = self.nc
        for e in ('act', 'dve', 'pool', 'pe'):
            self.csem[e] = self.stack.enter_context(nc.semaphore("c_" + e))
        for e, n in self.ndma:
            self.dsems[e] = [self.stack.enter_context(nc.semaphore("d_%s_%d" % (e, i))) for i in range(n)]
            self.dval[e] = [0] * n
            self.dn[e] = 0
        return self

    def __exit__(self, *a):
        return self.stack.__exit__(*a)

    def sbuf(self, name, shape, dtype):
        t = self.stack.enter_context(self.nc.sbuf_tensor(name, list(shape), dtype))
        return Tile(name, t[:])

    def psum(self, name, shape, dtype):
        t = self.stack.enter_context(self.nc.psum_tensor(name, list(shape), dtype))
        return Tile(name, t[:])

    def region(self, name):
        return Tile(name, None)

    def _deps(self, reads, writes):
        evs = []
        for t in reads:
            evs.append(t.w)
        for t in writes:
            evs.append(t.w)
            evs.extend(t.r.values())
        return evs

    def _need(self, eng, evs):
        w = self.waited[eng]
        best = {}
        pe_sem = self.csem['pe'].num
        for ev in evs:
            if ev is None:
                continue
            sem, val = ev
            k = sem.num
            if eng == 'pe' and k == pe_sem:
                continue
            if w.get(k, 0) >= val:
                continue
            if k not in best or best[k][1] < val:
                best[k] = (sem, val)
        out = []
        for k, (sem, val) in best.items():
            w[k] = val
            out.append((sem, val))
        return out

    def _record(self, ev, reads, writes):
        k = ev[0].num
        for t in reads:
            t.r[k] = ev
        for t in writes:
            t.w = ev
            t.r = {}

    def op(self, eng, fn, reads=(), writes=()):
        waits = self._need(eng, self._deps(reads, writes))
        self.cnt[eng] += 1
        sem = self.csem[eng]
        ev = (sem, self.cnt[eng])
        self.q[eng].append((waits, fn, sem, 1))
        self._record(ev, reads, writes)
        self.n_ops += 1
        return ev

    def dma(self, eng, out_ap, in_ap, reads=(), writes=(), final=False, **kw):
        evs = self._deps(reads, writes)
        n = len(self.dsems[eng])
        k = self.dn[eng] % n
        self.dn[eng] += 1
        sem = self.dsems[eng][k]
        prev = self.dval[eng][k]
        if prev:
            evs.append((sem, prev))
        waits = self._need(eng, evs)
        val = prev + 16
        self.dval[eng][k] = val
        ev = (sem, val)
        self.q[eng].append((waits, lambda e: e.dma_start(out=out_ap, in_=in_ap, **kw), sem, 16))
        self._record(ev, reads, writes)
        if final:
            self.finals.append(ev)
        self.n_ops += 1
        return ev


    def collective(self, groups, in_ap, out_ap, reads=(), writes=()):
        evs = self._deps(reads, writes)
        sem = self.stack.enter_context(self.nc.semaphore("cc_sem%d" % len(self.cc_list)))
        waits = self._need('pool', evs)
        ev = (sem, 1)
        self.cc_list.append(ev)
        self.q['pool'].append((waits, lambda e: e.collective_compute(
            "AllGather", ALU.bypass, replica_groups=groups, ins=[in_ap.opt()], outs=[out_ap.opt()]), sem, 1))
        self._record(ev, reads, writes)
        return ev

    def barrier(self):
        evs = []
        for e in ('act', 'dve', 'pool', 'pe'):
            if self.cnt[e]:
                evs.append((self.csem[e], self.cnt[e]))
        for e, _ in self.ndma:
            for s, v in zip(self.dsems[e], self.dval[e]):
                if v:
                    evs.append((s, v))
        evs.extend(self.cc_list)
        for e in ENGS:
            waits = self._need(e, evs)
            if waits:
                self.q[e].append((waits, None, None, 0))

    @staticmethod
    def _t(ts):
        return [t for t in ts if t is not None]

    def mm(self, pt, pap, lt, lap, rt, rap, start=True, stop=True):
        self.op('pe', lambda e: e.matmul(pap, lap, rap, start=start, stop=stop),
                reads=self._t((lt, rt)), writes=(pt,))

    def tr(self, pt, pap, it, iap, idap):
        self.op('pe', lambda e: e.transpose(pap, iap, idap), reads=self._t((it,)), writes=(pt,))

    def act(self, ot, oap, it, iap, func, bias=None, scale=None, rd=()):
        kw = {}
        if bias is not None:
            kw['bias'] = bias
        if scale is not None:
            kw['scale'] = scale
        self.op('act', lambda e: e.activation(oap, iap, func, **kw),
                reads=self._t((it,) + tuple(rd)), writes=(ot,))

    def tt(self, eng, ot, oap, at, aap, bt, bap, op):
        self.op(eng, lambda e: e.tensor_tensor(oap, aap, bap, op), reads=self._t((at, bt)), writes=(ot,))

    def ts(self, eng, ot, oap, it, iap, s1, s2, op0, op1=None, rd=()):
        if op1 is None:
            self.op(eng, lambda e: e.tensor_scalar(oap, iap, s1, s2, op0),
                    reads=self._t((it,) + tuple(rd)), writes=(ot,))
        else:
            self.op(eng, lambda e: e.tensor_scalar(oap, iap, s1, s2, op0, op1),
                    reads=self._t((it,) + tuple(rd)), writes=(ot,))

    def stt(self, ot, oap, at, aap, sc, bt, bap, op0, op1, rd=()):
        self.op('dve', lambda e: e.scalar_tensor_tensor(oap, aap, sc, bap, op0, op1),
                reads=self._t((at, bt) + tuple(rd)), writes=(ot,))

    def cp(self, eng, ot, oap, it, iap):
        if eng == 'act':
            self.op('act', lambda e: e.copy(oap, iap), reads=self._t((it,)), writes=(ot,))
        else:
            self.op(eng, lambda e: e.tensor_copy(oap, iap), reads=self._t((it,)), writes=(ot,))

    def ms(self, eng, t, ap, val):
        self.op(eng, lambda e: e.memset(ap, val), writes=(t,))

    def recip(self, ot, oap, it, iap):
        self.op('dve', lambda e: e.reciprocal(oap, iap), reads=self._t((it,)), writes=(ot,))

    def finish(self):
        evs = list(self.finals)
        for e in ('act', 'dve', 'pool', 'pe'):
            if self.cnt[e]:
                evs.append((self.csem[e], self.cnt[e]))
        for e, _ in self.ndma:
            for s, v in zip(self.dsems[e], self.dval[e]):
                if v:
                    evs.append((s, v))
        evs.extend(self.cc_list)
        waits = self._need('sync', evs)
        q = self.q
        q['sync'].append((waits, None, None, 0))
        with self.nc.Block() as block:
            for e in ENGS:
                lst = q[e]
                if not lst:
                    continue

                def body(eng, lst=lst):
                    for waits, fn, sem, inc in lst:
                        for s, v in waits:
                            eng.wait_ge(s, v)
                        if fn is not None:
                            fn(eng).then_inc(sem, inc)
                getattr(block, BLOCK_NAME[e])(body)


D = 2048
KC = 16
CTX = 256
FFN = 5632
FC = 44
HV = 4096
DEPTH = 4
RMS_EPS = 1e-6
LN_EPS = 1e-5
MIXER = {0: 'ret', 1: 'att', 2: 'cnv', 3: 'ret'}
ARENA = 47000


def _sv_layout():
    lay, n = {}, 0
    for name, w in (('call', 64), ('bm', 4), ('cc', 16), ('mod_b', 4 * 96), ('n1w', 64), ('n2w', 64),
                    ('fdw', 4 * 3 * FC), ('fdwb', 4 * FC), ('qg', 1), ('kg', 1), ('cb1', 32),
                    ('cdw', 31 * 16), ('cdwb', 16), ('clnw', 16), ('clnb', 16), ('cb2', 16), ('dec', 32), ('msk', 2), ('gnw', 64)):
        lay[name] = n
        n += w
    return lay, n


SVL, NSV = _sv_layout()
CSTL = {'ident': 0, 'ones': 128, 'perm': 256, 'd1': 384, 'd2': 512, 'u': 640, 'l': 768,
        'p1': 896, 'cmp': 897, 'c1p': 898, 'p0': 899, 'r1': 900, 'r2': 1028}
NCST = 1156


def _chunked(v):
    v = np.asarray(v, np.float32).reshape(-1, 128)
    return np.ascontiguousarray(v.T)


def pack_sv(inp, b, r=0):
    sv = np.zeros((128, NSV), np.float32)

    def put(name, arr):
        o = SVL[name]
        sv[:, o:o + arr.shape[1]] = arr
    ft = (2, 1, 0) if r else (0, 1, 2)
    ct = tuple(range(30, -1, -1)) if r else tuple(range(31))
    put('call', np.concatenate([_chunked(inp['c'][i]) for i in range(4)], axis=1))
    bm = np.zeros((128, 4), np.float32)
    bm[:, b] = 1.0
    put('bm', bm)
    put('cc', _chunked(inp['c_ctx']))
    put('mod_b', np.concatenate([_chunked(inp['mod_b'][l]) for l in range(4)], axis=1))
    put('n1w', np.concatenate([_chunked(inp['norm1_w'][l]) for l in range(4)], axis=1))
    put('n2w', np.concatenate([_chunked(inp['norm2_w'][l]) for l in range(4)], axis=1))
    put('fdw', np.concatenate([_chunked(inp['ffn_dw'][l][t]) for l in range(4) for t in ft], axis=1))
    put('fdwb', np.concatenate([_chunked(inp['ffn_dw_b'][l]) for l in range(4)], axis=1))
    put('qg', _chunked(inp['att_q_gain'][0]))
    put('kg', _chunked(inp['att_k_gain'][0]))
    put('cb1', _chunked(inp['cnv_b1'][0]))
    put('cdw', np.concatenate([_chunked(inp['cnv_dw'][0][t]) for t in ct], axis=1))
    put('cdwb', _chunked(inp['cnv_dw_b'][0]))
    put('clnw', _chunked(inp['cnv_ln_w'][0]))
    put('clnb', _chunked(inp['cnv_ln_b'][0]))
    put('cb2', _chunked(inp['cnv_b2'][0]))
    dec = np.asarray(inp['ret_decay'], np.float32)
    if r:
        dec = dec[:, ::-1, :]
    put('dec', np.broadcast_to(np.ascontiguousarray(dec).reshape(1, 32), (128, 32)))
    put('gnw', np.concatenate([_chunked(inp['ret_gn_w'][j]) for j in range(2)], axis=1))
    msk = np.zeros((128, 2), np.float32)
    msk[:, 1 - r] = 1.0
    put('msk', msk)
    return sv


def make_consts():
    cst = np.zeros((128, NCST), np.float32)
    p = np.arange(128)
    cst[:, 0:128] = np.eye(128)
    cst[:, 128:256] = 1.0
    perm = np.where((p % 64) < 32, p + 32, p - 32)
    pm = np.zeros((128, 128), np.float32)
    pm[perm, p] = 1.0
    cst[:, 256:384] = pm
    m = p[:, None].astype(np.float32)
    n = p[None, :].astype(np.float32)
    cst[:, 384:512] = np.maximum(n - m, 0)
    cst[:, 512:640] = np.maximum(m - n, 0)
    cst[:, 640:768] = (n >= m)
    cst[:, 768:896] = (m >= n)
    cst[:, 896] = p + 1
    cst[:, 897] = 128 - p
    cst[:, 898] = 127 - p
    cst[:, 899] = p
    cst[:, 900:1028] = np.broadcast_to(np.arange(1, 129, dtype=np.float32), (128, 128))
    cst[:, 1028:1156] = np.broadcast_to(128.0 - np.arange(128, dtype=np.float32), (128, 128))
    return cst


def make_rope(tpos):
    p = np.arange(128)
    t = np.asarray(tpos)
    row, col = t // 64, t % 64
    ii = p % 64
    fi = ii % 32
    inv = (10000.0 ** (-(np.arange(32, dtype=np.float32)) / 32.0)).astype(np.float32)
    pos = np.where((p < 64)[:, None], row[None, :], col[None, :]).astype(np.float32)
    ang = pos * inv[fi][:, None]
    cos = np.cos(ang).astype(np.float32)
    sin = np.sin(ang).astype(np.float32)
    sin = np.where((ii < 32)[:, None], -sin, sin).astype(np.float32)
    return np.ascontiguousarray(cos), np.ascontiguousarray(sin)


def _split(n, m):
    k = -(-n // m)
    base, rem = divmod(n, k)
    return [base + (1 if i < rem else 0) for i in range(k)]


class Arena:
    def __init__(self, tile):
        self.t = tile
        self.off = 0

    def reset(self):
        self.off = 0

    def alloc(self, name, fshape, dtype):
        n = 1
        for s in fshape:
            n *= s
        n32 = n if dtype == F32 else (n + 1) // 2
        assert self.off + n32 <= ARENA, (name, self.off, n32)
        ap = self.t.ap[:, self.off:self.off + n32]
        self.off += n32
        if dtype != F32:
            ap = ap.bitcast(dtype)[:, 0:n]
        if len(fshape) == 2:
            ap = ap.rearrange("p (a b) -> p a b", a=fshape[0])
        elif len(fshape) == 3:
            ap = ap.rearrange("p (a b c) -> p a b c", a=fshape[0], b=fshape[1])
        return Tile(name, ap)


def build(nlat=4, layers=(0, 1, 2, 3), do_mix=True, do_ffn=True, ncores=8):
    L = nlat * 512
    NT = CTX + L
    NCH = NT // 128
    nc = bass.Bass("TRN2", target_bir_lowering=False)

    def din(name, shape):
        return nc.dram_tensor(name, list(shape), F32, kind="ExternalInput").ap()

    def dscr(name, shape, dt):
        return nc.dram_tensor(name, list(shape), dt).ap()

    xT_in = din("xT", [D, NT])
    sv_d = din("sv", [128, NSV])
    cst_d = din("cst", [128, NCST])
    rcos = din("rope_cos", [128, L])
    rsin = din("rope_sin", [128, L])
    Wt = {}
    for l in layers:
        Wt['mw', l] = din("mw%d" % l, [D, (96 // ncores) * 128])
        if do_ffn:
            Wt['wg', l] = din("wg%d" % l, [D, FFN])
            Wt['wu', l] = din("wu%d" % l, [D, FFN])
            Wt['wd', l] = din("wd%d" % l, [FFN, D])
        if not do_mix:
            continue
        if MIXER[l] == 'ret':
            Wt['rq', l] = din("rq%d" % l, [D, D])
            Wt['rk', l] = din("rk%d" % l, [D, D])
            Wt['rv', l] = din("rv%d" % l, [D, HV])
            Wt['rg', l] = din("rg%d" % l, [D, HV])
            Wt['ro', l] = din("ro%d" % l, [HV, D])
            Wt['gn', l] = din("gn%d" % l, [1, HV])
        elif MIXER[l] == 'att':
            Wt['aq', l] = din("aq%d" % l, [D, D])
            Wt['akv', l] = din("akv%d" % l, [D, 1024])
            Wt['ao', l] = din("ao%d" % l, [D, D])
        else:
            Wt['c1', l] = din("c1_%d" % l, [D, 2 * D])
            Wt['c2', l] = din("c2_%d" % l, [D, D])
    outT = nc.dram_tensor("outT", [D, L], F32, kind="ExternalOutput").ap()
    XA = dscr("XA", [D, NT], F32)
    XB = dscr("XB", [D, NT], F32)
    QT = dscr("QT", [16, 128, NT], BF16)
    KT = dscr("KT", [16, 128, NT], BF16)
    KTOK = dscr("KTOK", [NCH, 128, D], BF16)
    VTOK = dscr("VTOK", [NCH, 128, HV], BF16)
    GTOK = dscr("GTOK", [NCH, 128, HV], BF16)
    ATS = dscr("ATS", [32, 128, NT], BF16)
    AFF = dscr("AFF", [FC, 128, NT], BF16)
    GT = dscr("GT", [16, 128, NT], F32)
    CT = dscr("CT", [16, 128, NT], F32)
    SBD = dscr("SBD", [8, NCH, 128, 1024], BF16)
    NLC = L // 128
    GROUPS = [[2 * i, 2 * i + 1] for i in range(ncores // 2)]
    NB = ncores // 2
    FPC = 96 // ncores
    NCOL = NB + 1
    nL = len(layers)
    MD_in = dscr("MD_in", [128, nL * FPC * NCOL], F32)
    MD_out = dscr("MD_out", [ncores, 128, nL * FPC * NCOL], F32)
    XH_in = dscr("XH_in", [D], F32)
    XH_out = dscr("XH_out", [2, D], F32)
    CH_in = dscr("CH_in", [16, 128, 15], F32)
    CH_out = dscr("CH_out", [2, 16, 128, 15], F32)
    KX_in = dscr("KX_in", [4, 128, L], BF16)
    KX_out = dscr("KX_out", [2, 4, 128, L], BF16)
    VX_in = dscr("VX_in", [NLC, 128, 512], BF16)
    VX_out = dscr("VX_out", [2, NLC, 128, 512], BF16)
    E_in = [dscr("E_in%d" % i, [4, 128, 1024], F32) for i in range(2)]
    E_out = [dscr("E_out%d" % i, [2, 4, 128, 1024], F32) for i in range(2)]

    def fm(ap2d):
        return ap2d.rearrange("(c p) n -> p c n", p=128)

    def hm(ap3d):
        return ap3d.rearrange("r p n -> p r n")

    XA3, XB3, OUT3 = fm(XA), fm(XB), fm(outT)

    mixer_tiles = [(0, 256, 1)] + [(CTX + 512 * i, 512, 0) for i in range(nlat)]
    ffn_tiles = [(0, 256, True, True, 1)]
    t0 = CTX
    sizes = _split(L, 510)
    for i, T in enumerate(sizes):
        ffn_tiles.append((t0, T, i == 0, 'partner' if i == len(sizes) - 1 else False, 0))
        t0 += T

    S = Sched(nc)
    with S:
        svt = S.sbuf("svt", [128, NSV], F32)
        cstt = S.sbuf("cstt", [128, NCST], F32)
        cb = S.sbuf("cb", [128, 384], BF16)
        modv = S.sbuf("modv", [128, 4, 96, 2], F32)
        AB = S.sbuf("AB", [128, 4, 2, 16, 2], F32)
        b2g = S.sbuf("b2g", [128, 16, 2], F32)
        sbf = S.sbuf("sbf", [128, 16, NCOL], BF16)
        epsT = S.sbuf("epsT", [128, 2], F32)
        lgT = S.sbuf("lgT", [128, 32], F32)
        hx = S.sbuf("hx", [128, 16], F32)
        hx2 = S.sbuf("hx2", [128, 2, 16], F32)
        arena_t = S.sbuf("arena", [128, ARENA], F32)
        A = Arena(arena_t)
        PS = [S.psum("ps%d" % i, [128, 512], F32) for i in range(8)]

        def sv(name, idx=0, n=1):
            o = SVL[name] + idx
            return svt.ap[:, o:o + n]

        def cst(name, n=128):
            o = CSTL[name]
            return cstt.ap[:, o:o + n]

        msk0, msk1 = sv('msk', 0), sv('msk', 1)

        def exchange(in_t, out_t):
            r_in, r_out = S.region('xin'), S.region('xout')
            S.collective(GROUPS, in_t, out_t, reads=[r_in], writes=[r_out])
            S.barrier()

        def halo_exchange(X3):
            S.dma('sync', XH_in.rearrange("(c p) -> p c", p=128), X3[:, :, NT - 1], allow_slow_non_contiguous=True)
            S.barrier()
            exchange(XH_in, XH_out)
            S.dma('sync', hx2.ap, XH_out.rearrange("r (c p) -> p r c", p=128), writes=[hx2], allow_slow_non_contiguous=True)
            S.ts('dve', hx, hx.ap, hx2, hx2.ap[:, 0, :], msk0, None, ALU.mult)
            S.stt(hx, hx.ap, hx2, hx2.ap[:, 1, :], msk1, hx, hx.ap, ALU.mult, ALU.add)
            S.barrier()

        ident_bf, ones_bf, perm_bf = cb.ap[:, 0:128], cb.ap[:, 128:256], cb.ap[:, 256:384]
        eps_rms, eps_ln = epsT.ap[:, 0:1], epsT.ap[:, 1:2]

        def mA(l, w, k, j):
            return AB.ap[:, l, w, k, j:j + 1]

        def mB(l, w, k, j):
            return modv.ap[:, l, (0 if w == 0 else 48) + k, j:j + 1]

        def mG(l, w, k, j):
            return modv.ap[:, l, (32 if w == 0 else 80) + k, j:j + 1]

        def run_stream(jobs, bufs):
            nb = len(bufs)
            n = len(jobs)

            def load(s):
                buf = bufs[s % nb]
                for (W2, kc, c0, ncols, off) in jobs[s]['loads']:
                    dst = buf.ap[:, off:off + kc * ncols].rearrange("p (c n) -> p c n", c=kc)
                    S.dma('pool', dst, W2[:, c0:c0 + ncols].rearrange("(c p) n -> p c n", p=128), writes=[buf])
            for s in range(min(nb - 1, n)):
                load(s)
            for s in range(n):
                if s + nb - 1 < n:
                    load(s + nb - 1)
                if jobs[s].get('pre') is not None:
                    jobs[s]['pre']()
                jobs[s]['fn'](bufs[s % nb])

        def wview(buf, off, kc, ncols):
            return buf.ap[:, off:off + kc * ncols].rearrange("p (c n) -> p c n", c=kc)

        def norm_mod(xt, W, l, w, j, hT, sq, sd, rs, tmp, hoff=0, bank=6):
            pst = PS[bank]
            for k in range(KC):
                q = sq[k % 2]
                S.act(q, q.ap[:, :W], xt, xt.ap[:, k, :W], AF.Square)
                S.mm(pst, pst.ap[:, :W], None, ones_bf, q, q.ap[:, :W], k == 0, k == KC - 1)
            S.act(sd, sd.ap[:, :W], pst, pst.ap[:, :W], AF.Sqrt, bias=eps_rms, scale=1.0 / D)
            S.recip(rs, rs.ap[:, :W], sd, sd.ap[:, :W])
            for k in range(KC):
                t = tmp[k % 2]
                S.tt('dve', t, t.ap[:, :W], xt, xt.ap[:, k, :W], rs, rs.ap[:, :W], ALU.mult)
                S.act(hT, hT.ap[:, k, hoff:hoff + W], t, t.ap[:, :W], AF.Identity, bias=mB(l, w, k, j), scale=mA(l, w, k, j))

        S.dma('sync', svt.ap, sv_d, writes=[svt])
        S.dma('sync', cstt.ap, cst_d, writes=[cstt])
        S.dma('sync', XA, xT_in)
        S.cp('dve', cb, cb.ap, cstt, cstt.ap[:, 0:384])
        S.ms('pool', epsT, epsT.ap[:, 0:1], RMS_EPS)
        S.ms('pool', epsT, epsT.ap[:, 1:2], LN_EPS)
        for b_ in range(NB):
            S.act(sbf, sbf.ap[:, :, b_], svt, sv('call', 16 * b_, 16), AF.Silu)
        S.act(sbf, sbf.ap[:, :, NB], svt, sv('cc', 0, 16), AF.Silu)
        S.act(lgT, lgT.ap, svt, sv('dec', 0, 32), AF.Exp)
        S.ts('dve', lgT, lgT.ap, lgT, lgT.ap, -1.0, None, ALU.mult)
        A.reset()
        wb = [A.alloc('wb%d' % i, (8192,), BF16) for i in range(3)]
        mdp = A.alloc('mdp', (nL, FPC, NCOL), F32)
        modall = A.alloc('modall', (nL, 96, NCOL), F32)
        nsl = (FPC * 128) // 512
        for li, l in enumerate(layers):
            pm = PS[li % 2]
            jobs = []
            for s_ in range(nsl):
                def fn(buf, s_=s_, pm=pm):
                    wv = wview(buf, 0, KC, 512)
                    for f4 in range(4):
                        f = s_ * 4 + f4
                        for k in range(KC):
                            S.mm(pm, pm.ap[:, NCOL * f:NCOL * (f + 1)], buf, wv[:, k, f4 * 128:(f4 + 1) * 128],
                                 sbf, sbf.ap[:, k, :], k == 0, k == KC - 1)
                jobs.append(dict(loads=[(Wt['mw', l], KC, s_ * 512, 512, 0)], fn=fn))
            run_stream(jobs, wb)
            S.cp('act', mdp, mdp.ap[:, li], pm, pm.ap[:, 0:FPC * NCOL].rearrange("p (f c) -> p f c", c=NCOL))
        S.dma('sync', MD_in, mdp.ap.rearrange("p l f c -> p (l f c)"), reads=[mdp])
        S.barrier()
        S.collective([list(range(ncores))], MD_in, MD_out)
        S.barrier()
        for li in range(nL):
            S.dma('sync', modall.ap[:, li].rearrange("p (r f) c -> p r f c", r=ncores),
                  MD_out.rearrange("r p (l f c) -> l p r f c", l=nL, f=FPC)[li], writes=[modall])
        for li, l in enumerate(layers):
            mb = sv('mod_b', l * 96, 96)
            S.tt('dve', modv, modv.ap[:, l, :, 1], modall, modall.ap[:, li, :, NB], svt, mb, ALU.add)
            S.stt(modv, modv.ap[:, l, :, 0], modall, modall.ap[:, li, :, 0], sv('bm', 0), svt, mb, ALU.mult, ALU.add)
            for b_ in range(1, NB):
                S.stt(modv, modv.ap[:, l, :, 0], modall, modall.ap[:, li, :, b_], sv('bm', b_),
                      modv, modv.ap[:, l, :, 0], ALU.mult, ALU.add)
        for l in layers:
            for w in range(2):
                base = 16 if w == 0 else 64
                nw = sv('n1w' if w == 0 else 'n2w', l * 16, 16)
                for j in range(2):
                    S.ts('dve', AB, AB.ap[:, l, w, :, j], modv, modv.ap[:, l, base:base + 16, j], 1.0, None, ALU.add)
                    S.tt('dve', AB, AB.ap[:, l, w, :, j], AB, AB.ap[:, l, w, :, j], svt, nw, ALU.mult)
            if MIXER[l] == 'cnv':
                for j in range(2):
                    S.tt('dve', b2g, b2g.ap[:, :, j], svt, sv('cb2', 0, 16), modv, modv.ap[:, l, 32:48, j], ALU.mult)
        S.barrier()

        LB = CTX + 2
        NE = LB + L + 2

        def ffn_phase(l, XI3, XO3, final):
            skip_ctx = (l == DEPTH - 1)
            wins = ([] if skip_ctx else [(0, CTX, 1, 1)]) + [(CTX + 512 * i, 512, 0, LB + 1 + 512 * i) for i in range(nlat)]
            A.reset()
            hT = A.alloc('hT', (KC, NE), BF16)
            xt = A.alloc('xt', (KC, 256), F32)
            wb = [A.alloc('wb%d' % i, (8192,), BF16) for i in range(2)]
            gseg = [A.alloc('gseg%d' % i, (NE,), F32) for i in range(2)]
            acc = A.alloc('acc', (NT,), F32)
            ast = [A.alloc('ast%d' % i, (NT,), BF16) for i in range(2)]
            sq = [A.alloc('sq%d' % i, (256,), BF16) for i in range(2)]
            tmp = [A.alloc('tmp%d' % i, (256,), F32) for i in range(2)]
            sd = A.alloc('sd', (256,), F32)
            rs = A.alloc('rs', (256,), F32)
            for zc in (0, CTX + 1, LB):
                S.ms('pool', hT, hT.ap[:, :, zc:zc + 1], 0.0)
                for g_ in gseg:
                    S.ms('pool', g_, g_.ap[:, zc:zc + 1], 0.0)
            pieces = ([] if skip_ctx else [(0, 256, 1, 1)]) + [(CTX + 256 * i, 256, 0, LB + 1 + 256 * i) for i in range(L // 256)]
            for (c0, Wp, j, e0) in pieces:
                S.dma('sync', xt.ap[:, :, :Wp], XI3[:, :, c0:c0 + Wp], writes=[xt])
                norm_mod(xt, Wp, l, 1, j, hT, sq, sd, rs, tmp, hoff=e0, bank=7)
            S.cp('pool', xt, xt.ap[:, :, 0], hx, hx.ap)
            norm_mod(xt, 1, l, 1, 0, hT, sq, sd, rs, tmp, hoff=LB + L + 1, bank=7)
            jobs = []
            cnt = [0, 0]
            for j2 in range(FC // 2):
                def fn(buf, j2=j2):
                    wgv = wview(buf, 0, KC, 256)
                    wuv = wview(buf, 4096, KC, 256)
                    for jj in range(2):
                        jh = 2 * j2 + jj
                        gs = gseg[jh % 2]
                        pus = []
                        for (c0, T, j, e0) in wins:
                            pg = PS[cnt[0] % 3]
                            cnt[0] += 1
                            for k in range(KC):
                                S.mm(pg, pg.ap[:, :T], buf, wgv[:, k, jj * 128:(jj + 1) * 128],
                                     hT, hT.ap[:, k, e0:e0 + T], k == 0, k == KC - 1)
                            S.cp('act', gs, gs.ap[:, e0:e0 + T], pg, pg.ap[:, :T])
                        pg = PS[3]
                        eh = LB + L + 1
                        for k in range(KC):
                            S.mm(pg, pg.ap[:, 0:1], buf, wgv[:, k, jj * 128:(jj + 1) * 128], hT, hT.ap[:, k, eh:eh + 1], k == 0, k == KC - 1)
                        S.cp('act', gs, gs.ap[:, eh:eh + 1], pg, pg.ap[:, 0:1])
                        segs = ([] if skip_ctx else [(0, CTX, 0)]) + [(CTX, L, LB)]
                        for (a0, n, e) in segs:
                            S.act(acc, acc.ap[:, a0:a0 + n], gs, gs.ap[:, e + 1:e + 1 + n], AF.Identity,
                                  bias=sv('fdwb', l * FC + jh), scale=sv('fdw', (l * 3 + 1) * FC + jh))
                            S.stt(acc, acc.ap[:, a0:a0 + n], gs, gs.ap[:, e:e + n], sv('fdw', (l * 3 + 0) * FC + jh),
                                  acc, acc.ap[:, a0:a0 + n], ALU.mult, ALU.add)
                            S.stt(acc, acc.ap[:, a0:a0 + n], gs, gs.ap[:, e + 2:e + 2 + n], sv('fdw', (l * 3 + 2) * FC + jh),
                                  acc, acc.ap[:, a0:a0 + n], ALU.mult, ALU.add)
                        lo = CTX if skip_ctx else 0
                        S.act(acc, acc.ap[:, lo:NT], acc, acc.ap[:, lo:NT], AF.Silu)
                        a_ = ast[jh % 2]
                        for (c0, T, j, e0) in wins:
                            pu = PS[4 + cnt[1] % 3]
                            cnt[1] += 1
                            for k in range(KC):
                                S.mm(pu, pu.ap[:, :T], buf, wuv[:, k, jj * 128:(jj + 1) * 128],
                                     hT, hT.ap[:, k, e0:e0 + T], k == 0, k == KC - 1)
                            S.tt('dve', a_, a_.ap[:, c0:c0 + T], acc, acc.ap[:, c0:c0 + T], pu, pu.ap[:, :T], ALU.mult)
                        S.dma('sync', AFF[jh, :, lo:NT], a_.ap[:, lo:NT], reads=[a_])
                jobs.append(dict(loads=[(Wt['wg', l], KC, j2 * 256, 256, 0), (Wt['wu', l], KC, j2 * 256, 256, 4096)], fn=fn))
            run_stream(jobs, wb)
            S.barrier()
            A.reset()
            wd = A.alloc('wd', (FC, 512), BF16)
            at = [A.alloc('at%d' % i, (FC, 512), BF16) for i in range(2)]
            xo = [A.alloc('xo%d' % i, (4, 512), F32) for i in range(2)]
            AFF3 = hm(AFF)
            it = 0
            seq = [(fs, w) for fs in range(4) for w in wins]

            def load_at(i):
                (c0, T, j, e0) = seq[i][1]
                a_ = at[i % 2]
                S.dma('sync', a_.ap[:, :, :T], AFF3[:, :, c0:c0 + T], writes=[a_])
            load_at(0)
            for i, (fs, (c0, T, j, e0)) in enumerate(seq):
                if i % len(wins) == 0:
                    S.dma('pool', wd.ap, Wt['wd', l][:, fs * 512:(fs + 1) * 512].rearrange("(c p) n -> p c n", p=128), writes=[wd])
                if i + 1 < len(seq):
                    load_at(i + 1)
                a_, x_ = at[i % 2], xo[i % 2]
                S.dma('sync', x_.ap[:, :, :T], XI3[:, 4 * fs:4 * fs + 4, c0:c0 + T], writes=[x_])
                for ff in range(4):
                    f = 4 * fs + ff
                    po = PS[(4 * i + ff) % 4]
                    for jh in range(FC):
                        S.mm(po, po.ap[:, :T], wd, wd.ap[:, jh, ff * 128:(ff + 1) * 128], a_, a_.ap[:, jh, :T], jh == 0, jh == FC - 1)
                    S.stt(x_, x_.ap[:, ff, :T], po, po.ap[:, :T], mG(l, 1, f, j), x_, x_.ap[:, ff, :T], ALU.mult, ALU.add)
                if final and j == 1:
                    pass
                elif final:
                    S.dma('sync', OUT3[:, 4 * fs:4 * fs + 4, c0 - CTX:c0 - CTX + T], x_.ap[:, :, :T], reads=[x_], final=True)
                else:
                    S.dma('sync', XO3[:, 4 * fs:4 * fs + 4, c0:c0 + T], x_.ap[:, :, :T], reads=[x_])
            S.barrier()

        def copy_phase(XI, XO):
            S.dma('sync', XO, XI)
            S.barrier()

        def cnv_phase(l, XI3, XO3):
            A.reset()
            xt = A.alloc('xt', (KC, 512), F32)
            hT = A.alloc('hT', (KC, 512), BF16)
            gl = A.alloc('gl', (KC, 512), F32)
            wb = [A.alloc('wb%d' % i, (8192,), BF16) for i in range(3)]
            sgm = [A.alloc('sgm%d' % i, (512,), F32) for i in range(2)]
            sq = [A.alloc('sq%d' % i, (512,), BF16) for i in range(2)]
            tmp = [A.alloc('tmp%d' % i, (512,), F32) for i in range(2)]
            sd = A.alloc('sd', (512,), F32)
            rs = A.alloc('rs', (512,), F32)
            GT3 = hm(GT)
            jobs = []
            for (t0, T, j) in mixer_tiles:
                def pre(t0=t0, T=T, j=j):
                    S.dma('sync', xt.ap[:, :, :T], XI3[:, :, t0:t0 + T], writes=[xt])
                    norm_mod(xt, T, l, 0, j, hT, sq, sd, rs, tmp)
                for c in range(KC):
                    def fn(buf, c=c, t0=t0, T=T):
                        wa = wview(buf, 0, KC, 128)
                        wg_ = wview(buf, 2048, KC, 128)
                        pa, pb = PS[c % 2], PS[2 + c % 2]
                        for k in range(KC):
                            S.mm(pa, pa.ap[:, :T], buf, wa[:, k, :], hT, hT.ap[:, k, :T], k == 0, k == KC - 1)
                        for k in range(KC):
                            S.mm(pb, pb.ap[:, :T], buf, wg_[:, k, :], hT, hT.ap[:, k, :T], k == 0, k == KC - 1)
                        g_ = sgm[c % 2]
                        S.act(g_, g_.ap[:, :T], pb, pb.ap[:, :T], AF.Sigmoid, bias=sv('cb1', 16 + c))
                        S.stt(gl, gl.ap[:, c, :T], pa, pa.ap[:, :T], sv('cb1', c), g_, g_.ap[:, :T], ALU.add, ALU.mult)
                        if c == KC - 1:
                            S.dma('sync', GT3[:, :, t0:t0 + T], gl.ap[:, :, :T], reads=[gl])
                    jobs.append(dict(pre=pre if c == 0 else None,
                                     loads=[(Wt['c1', l], KC, c * 128, 128, 0), (Wt['c1', l], KC, D + c * 128, 128, 2048)],
                                     fn=fn))
            run_stream(jobs, wb)
            S.barrier()
            S.dma('sync', CH_in.rearrange("c p d -> p c d"), GT3[:, :, NT - 15:NT])
            S.barrier()
            exchange(CH_in, CH_out)
            A.reset()
            LB2 = CTX + 30
            NE2 = LB2 + L + 30
            ch2 = A.alloc('ch2', (2, KC, 15), F32)
            hal = A.alloc('hal', (KC, 15), F32)
            gxr = [A.alloc('gxr%d' % i, (NE2,), F32) for i in range(2)]
            gxb = [A.alloc('gxb%d' % i, (NE2,), BF16) for i in range(2)]
            dgs = [A.alloc('dg%d' % i, (31, 128), BF16) for i in range(2)]
            cts = [A.alloc('ct%d' % i, (NT,), F32) for i in range(2)]
            S.dma('sync', ch2.ap, CH_out.rearrange("r c p d -> p r c d"), writes=[ch2])
            for d in range(15):
                S.ts('dve', hal, hal.ap[:, :, d], ch2, ch2.ap[:, 0, :, 14 - d], msk0, None, ALU.mult)
                S.stt(hal, hal.ap[:, :, d], ch2, ch2.ap[:, 1, :, 14 - d], msk1, hal, hal.ap[:, :, d], ALU.mult, ALU.add)
            for g_ in gxr:
                for z0 in (0, 15 + CTX, LB2):
                    S.ms('pool', g_, g_.ap[:, z0:z0 + 15], 0.0)
            wi = 0
            for k in range(KC):
                gr, gbf, dgk, ctk = gxr[k % 2], gxb[k % 2], dgs[k % 2], cts[k % 2]
                S.dma('sync', gr.ap[:, 15:15 + CTX], GT[k, :, 0:CTX], writes=[gr])
                S.dma('sync', gr.ap[:, LB2 + 15:LB2 + 15 + L], GT[k, :, CTX:NT], writes=[gr])
                S.cp('dve', gr, gr.ap[:, LB2 + 15 + L:NE2], hal, hal.ap[:, k, :])
                S.cp('act', gbf, gbf.ap, gr, gr.ap)
                for tp in range(31):
                    S.ts('dve', dgk, dgk.ap[:, tp, :], None, ident_bf, sv('cdw', tp * 16 + k), None, ALU.mult)
                for (c0, T, e0) in [(0, CTX, 0)] + [(CTX + 512 * i, 512, LB2 + 512 * i) for i in range(nlat)]:
                    ps = PS[wi % 4]
                    wi += 1
                    for tp in range(31):
                        S.mm(ps, ps.ap[:, :T], dgk, dgk.ap[:, tp, :], gbf, gbf.ap[:, e0 + tp:e0 + tp + T], tp == 0, tp == 30)
                    S.act(ctk, ctk.ap[:, c0:c0 + T], ps, ps.ap[:, :T], AF.Identity, bias=sv('cdwb', k))
                S.dma('sync', CT[k], ctk.ap, reads=[ctk])
            S.barrier()
            A.reset()
            CT3 = hm(CT)
            ac = A.alloc('ac', (KC, 512), F32)
            a2 = A.alloc('a2', (KC, 512), BF16)
            xt = A.alloc('xt', (KC, 512), F32)
            wb = [A.alloc('wb%d' % i, (2048,), BF16) for i in range(3)]
            xb = [A.alloc('xb%d' % i, (512,), BF16) for i in range(2)]
            sq = [A.alloc('sq%d' % i, (512,), BF16) for i in range(2)]
            tmp = [A.alloc('tmp%d' % i, (512,), F32) for i in range(2)]
            mean = A.alloc('mean', (512,), F32)
            msq = A.alloc('msq', (512,), F32)
            var = A.alloc('var', (512,), F32)
            rs = A.alloc('rs', (512,), F32)
            jobs = []
            for (t0, T, j) in mixer_tiles:
                if l == DEPTH - 1 and j == 1:
                    continue

                def pre(t0=t0, T=T, j=j):
                    S.dma('sync', ac.ap[:, :, :T], CT3[:, :, t0:t0 + T], writes=[ac])
                    S.dma('sync', xt.ap[:, :, :T], XI3[:, :, t0:t0 + T], writes=[xt])
                    pm_, pq_ = PS[4], PS[5]
                    for k in range(KC):
                        b_ = xb[k % 2]
                        q_ = sq[k % 2]
                        S.cp('act', b_, b_.ap[:, :T], ac, ac.ap[:, k, :T])
                        S.act(q_, q_.ap[:, :T], ac, ac.ap[:, k, :T], AF.Square)
                        S.mm(pm_, pm_.ap[:, :T], None, ones_bf, b_, b_.ap[:, :T], k == 0, k == KC - 1)
                        S.mm(pq_, pq_.ap[:, :T], None, ones_bf, q_, q_.ap[:, :T], k == 0, k == KC - 1)
                    S.act(mean, mean.ap[:, :T], pm_, pm_.ap[:, :T], AF.Copy, scale=1.0 / D)
                    S.tt('dve', msq, msq.ap[:, :T], mean, mean.ap[:, :T], mean, mean.ap[:, :T], ALU.mult)
                    S.stt(var, var.ap[:, :T], pq_, pq_.ap[:, :T], 1.0 / D, msq, msq.ap[:, :T], ALU.mult, ALU.subtract)
                    S.act(var, var.ap[:, :T], var, var.ap[:, :T], AF.Sqrt, bias=eps_ln, scale=1.0)
                    S.recip(rs, rs.ap[:, :T], var, var.ap[:, :T])
                    for k in range(KC):
                        t_ = tmp[k % 2]
                        S.tt('dve', t_, t_.ap[:, :T], ac, ac.ap[:, k, :T], mean, mean.ap[:, :T], ALU.subtract)
                        S.tt('dve', t_, t_.ap[:, :T], t_, t_.ap[:, :T], rs, rs.ap[:, :T], ALU.mult)
                        S.act(a2, a2.ap[:, k, :T], t_, t_.ap[:, :T], AF.Silu, bias=sv('clnb', k), scale=sv('clnw', k))
                for f in range(KC):
                    def fn(buf, f=f, t0=t0, T=T, j=j):
                        wv = wview(buf, 0, KC, 128)
                        pw = PS[f % 2]
                        for k in range(KC):
                            S.mm(pw, pw.ap[:, :T], buf, wv[:, k, :], a2, a2.ap[:, k, :T], k == 0, k == KC - 1)
                        S.stt(xt, xt.ap[:, f, :T], pw, pw.ap[:, :T], mG(l, 0, f, j), xt, xt.ap[:, f, :T], ALU.mult, ALU.add)
                        S.ts('dve', xt, xt.ap[:, f, :T], xt, xt.ap[:, f, :T], b2g.ap[:, f, j:j + 1], None, ALU.add)
                        if f == KC - 1:
                            S.dma('sync', XO3[:, :, t0:t0 + T], xt.ap[:, :, :T], reads=[xt])
                    jobs.append(dict(pre=pre if f == 0 else None, loads=[(Wt['c2', l], KC, f * 128, 128, 0)], fn=fn))
            run_stream(jobs, wb)
            S.barrier()


        def att_phase(l, XI3, XO3):
            A.reset()
            xt = A.alloc('xt', (KC, 512), F32)
            hT = A.alloc('hT', (KC, 512), BF16)
            wb = [A.alloc('wb%d' % i, (8192,), BF16) for i in range(3)]
            sq = [A.alloc('sq%d' % i, (512,), BF16) for i in range(2)]
            tmp = [A.alloc('tmp%d' % i, (512,), F32) for i in range(2)]
            sd = A.alloc('sd', (512,), F32)
            rs = A.alloc('rs', (512,), F32)
            cs = A.alloc('cs', (512,), F32)
            sn = A.alloc('sn', (512,), F32)
            qn = [A.alloc('qn%d' % i, (512,), BF16) for i in range(2)]
            qr = [A.alloc('qr%d' % i, (512,), BF16) for i in range(2)]
            hsd = [A.alloc('hsd%d' % i, (512,), F32) for i in range(2)]
            hrs = [A.alloc('hrs%d' % i, (512,), F32) for i in range(2)]
            t1 = [A.alloc('t1%d' % i, (512,), F32) for i in range(2)]
            t2 = [A.alloc('t2%d' % i, (512,), F32) for i in range(2)]
            vt = [A.alloc('vt%d' % i, (512,), BF16) for i in range(2)]
            cnt = [0]
            jobs = []
            for (t0, T, j) in mixer_tiles:
                def pre(t0=t0, T=T, j=j):
                    S.dma('sync', xt.ap[:, :, :T], XI3[:, :, t0:t0 + T], writes=[xt])
                    if j == 0:
                        S.dma('sync', cs.ap[:, :T], rcos[:, t0 - CTX:t0 - CTX + T], writes=[cs])
                        S.dma('sync', sn.ap[:, :T], rsin[:, t0 - CTX:t0 - CTX + T], writes=[sn])
                    norm_mod(xt, T, l, 0, j, hT, sq, sd, rs, tmp)

                def head(buf, wv, hh, gain, dst, t0, T, j):
                    i = cnt[0]
                    cnt[0] += 1
                    pq, pss, pr = PS[i % 2], PS[2 + i % 2], PS[4 + i % 2]
                    for k in range(KC):
                        S.mm(pq, pq.ap[:, :T], buf, wv[:, k, hh * 128:(hh + 1) * 128], hT, hT.ap[:, k, :T], k == 0, k == KC - 1)
                    q_ = sq[i % 2]
                    S.act(q_, q_.ap[:, :T], pq, pq.ap[:, :T], AF.Square)
                    S.mm(pss, pss.ap[:, :T], None, ones_bf, q_, q_.ap[:, :T], True, True)
                    d_, r_ = hsd[i % 2], hrs[i % 2]
                    S.act(d_, d_.ap[:, :T], pss, pss.ap[:, :T], AF.Sqrt, bias=eps_rms, scale=1.0 / 128)
                    S.recip(r_, r_.ap[:, :T], d_, d_.ap[:, :T])
                    n_ = qn[i % 2]
                    S.stt(n_, n_.ap[:, :T], pq, pq.ap[:, :T], gain, r_, r_.ap[:, :T], ALU.mult, ALU.mult)
                    if j == 0:
                        S.mm(pr, pr.ap[:, :T], None, perm_bf, n_, n_.ap[:, :T], True, True)
                        a_, b_, o_ = t1[i % 2], t2[i % 2], qr[i % 2]
                        S.tt('dve', a_, a_.ap[:, :T], n_, n_.ap[:, :T], cs, cs.ap[:, :T], ALU.mult)
                        S.tt('dve', b_, b_.ap[:, :T], pr, pr.ap[:, :T], sn, sn.ap[:, :T], ALU.mult)
                        S.tt('dve', o_, o_.ap[:, :T], a_, a_.ap[:, :T], b_, b_.ap[:, :T], ALU.add)
                        S.dma('sync', dst[:, t0:t0 + T], o_.ap[:, :T], reads=[o_])
                    else:
                        S.dma('sync', dst[:, t0:t0 + T], n_.ap[:, :T], reads=[n_])

                for s in range(8):
                    def fn(buf, s=s, t0=t0, T=T, j=j):
                        wv = wview(buf, 0, KC, 256)
                        for hh in range(2):
                            head(buf, wv, hh, sv('qg'), QT[2 * s + hh], t0, T, j)
                    jobs.append(dict(pre=pre if s == 0 else None, loads=[(Wt['aq', l], KC, s * 256, 256, 0)], fn=fn))

                def fnk(buf, t0=t0, T=T, j=j):
                    wv = wview(buf, 0, KC, 512)
                    for hh in range(4):
                        head(buf, wv, hh, sv('kg'), KT[hh], t0, T, j)
                jobs.append(dict(loads=[(Wt['akv', l], KC, 0, 512, 0)], fn=fnk))

                def fnv(buf, t0=t0, T=T):
                    wv = wview(buf, 0, KC, 512)
                    for c in range(T // 128):
                        pv = PS[6 + c % 2]
                        for k in range(KC):
                            S.mm(pv, pv.ap, hT, hT.ap[:, k, c * 128:(c + 1) * 128], buf, wv[:, k, :], k == 0, k == KC - 1)
                        v_ = vt[c % 2]
                        S.cp('act', v_, v_.ap, pv, pv.ap)
                        S.dma('sync', VTOK[t0 // 128 + c, :, 0:512], v_.ap, reads=[v_])
                jobs.append(dict(loads=[(Wt['akv', l], KC, 512, 512, 0)], fn=fnv))
            run_stream(jobs, wb)
            S.barrier()
            S.dma('sync', KX_in, KT[0:4, :, CTX:NT])
            S.dma('sync', VX_in, VTOK[2:NCH, :, 0:512])
            S.barrier()
            exchange(KX_in, KX_out)
            exchange(VX_in, VX_out)
            A.reset()
            NK = CTX + 2 * L
            NKC = NK // 128
            Ks = A.alloc('Ks', (4, NK), BF16)
            Vs = A.alloc('Vs', (NKC, 512), BF16)
            qs = A.alloc('qs', (KC, 512), BF16)
            at = A.alloc('at', (KC, 512), BF16)
            xt = A.alloc('xt', (KC, 512), F32)
            wb = [A.alloc('wb%d' % i, (4096,), BF16) for i in range(2)]
            Eb = [A.alloc('E%d' % i, (512,), BF16) for i in range(3)]
            rec = [A.alloc('rec%d' % i, (512,), F32) for i in range(2)]
            S.dma('sync', Ks.ap[:, :, 0:CTX], hm(KT)[:, 0:4, 0:CTX], writes=[Ks])
            S.dma('sync', Vs.ap[:, 0:2, :], hm(VTOK)[:, 0:2, 0:512], writes=[Vs])
            for r_ in range(2):
                S.dma('sync', Ks.ap[:, :, CTX + r_ * L:CTX + (r_ + 1) * L], hm(KX_out[r_]), writes=[Ks])
                S.dma('sync', Vs.ap[:, 2 + r_ * NLC:2 + (r_ + 1) * NLC, :], hm(VX_out[r_]), writes=[Vs])
            QT3 = hm(QT)
            scale = 128.0 ** -0.5
            jobs = []
            for (t0, T, j) in mixer_tiles:
                if l == DEPTH - 1 and j == 1:
                    continue
                chunks = [0, 1] if j == 1 else list(range(NKC))

                def pre(t0=t0, T=T, j=j, chunks=chunks):
                    S.dma('sync', qs.ap[:, :, :T], QT3[:, :, t0:t0 + T], writes=[qs])
                    S.dma('sync', xt.ap[:, :, :T], XI3[:, :, t0:t0 + T], writes=[xt])
                    n = len(chunks)
                    for h in range(16):
                        kvh = h // 4
                        po, pd = PS[2 + (h % 2) * 2], PS[3 + (h % 2) * 2]

                        def score(ci):
                            c = chunks[ci]
                            ps = PS[ci % 2]
                            S.mm(ps, ps.ap[:, :T], Ks, Ks.ap[:, kvh, c * 128:(c + 1) * 128], qs, qs.ap[:, h, :T], True, True)
                        score(0)
                        for ci in range(n):
                            if ci + 1 < n:
                                score(ci + 1)
                            c = chunks[ci]
                            ps = PS[ci % 2]
                            E = Eb[ci % 3]
                            S.act(E, E.ap[:, :T], ps, ps.ap[:, :T], AF.Exp, scale=scale)
                            S.mm(po, po.ap[:, :T], Vs, Vs.ap[:, c, kvh * 128:(kvh + 1) * 128], E, E.ap[:, :T], ci == 0, ci == n - 1)
                            S.mm(pd, pd.ap[:, :T], None, ones_bf, E, E.ap[:, :T], ci == 0, ci == n - 1)
                        r_ = rec[h % 2]
                        S.recip(r_, r_.ap[:, :T], pd, pd.ap[:, :T])
                        S.tt('dve', at, at.ap[:, h, :T], po, po.ap[:, :T], r_, r_.ap[:, :T], ALU.mult)

                for s in range(8):
                    def fn(buf, s=s, t0=t0, T=T, j=j):
                        wv = wview(buf, 0, KC, 256)
                        for ff in range(2):
                            f = 2 * s + ff
                            pw = PS[6 + f % 2]
                            for k in range(KC):
                                S.mm(pw, pw.ap[:, :T], buf, wv[:, k, ff * 128:(ff + 1) * 128], at, at.ap[:, k, :T], k == 0, k == KC - 1)
                            S.stt(xt, xt.ap[:, f, :T], pw, pw.ap[:, :T], mG(l, 0, f, j), xt, xt.ap[:, f, :T], ALU.mult, ALU.add)
                            if f == KC - 1:
                                S.dma('sync', XO3[:, :, t0:t0 + T], xt.ap[:, :, :T], reads=[xt])
                    jobs.append(dict(pre=pre if s == 0 else None, loads=[(Wt['ao', l], KC, s * 256, 256, 0)], fn=fn))
            run_stream(jobs, wb)
            S.barrier()


        def ret_phase(l, XI3, XO3):
            jl = l // 3
            A.reset()
            xt = A.alloc('xt', (KC, 512), F32)
            hT = A.alloc('hT', (KC, 512), BF16)
            wb = [A.alloc('wb%d' % i, (8192,), BF16) for i in range(3)]
            sq = [A.alloc('sq%d' % i, (512,), BF16) for i in range(2)]
            tmp = [A.alloc('tmp%d' % i, (512,), F32) for i in range(2)]
            sd = A.alloc('sd', (512,), F32)
            rs = A.alloc('rs', (512,), F32)
            ob = [A.alloc('ob%d' % i, (512,), BF16) for i in range(4)]
            cnt = [0]
            jobs = []
            for (t0, T, j) in mixer_tiles:
                def pre(t0=t0, T=T, j=j):
                    S.dma('sync', xt.ap[:, :, :T], XI3[:, :, t0:t0 + T], writes=[xt])
                    norm_mod(xt, T, l, 0, j, hT, sq, sd, rs, tmp)
                for (wname, dst, scl) in (('rq', QT, 1.0), ('rk', KT, 1.0 / 16)):
                    for s in range(8):
                        def fn(buf, s=s, t0=t0, T=T, dst=dst, scl=scl):
                            wv = wview(buf, 0, KC, 256)
                            for hh in range(2):
                                i = cnt[0]
                                cnt[0] += 1
                                pq = PS[i % 2]
                                for k in range(KC):
                                    S.mm(pq, pq.ap[:, :T], buf, wv[:, k, hh * 128:(hh + 1) * 128], hT, hT.ap[:, k, :T], k == 0, k == KC - 1)
                                o_ = ob[i % 4]
                                S.act(o_, o_.ap[:, :T], pq, pq.ap[:, :T], AF.Copy, scale=scl)
                                S.dma('sync', dst[2 * s + hh, :, t0:t0 + T], o_.ap[:, :T], reads=[o_])
                        jobs.append(dict(pre=pre if (wname == 'rq' and s == 0) else None,
                                         loads=[(Wt[wname, l], KC, s * 256, 256, 0)], fn=fn))
                for (wname, dst, nsl, func, scl) in (('rk', KTOK, 4, AF.Copy, 1.0 / 16), ('rv', VTOK, 8, AF.Copy, 1.0),
                                                     ('rg', GTOK, 8, AF.Silu, 1.0)):
                    for s in range(nsl):
                        def fn(buf, s=s, t0=t0, T=T, dst=dst, func=func, scl=scl):
                            wv = wview(buf, 0, KC, 512)
                            for c in range(T // 128):
                                i = cnt[0]
                                cnt[0] += 1
                                pv = PS[2 + i % 2]
                                for k in range(KC):
                                    S.mm(pv, pv.ap, hT, hT.ap[:, k, c * 128:(c + 1) * 128], buf, wv[:, k, :], k == 0, k == KC - 1)
                                o_ = ob[i % 4]
                                S.act(o_, o_.ap, pv, pv.ap, func, scale=scl)
                                S.dma('sync', dst[t0 // 128 + c, :, s * 512:(s + 1) * 512], o_.ap, reads=[o_])
                        jobs.append(dict(loads=[(Wt[wname, l], KC, s * 512, 512, 0)], fn=fn))
            run_stream(jobs, wb)
            S.barrier()
            QT3, KT3, ATS3 = hm(QT), hm(KT), hm(ATS)
            KTOK3, VTOK3 = hm(KTOK), hm(VTOK)

            def head_consts(hc, h):
                lgf = lgT.ap[:, jl * 16 + h:jl * 16 + h + 1]
                lgb = lgT.ap[:, jl * 16 + 8 + h:jl * 16 + 8 + h + 1]
                S.act(hc, hc.ap[:, 0:1], None, cst('p1', 1), AF.Exp, scale=lgf)
                S.act(hc, hc.ap[:, 1:2], None, cst('cmp', 1), AF.Exp, scale=lgb)
                S.act(hc, hc.ap[:, 2:3], None, cst('c1p', 1), AF.Exp, scale=lgf)
                S.act(hc, hc.ap[:, 3:4], None, cst('p0', 1), AF.Exp, scale=lgb)
                S.act(hc, hc.ap[:, 4:5], None, lgf, AF.Exp, scale=128.0)
                S.act(hc, hc.ap[:, 5:6], None, lgb, AF.Exp, scale=128.0)
                return lgf, lgb

            A.reset()
            Kfs = [A.alloc('Kf%d' % i, (NCH, 256), BF16) for i in range(2)]
            Vs_ = [A.alloc('V%d' % i, (NCH, 512), BF16) for i in range(2)]
            stall = [A.alloc('stall%d' % i, (NCH, 2, 512), BF16) for i in range(2)]
            Sms = [A.alloc('Sm%d' % i, (2, 512), F32) for i in range(2)]
            hcs = [A.alloc('hc%d' % i, (8,), F32) for i in range(2)]
            for h in range(8):
                Kf, V, sta, hc = Kfs[h % 2], Vs_[h % 2], stall[h % 2], hcs[h % 2]
                S.dma('sync', Kf.ap, KTOK3[:, :, h * 256:(h + 1) * 256], writes=[Kf])
                S.dma('sync', V.ap, VTOK3[:, :, h * 512:(h + 1) * 512], writes=[V])
                head_consts(hc, h)
                kdf, cdf = hc.ap[:, 2:3], hc.ap[:, 4:5]
                S.act(Kf, Kf.ap, Kf, Kf.ap, AF.Copy, scale=kdf, rd=(hc,))
                Sm = Sms[0]
                S.ms('pool', Sm, Sm.ap, 0.0)
                for c in range(NCH):
                    Sn = Sms[(c + 1) % 2]
                    S.cp('act', sta, sta.ap[:, c], Sm, Sm.ap)
                    for kc in range(2):
                        pb_ = PS[(2 * c + kc) % 4]
                        S.mm(pb_, pb_.ap, Kf, Kf.ap[:, c, kc * 128:(kc + 1) * 128], V, V.ap[:, c, :], True, True)
                    for kc in range(2):
                        pb_ = PS[(2 * c + kc) % 4]
                        S.stt(Sn, Sn.ap[:, kc, :], Sm, Sm.ap[:, kc, :], cdf, pb_, pb_.ap, ALU.mult, ALU.add, rd=(hc,))
                    Sm = Sn
                S.dma('sync', hm(SBD[h]).rearrange("p c (a b) -> p c a b", a=2), sta.ap, reads=[sta])
                S.dma('sync', E_in[h // 4][h % 4].rearrange("p (a b) -> p a b", a=2), Sm.ap, reads=[Sm])
            S.barrier()
            exchange(E_in[0], E_out[0])
            exchange(E_in[1], E_out[1])
            A.reset()
            q = A.alloc('q', (2, NT), BF16)
            qf = A.alloc('qf', (2, NT), BF16)
            qb = A.alloc('qb', (2, NT), BF16)
            k_ = A.alloc('k', (2, NT), BF16)
            Kb = A.alloc('Kb', (NCH, 256), BF16)
            V = A.alloc('V', (NCH, 512), BF16)
            SBr = [A.alloc('SB%d' % i, (2, 512), BF16) for i in range(4)]
            oo = A.alloc('oo', (NCH, 512), F32)
            atb = A.alloc('atb', (4, NT), BF16)
            Sms = [A.alloc('Sm%d' % i, (2, 512), F32) for i in range(2)]
            Scs = [A.alloc('Sc%d' % i, (2, 512), BF16) for i in range(4)]
            e2 = A.alloc('e2', (2, 2, 512), F32)
            gb = [A.alloc('gb%d' % i, (512,), BF16) for i in range(4)]
            m1 = A.alloc('m1', (128,), F32)
            m2 = A.alloc('m2', (128,), F32)
            Mc = A.alloc('Mc', (128,), F32)
            patf = A.alloc('patf', (128,), F32)
            patb = A.alloc('patb', (128,), F32)
            hc = A.alloc('hc', (8,), F32)
            scb = [A.alloc('scb%d' % i, (128,), BF16) for i in range(3)]
            yb = [A.alloc('yb%d' % i, (512,), BF16) for i in range(4)]
            st6 = A.alloc('st6', (NCH, 6), F32)
            mv = A.alloc('mv', (NCH, 2), F32)
            rsd = A.alloc('rsd', (NCH,), F32)
            nmr = A.alloc('nmr', (NCH,), F32)
            order_b = list(range(NCH - 1, 1, -1)) + [1, 0]
            cmin = 2 if l == DEPTH - 1 else 0
            outs = [c for c in order_b if c >= cmin]

            def q4(t):
                return t.ap.rearrange("p a (c n) -> p a c n", n=128)

            def pat4(t):
                return t.ap.unsqueeze(1).unsqueeze(1).to_broadcast([128, 2, NCH, 128])
            for h in range(8):
                S.dma('sync', q.ap, QT3[:, 2 * h:2 * h + 2, :], writes=[q])
                S.dma('sync', k_.ap, KT3[:, 2 * h:2 * h + 2, :], writes=[k_])
                S.dma('sync', Kb.ap, KTOK3[:, :, h * 256:(h + 1) * 256], writes=[Kb])
                S.dma('sync', V.ap, VTOK3[:, :, h * 512:(h + 1) * 512], writes=[V])
                S.dma('sync', e2.ap, E_out[h // 4][:, h % 4].rearrange("r p (a b) -> p r a b", a=2), writes=[e2])
                lgf, lgb = head_consts(hc, h)
                S.act(m1, m1.ap, None, cst('d1'), AF.Exp, scale=lgf)
                S.tt('dve', m1, m1.ap, m1, m1.ap, None, cst('u'), ALU.mult)
                S.act(m2, m2.ap, None, cst('d2'), AF.Exp, scale=lgb)
                S.tt('dve', m2, m2.ap, m2, m2.ap, None, cst('l'), ALU.mult)
                S.tt('dve', Mc, Mc.ap, m1, m1.ap, m2, m2.ap, ALU.add)
                S.act(patf, patf.ap, None, cst('r1'), AF.Exp, scale=lgf)
                S.act(patb, patb.ap, None, cst('r2'), AF.Exp, scale=lgb)
                S.tt('dve', qf, q4(qf), q, q4(q), patf, pat4(patf), ALU.mult)
                S.tt('dve', qb, q4(qb), q, q4(q), patb, pat4(patb), ALU.mult)
                qdf, qdb, kdf, kdb, cdf, cdb = (hc.ap[:, i:i + 1] for i in range(6))
                S.act(Kb, Kb.ap, Kb, Kb.ap, AF.Copy, scale=kdb, rd=(hc,))
                Sm = Sms[0]
                mi = 0
                S.ts('dve', Sm, Sm.ap, e2, e2.ap[:, 0], msk0, None, ALU.mult)
                S.stt(Sm, Sm.ap, e2, e2.ap[:, 1], msk1, Sm, Sm.ap, ALU.mult, ALU.add)
                si = 0
                S.cp('act', Scs[0], Scs[0].ap, Sm, Sm.ap)

                def emit_pb(c, po, Sc):
                    csl = slice(c * 128, (c + 1) * 128)
                    for kc in range(2):
                        S.mm(po, po.ap, qb, qb.ap[:, kc, csl], Sc, Sc.ap[:, kc, :], False, kc == 1)
                    S.cp('act', oo, oo.ap[:, c, :], po, po.ap)

                prev = None
                for i, c in enumerate(order_b):
                    if c == 1:
                        S.ms('pool', Sm, Sm.ap, 0.0)
                        si += 1
                        S.ms('pool', Scs[si % 4], Scs[si % 4].ap, 0.0)
                    Sc = Scs[si % 4]
                    csl = slice(c * 128, (c + 1) * 128)
                    want = c >= cmin
                    upd = c not in (2, 0)
                    pss = PS[0]
                    po = PS[3 + i % 4]
                    if want:
                        sb_ = SBr[i % 4]
                        S.dma('sync', sb_.ap, SBD[h, c].rearrange("p (a b) -> p a b", a=2), writes=[sb_])
                        for kc in range(2):
                            S.mm(pss, pss.ap[:, :128], k_, k_.ap[:, kc, csl], q, q.ap[:, kc, csl], kc == 0, kc == 1)
                        sc_ = scb[i % 3]
                        S.tt('dve', sc_, sc_.ap, pss, pss.ap[:, :128], Mc, Mc.ap, ALU.mult)
                    if upd:
                        for kc in range(2):
                            S.mm(PS[1 + kc], PS[1 + kc].ap, Kb, Kb.ap[:, c, kc * 128:(kc + 1) * 128], V, V.ap[:, c, :], True, True)
                    if want:
                        for kc in range(2):
                            S.mm(po, po.ap, qf, qf.ap[:, kc, csl], sb_, sb_.ap[:, kc, :], kc == 0, False)
                        S.mm(po, po.ap, sc_, sc_.ap, V, V.ap[:, c, :], False, False)
                    if prev is not None:
                        emit_pb(*prev)
                    if upd:
                        mi += 1
                        Sn = Sms[mi % 2]
                        for kc in range(2):
                            S.stt(Sn, Sn.ap[:, kc, :], Sm, Sm.ap[:, kc, :], cdb, PS[1 + kc], PS[1 + kc].ap, ALU.mult, ALU.add, rd=(hc,))
                        Sm = Sn
                        si += 1
                        S.cp('act', Scs[si % 4], Scs[si % 4].ap, Sm, Sm.ap)
                    prev = (c, po, Sc) if want else None
                if prev is not None:
                    emit_pb(*prev)
                for c in outs:
                    S.op('dve', lambda e, a=st6.ap[:, c, :], b=oo.ap[:, c, :]: e.bn_stats(a, b), reads=[oo], writes=[st6])
                    S.op('dve', lambda e, a=mv.ap[:, c, :], b=st6.ap[:, c, :]: e.bn_aggr(a, b), reads=[st6], writes=[mv])
                S.act(rsd, rsd.ap[:, cmin:NCH], mv, mv.ap[:, cmin:NCH, 1], AF.Sqrt, bias=eps_ln, scale=1.0)
                S.recip(rsd, rsd.ap[:, cmin:NCH], rsd, rsd.ap[:, cmin:NCH])
                S.stt(nmr, nmr.ap[:, cmin:NCH], mv, mv.ap[:, cmin:NCH, 0], -1.0, rsd, rsd.ap[:, cmin:NCH], ALU.mult, ALU.mult)

                def s3(i):
                    c = outs[i]
                    g_ = gb[i % 4]
                    S.dma('sync', g_.ap, GTOK[c, :, h * 512:(h + 1) * 512], writes=[g_])
                    S.act(oo, oo.ap[:, c, :], oo, oo.ap[:, c, :], AF.Identity, bias=nmr.ap[:, c:c + 1], scale=rsd.ap[:, c:c + 1], rd=(nmr, rsd))
                    yb_ = yb[i % 4]
                    S.tt('dve', yb_, yb_.ap, oo, oo.ap[:, c, :], g_, g_.ap, ALU.mult)

                def s4(i):
                    c = outs[i]
                    yb_ = yb[i % 4]
                    pt = PS[1 + i % 2]
                    ptv = pt.ap[:, 0:256].bitcast(BF16)
                    for vc in range(4):
                        S.tr(pt, ptv[:, vc * 128:(vc + 1) * 128], yb_, yb_.ap[:, vc * 128:(vc + 1) * 128], ident_bf)
                    S.tt('dve', atb, atb.ap[:, :, c * 128:(c + 1) * 128], pt, ptv.rearrange("p (a b) -> p a b", a=4),
                         None, sv('gnw', jl * 32 + h * 4, 4).unsqueeze(2).to_broadcast([128, 4, 128]), ALU.mult)
                s3(0)
                for i in range(len(outs)):
                    if i + 1 < len(outs):
                        s3(i + 1)
                    s4(i)
                S.dma('sync', ATS3[:, 4 * h:4 * h + 4, cmin * 128:NT], atb.ap[:, :, cmin * 128:NT], reads=[atb])
            S.barrier()
            A.reset()
            at = A.alloc('at', (32, 512), BF16)
            xt = A.alloc('xt', (KC, 512), F32)
            wb = [A.alloc('wb%d' % i, (8192,), BF16) for i in range(3)]
            jobs = []
            for (t0, T, j) in mixer_tiles:
                if l == DEPTH - 1 and j == 1:
                    continue

                def pre(t0=t0, T=T):
                    S.dma('sync', at.ap[:, :, :T], ATS3[:, :, t0:t0 + T], writes=[at])
                    S.dma('sync', xt.ap[:, :, :T], XI3[:, :, t0:t0 + T], writes=[xt])
                for s in range(8):
                    def fn(buf, s=s, t0=t0, T=T, j=j):
                        wv = wview(buf, 0, 32, 256)
                        for ff in range(2):
                            f = 2 * s + ff
                            pw = PS[f % 2]
                            for k in range(32):
                                S.mm(pw, pw.ap[:, :T], buf, wv[:, k, ff * 128:(ff + 1) * 128], at, at.ap[:, k, :T], k == 0, k == 31)
                            S.stt(xt, xt.ap[:, f, :T], pw, pw.ap[:, :T], mG(l, 0, f, j), xt, xt.ap[:, f, :T], ALU.mult, ALU.add)
                            if f == KC - 1:
                                S.dma('sync', XO3[:, :, t0:t0 + T], xt.ap[:, :, :T], reads=[xt])
                    jobs.append(dict(pre=pre if s == 0 else None, loads=[(Wt['ro', l], 32, s * 256, 256, 0)], fn=fn))
            run_stream(jobs, wb)
            S.barrier()

        PHASES = {'cnv': cnv_phase, 'att': att_phase, 'ret': ret_phase}
        cur, nxt = (XA, XA3), (XB, XB3)
        for li, l in enumerate(layers):
            if do_mix and MIXER[l] in PHASES:
                PHASES[MIXER[l]](l, cur[1], nxt[1])
            else:
                copy_phase(cur[0], nxt[0])
            cur, nxt = nxt, cur
            lastl = (li == len(layers) - 1)
            if do_ffn:
                halo_exchange(cur[1])
                ffn_phase(l, cur[1], nxt[1], lastl)
                cur, nxt = nxt, cur
            elif lastl:
                S.dma('sync', outT, cur[0][:, CTX:NT], final=True)
        S.finish()
    return nc


def _weights_for(inp, layers, do_mix=True, do_ffn=True):
    m = {}
    for l in layers:
        if do_ffn:
            m["wg%d" % l] = inp['ffn_w_gate'][l]
            m["wu%d" % l] = inp['ffn_w_up'][l]
            m["wd%d" % l] = inp['ffn_w_down'][l]
        if not do_mix:
            continue
        if MIXER[l] == 'ret':
            j = l // 3
            m["rq%d" % l] = inp['ret_wq'][j]
            m["rk%d" % l] = inp['ret_wk'][j]
            m["rv%d" % l] = inp['ret_wv'][j]
            m["rg%d" % l] = inp['ret_wg'][j]
            m["ro%d" % l] = inp['ret_wo'][j]
            m["gn%d" % l] = inp['ret_gn_w'][j].reshape(1, HV)
        elif MIXER[l] == 'att':
            m["aq%d" % l] = inp['att_wq'][0]
            m["akv%d" % l] = inp['att_wkv'][0]
            m["ao%d" % l] = inp['att_wo'][0]
        else:
            m["c1_%d" % l] = inp['cnv_w1'][0]
            m["c2_%d" % l] = inp['cnv_w2'][0]
    return {k: np.ascontiguousarray(np.asarray(v, np.float32)) for k, v in m.items()}


def run(inp, nlat=4, layers=(0, 1, 2, 3), do_mix=True, do_ffn=True, nb=4):
    inp = {k: np.asarray(v) for k, v in inp.items()}
    L = nlat * 512
    ncores = 2 * nb
    prog = build(nlat, layers, do_mix, do_ffn, ncores)
    fw = (96 // ncores) * 128
    cst = make_consts()
    wts = _weights_for(inp, layers, do_mix, do_ffn)
    in_maps = []
    for b in range(nb):
        for r in range(2):
            if r == 0:
                tpos = np.arange(L)
                xc, xl = inp['ctx'][b], inp['x'][b][:L]
            else:
                tpos = np.arange(2 * L - 1, L - 1, -1)
                xc, xl = inp['ctx'][b][::-1], inp['x'][b][L:2 * L][::-1]
            rc, rs_ = make_rope(tpos)
            xT = np.concatenate([xc.T, xl.T], axis=1)
            m = {'xT': np.ascontiguousarray(xT, dtype=np.float32), 'sv': pack_sv(inp, b, r), 'cst': cst,
                 'rope_cos': rc, 'rope_sin': rs_}
            m.update(wts)
            ci = 2 * b + r
            for l in layers:
                m["mw%d" % l] = np.ascontiguousarray(inp['mod_w'][l][:, ci * fw:(ci + 1) * fw], dtype=np.float32)
            in_maps.append(m)
    res = run_bass_kernel_spmd(prog, in_maps, core_ids=list(range(2 * nb)))
    outs = []
    for b in range(nb):
        o0 = res.results[2 * b]['outT'].T
        o1 = res.results[2 * b + 1]['outT'].T[::-1]
        outs.append(np.concatenate([o0, o1], axis=0))
    return np.ascontiguousarray(np.stack(outs))


def kernel(**inputs):
    return run(inputs).astype(np.float32)
```

```python
import contextlib
import numpy as np
import concourse.bass as bass
import concourse.mybir as mybir
from concourse.bass_utils import run_bass_kernel_spmd

F32 = mybir.dt.float32
BF16 = mybir.dt.bfloat16
AF = mybir.ActivationFunctionType
ALU = mybir.AluOpType

ENGS = ('sync', 'act', 'dve', 'pool', 'pe')
BLOCK_NAME = {'sync': 'sync', 'act': 'scalar', 'dve': 'vector', 'pool': 'gpsimd', 'pe': 'tensor'}


class Tile:
    __slots__ = ('name', 'ap', 'w', 'r')

    def __init__(self, name, ap=None):
        self.name = name
        self.ap = ap
        self.w = None
        self.r = {}


class Sched:
    def __init__(self, nc, ndma=(('sync', 30), ('pool', 30), ('act', 8))):
        self.nc = nc
        self.stack = contextlib.ExitStack()
        self.q = {e: [] for e in ENGS}
        self.cnt = {e: 0 for e in ENGS}
        self.waited = {e: {} for e in ENGS}
        self.csem = {}
        self.dsems, self.dval, self.dn = {}, {}, {}
        self.ndma = ndma
        self.finals = []
        self.n_ops = 0
        self.cc_list = []

    def __enter__(self):
        self.stack.__enter__()
        nc = self.nc
        for e in ('act', 'dve', 'pool', 'pe'):
            self.csem[e] = self.stack.enter_context(nc.semaphore("c_" + e))
        for e, n in self.ndma:
            self.dsems[e] = [self.stack.enter_context(nc.semaphore("d_%s_%d" % (e, i))) for i in range(n)]
            self.dval[e] = [0] * n
            self.dn[e] = 0
        return self

    def __exit__(self, *a):
        return self.stack.__exit__(*a)

    def sbuf(self, name, shape, dtype):
        t = self.stack.enter_context(self.nc.sbuf_tensor(name, list(shape), dtype))
        return Tile(name, t[:])

    def psum(self, name, shape, dtype):
        t = self.stack.enter_context(self.nc.psum_tensor(name, list(shape), dtype))
        return Tile(name, t[:])

    def region(self, name):
        return Tile(name, None)

    def _deps(self, reads, writes):
        evs = []
        for t in reads:
            evs.append(t.w)
        for t in writes:
            evs.append(t.w)
            evs.extend(t.r.values())
        return evs

    def _need(self, eng, evs):
        w = self.waited[eng]
        best = {}
        pe_sem = self.csem['pe'].num
        for ev in evs:
            if ev is None:
                continue
            sem, val = ev
            k = sem.num
            if eng == 'pe' and k == pe_sem:
                continue
            if w.get(k, 0) >= val:
                continue
            if k not in best or best[k][1] < val:
                best[k] = (sem, val)
        out = []
        for k, (sem, val) in best.items():
            w[k] = val
            out.append((sem, val))
        return out

    def _record(self, ev, reads, writes):
        k = ev[0].num
        for t in reads:
            t.r[k] = ev
        for t in writes:
            t.w = ev
            t.r = {}

    def op(self, eng, fn, reads=(), writes=()):
        waits = self._need(eng, self._deps(reads, writes))
        self.cnt[eng] += 1
        sem = self.csem[eng]
        ev = (sem, self.cnt[eng])
        self.q[eng].append((waits, fn, sem, 1))
        self._record(ev, reads, writes)
        self.n_ops += 1
        return ev

    def dma(self, eng, out_ap, in_ap, reads=(), writes=(), final=False, **kw):
        evs = self._deps(reads, writes)
        n = len(self.dsems[eng])
        k = self.dn[eng] % n
        self.dn[eng] += 1
        sem = self.dsems[eng][k]
        prev = self.dval[eng][k]
        if prev:
            evs.append((sem, prev))
        waits = self._need(eng, evs)
        val = prev + 16
        self.dval[eng][k] = val
        ev = (sem, val)
        self.q[eng].append((waits, lambda e: e.dma_start(out=out_ap, in_=in_ap, **kw), sem, 16))
        self._record(ev, reads, writes)
        if final:
            self.finals.append(ev)
        self.n_ops += 1
        return ev


    def collective(self, groups, in_ap, out_ap, reads=(), writes=()):
        evs = self._deps(reads, writes)
        sem = self.stack.enter_context(self.nc.semaphore("cc_sem%d" % len(self.cc_list)))
        waits = self._need('pool', evs)
        ev = (sem, 1)
        self.cc_list.append(ev)
        self.q['pool'].append((waits, lambda e: e.collective_compute(
            "AllGather", ALU.bypass, replica_groups=groups, ins=[in_ap.opt()], outs=[out_ap.opt()]), sem, 1))
        self._record(ev, reads, writes)
        return ev

    def barrier(self):
        evs = []
        for e in ('act', 'dve', 'pool', 'pe'):
            if self.cnt[e]:
                evs.append((self.csem[e], self.cnt[e]))
        for e, _ in self.ndma:
            for s, v in zip(self.dsems[e], self.dval[e]):
                if v:
                    evs.append((s, v))
        evs.extend(self.cc_list)
        for e in ENGS:
            waits = self._need(e, evs)
            if waits:
                self.q[e].append((waits, None, None, 0))

    @staticmethod
    def _t(ts):
        return [t for t in ts if t is not None]

    def mm(self, pt, pap, lt, lap, rt, rap, start=True, stop=True):
        self.op('pe', lambda e: e.matmul(pap, lap, rap, start=start, stop=stop),
                reads=self._t((lt, rt)), writes=(pt,))

    def tr(self, pt, pap, it, iap, idap):
        self.op('pe', lambda e: e.transpose(pap, iap, idap), reads=self._t((it,)), writes=(pt,))

    def act(self, ot, oap, it, iap, func, bias=None, scale=None, rd=()):
        kw = {}
        if bias is not None:
            kw['bias'] = bias
        if scale is not None:
            kw['scale'] = scale
        self.op('act', lambda e: e.activation(oap, iap, func, **kw),
                reads=self._t((it,) + tuple(rd)), writes=(ot,))

    def tt(self, eng, ot, oap, at, aap, bt, bap, op):
        self.op(eng, lambda e: e.tensor_tensor(oap, aap, bap, op), reads=self._t((at, bt)), writes=(ot,))

    def ts(self, eng, ot, oap, it, iap, s1, s2, op0, op1=None, rd=()):
        if op1 is None:
            self.op(eng, lambda e: e.tensor_scalar(oap, iap, s1, s2, op0),
                    reads=self._t((it,) + tuple(rd)), writes=(ot,))
        else:
            self.op(eng, lambda e: e.tensor_scalar(oap, iap, s1, s2, op0, op1),
                    reads=self._t((it,) + tuple(rd)), writes=(ot,))

    def stt(self, ot, oap, at, aap, sc, bt, bap, op0, op1, rd=()):
        self.op('dve', lambda e: e.scalar_tensor_tensor(oap, aap, sc, bap, op0, op1),
                reads=self._t((at, bt) + tuple(rd)), writes=(ot,))

    def cp(self, eng, ot, oap, it, iap):
        if eng == 'act':
            self.op('act', lambda e: e.copy(oap, iap), reads=self._t((it,)), writes=(ot,))
        else:
            self.op(eng, lambda e: e.tensor_copy(oap, iap), reads=self._t((it,)), writes=(ot,))

    def ms(self, eng, t, ap, val):
        self.op(eng, lambda e: e.memset(ap, val), writes=(t,))

    def recip(self, ot, oap, it, iap):
        self.op('dve', lambda e: e.reciprocal(oap, iap), reads=self._t((it,)), writes=(ot,))

    def finish(self):
        evs = list(self.finals)
        for e in ('act', 'dve', 'pool', 'pe'):
            if self.cnt[e]:
                evs.append((self.csem[e], self.cnt[e]))
        for e, _ in self.ndma:
            for s, v in zip(self.dsems[e], self.dval[e]):
                if v:
                    evs.append((s, v))
        evs.extend(self.cc_list)
        waits = self._need('sync', evs)
        q = self.q
        q['sync'].append((waits, None, None, 0))
        with self.nc.Block() as block:
            for e in ENGS:
                lst = q[e]
                if not lst:
                    continue

                def body(eng, lst=lst):
                    for waits, fn, sem, inc in lst:
                        for s, v in waits:
                            eng.wait_ge(s, v)
                        if fn is not None:
                            fn(eng).then_inc(sem, inc)
                getattr(block, BLOCK_NAME[e])(body)


D = 2048
KC = 16
CTX = 256
FFN = 5632
FC = 44
HV = 4096
DEPTH = 4
RMS_EPS = 1e-6
LN_EPS = 1e-5
MIXER = {0: 'ret', 1: 'att', 2: 'cnv', 3: 'ret'}
ARENA = 48400


def _sv_layout():
    lay, n = {}, 0
    for name, w in (('call', 64), ('bm', 4), ('cc', 16), ('mod_b', 4 * 96), ('n1w', 64), ('n2w', 64),
                    ('fdw', 4 * 3 * FC), ('fdwb', 4 * FC), ('qg', 1), ('kg', 1), ('cb1', 32),
                    ('cdw', 31 * 16), ('cdwb', 16), ('clnw', 16), ('clnb', 16), ('cb2', 16), ('dec', 32), ('msk', 2), ('gnw', 64)):
        lay[name] = n
        n += w
    return lay, n


SVL, NSV = _sv_layout()
CSTL = {'ident': 0, 'ones': 128, 'perm': 256, 'd1': 384, 'd2': 512, 'u': 640, 'l': 768,
        'p1': 896, 'cmp': 897, 'c1p': 898, 'p0': 899, 'r1': 900, 'r2': 1028}
NCST = 1156


def _chunked(v):
    v = np.asarray(v, np.float32).reshape(-1, 128)
    return np.ascontiguousarray(v.T)


def pack_sv(inp, b, r=0):
    sv = np.zeros((128, NSV), np.float32)

    def put(name, arr):
        o = SVL[name]
        sv[:, o:o + arr.shape[1]] = arr
    ft = (2, 1, 0) if r else (0, 1, 2)
    ct = tuple(range(30, -1, -1)) if r else tuple(range(31))
    put('call', np.concatenate([_chunked(inp['c'][i]) for i in range(4)], axis=1))
    bm = np.zeros((128, 4), np.float32)
    bm[:, b] = 1.0
    put('bm', bm)
    put('cc', _chunked(inp['c_ctx']))
    put('mod_b', np.concatenate([_chunked(inp['mod_b'][l]) for l in range(4)], axis=1))
    put('n1w', np.concatenate([_chunked(inp['norm1_w'][l]) for l in range(4)], axis=1))
    put('n2w', np.concatenate([_chunked(inp['norm2_w'][l]) for l in range(4)], axis=1))
    put('fdw', np.concatenate([_chunked(inp['ffn_dw'][l][t]) for l in range(4) for t in ft], axis=1))
    put('fdwb', np.concatenate([_chunked(inp['ffn_dw_b'][l]) for l in range(4)], axis=1))
    put('qg', _chunked(inp['att_q_gain'][0]))
    put('kg', _chunked(inp['att_k_gain'][0]))
    put('cb1', _chunked(inp['cnv_b1'][0]))
    put('cdw', np.concatenate([_chunked(inp['cnv_dw'][0][t]) for t in ct], axis=1))
    put('cdwb', _chunked(inp['cnv_dw_b'][0]))
    put('clnw', _chunked(inp['cnv_ln_w'][0]))
    put('clnb', _chunked(inp['cnv_ln_b'][0]))
    put('cb2', _chunked(inp['cnv_b2'][0]))
    dec = np.asarray(inp['ret_decay'], np.float32)
    if r:
        dec = dec[:, ::-1, :]
    put('dec', np.broadcast_to(np.ascontiguousarray(dec).reshape(1, 32), (128, 32)))
    put('gnw', np.concatenate([_chunked(inp['ret_gn_w'][j]) for j in range(2)], axis=1))
    msk = np.zeros((128, 2), np.float32)
    msk[:, 1 - r] = 1.0
    put('msk', msk)
    return sv


def make_consts():
    cst = np.zeros((128, NCST), np.float32)
    p = np.arange(128)
    cst[:, 0:128] = np.eye(128)
    cst[:, 128:256] = 1.0
    perm = np.where((p % 64) < 32, p + 32, p - 32)
    pm = np.zeros((128, 128), np.float32)
    pm[perm, p] = 1.0
    cst[:, 256:384] = pm
    m = p[:, None].astype(np.float32)
    n = p[None, :].astype(np.float32)
    cst[:, 384:512] = np.maximum(n - m, 0)
    cst[:, 512:640] = np.maximum(m - n, 0)
    cst[:, 640:768] = (n >= m)
    cst[:, 768:896] = (m >= n)
    cst[:, 896] = p + 1
    cst[:, 897] = 128 - p
    cst[:, 898] = 127 - p
    cst[:, 899] = p
    cst[:, 900:1028] = np.broadcast_to(np.arange(1, 129, dtype=np.float32), (128, 128))
    cst[:, 1028:1156] = np.broadcast_to(128.0 - np.arange(128, dtype=np.float32), (128, 128))
    return cst


def make_rope(tpos):
    p = np.arange(128)
    t = np.asarray(tpos)
    row, col = t // 64, t % 64
    ii = p % 64
    fi = ii % 32
    inv = (10000.0 ** (-(np.arange(32, dtype=np.float32)) / 32.0)).astype(np.float32)
    pos = np.where((p < 64)[:, None], row[None, :], col[None, :]).astype(np.float32)
    ang = pos * inv[fi][:, None]
    cos = np.cos(ang).astype(np.float32)
    sin = np.sin(ang).astype(np.float32)
    sin = np.where((ii < 32)[:, None], -sin, sin).astype(np.float32)
    return np.ascontiguousarray(cos), np.ascontiguousarray(sin)


def _split(n, m):
    k = -(-n // m)
    base, rem = divmod(n, k)
    return [base + (1 if i < rem else 0) for i in range(k)]


class Arena:
    def __init__(self, tile):
        self.t = tile
        self.off = 0

    def reset(self):
        self.off = 0

    def alloc(self, name, fshape, dtype):
        n = 1
        for s in fshape:
            n *= s
        n32 = n if dtype == F32 else (n + 1) // 2
        assert self.off + n32 <= ARENA, (name, self.off, n32)
        ap = self.t.ap[:, self.off:self.off + n32]
        self.off += n32
        if dtype != F32:
            ap = ap.bitcast(dtype)[:, 0:n]
        if len(fshape) == 2:
            ap = ap.rearrange("p (a b) -> p a b", a=fshape[0])
        elif len(fshape) == 3:
            ap = ap.rearrange("p (a b c) -> p a b c", a=fshape[0], b=fshape[1])
        return Tile(name, ap)


def build(nlat=4, layers=(0, 1, 2, 3), do_mix=True, do_ffn=True, ncores=8):
    L = nlat * 512
    NT = CTX + L
    NCH = NT // 128
    nc = bass.Bass("TRN2", target_bir_lowering=False)

    def din(name, shape):
        return nc.dram_tensor(name, list(shape), F32, kind="ExternalInput").ap()

    def dscr(name, shape, dt):
        return nc.dram_tensor(name, list(shape), dt).ap()

    xT_in = din("xT", [D, NT])
    sv_d = din("sv", [128, NSV])
    cst_d = din("cst", [128, NCST])
    rcos = din("rope_cos", [128, L])
    rsin = din("rope_sin", [128, L])
    Wt = {}
    for l in layers:
        Wt['mw', l] = din("mw%d" % l, [D, (96 // ncores) * 128])
        if do_ffn:
            Wt['wg', l] = din("wg%d" % l, [D, FFN])
            Wt['wu', l] = din("wu%d" % l, [D, FFN])
            Wt['wd', l] = din("wd%d" % l, [FFN, D])
        if not do_mix:
            continue
        if MIXER[l] == 'ret':
            Wt['rq', l] = din("rq%d" % l, [D, D])
            Wt['rk', l] = din("rk%d" % l, [D, D])
            Wt['rv', l] = din("rv%d" % l, [D, HV])
            Wt['rg', l] = din("rg%d" % l, [D, HV])
            Wt['ro', l] = din("ro%d" % l, [HV, D])
            Wt['gn', l] = din("gn%d" % l, [1, HV])
        elif MIXER[l] == 'att':
            Wt['aq', l] = din("aq%d" % l, [D, D])
            Wt['akv', l] = din("akv%d" % l, [D, 1024])
            Wt['ao', l] = din("ao%d" % l, [D, D])
        else:
            Wt['c1', l] = din("c1_%d" % l, [D, 2 * D])
            Wt['c2', l] = din("c2_%d" % l, [D, D])
    outT = nc.dram_tensor("outT", [D, L], F32, kind="ExternalOutput").ap()
    XA = dscr("XA", [D, NT], F32)
    XB = dscr("XB", [D, NT], F32)
    QT = dscr("QT", [16, 128, NT], BF16)
    KT = dscr("KT", [16, 128, NT], BF16)
    KTOK = dscr("KTOK", [NCH, 128, D], BF16)
    VTOK = dscr("VTOK", [NCH, 128, HV], BF16)
    GTOK = dscr("GTOK", [NCH, 128, HV], BF16)
    ATS = dscr("ATS", [32, 128, NT], BF16)
    AFF = dscr("AFF", [FC, 128, NT], BF16)
    GT = dscr("GT", [16, 128, NT], F32)
    CT = dscr("CT", [16, 128, NT], F32)
    SBD = dscr("SBD", [8, NCH, 128, 1024], BF16)
    NLC = L // 128
    GROUPS = [[2 * i, 2 * i + 1] for i in range(ncores // 2)]
    NB = ncores // 2
    FPC = 96 // ncores
    NCOL = NB + 1
    nL = len(layers)
    MD_in = dscr("MD_in", [128, nL * FPC * NCOL], F32)
    MD_out = dscr("MD_out", [ncores, 128, nL * FPC * NCOL], F32)
    XH_in = dscr("XH_in", [D], F32)
    XH_out = dscr("XH_out", [2, D], F32)
    CH_in = dscr("CH_in", [16, 128, 15], F32)
    CH_out = dscr("CH_out", [2, 16, 128, 15], F32)
    KX_in = dscr("KX_in", [4, 128, L], BF16)
    KX_out = dscr("KX_out", [2, 4, 128, L], BF16)
    VX_in = dscr("VX_in", [NLC, 128, 512], BF16)
    VX_out = dscr("VX_out", [2, NLC, 128, 512], BF16)
    E_in = [dscr("E_in%d" % i, [4, 128, 1024], F32) for i in range(2)]
    E_out = [dscr("E_out%d" % i, [2, 4, 128, 1024], F32) for i in range(2)]

    def fm(ap2d):
        return ap2d.rearrange("(c p) n -> p c n", p=128)

    def hm(ap3d):
        return ap3d.rearrange("r p n -> p r n")

    XA3, XB3, OUT3 = fm(XA), fm(XB), fm(outT)

    mixer_tiles = [(0, 256, 1)] + [(CTX + 512 * i, 512, 0) for i in range(nlat)]
    ffn_tiles = [(0, 256, True, True, 1)]
    t0 = CTX
    sizes = _split(L, 510)
    for i, T in enumerate(sizes):
        ffn_tiles.append((t0, T, i == 0, 'partner' if i == len(sizes) - 1 else False, 0))
        t0 += T

    S = Sched(nc)
    with S:
        svt = S.sbuf("svt", [128, NSV], F32)
        cstt = S.sbuf("cstt", [128, NCST], F32)
        cb = S.sbuf("cb", [128, 384], BF16)
        modv = S.sbuf("modv", [128, 4, 96, 2], F32)
        AB = S.sbuf("AB", [128, 4, 2, 16, 2], F32)
        b2g = S.sbuf("b2g", [128, 16, 2], F32)
        sbf = S.sbuf("sbf", [128, 16, NCOL], BF16)
        epsT = S.sbuf("epsT", [128, 2], F32)
        lgT = S.sbuf("lgT", [128, 32], F32)
        hx = S.sbuf("hx", [128, 16], F32)
        hx2 = S.sbuf("hx2", [128, 2, 16], F32)
        arena_t = S.sbuf("arena", [128, ARENA], F32)
        A = Arena(arena_t)
        PS = [S.psum("ps%d" % i, [128, 512], F32) for i in range(8)]

        def sv(name, idx=0, n=1):
            o = SVL[name] + idx
            return svt.ap[:, o:o + n]

        def cst(name, n=128):
            o = CSTL[name]
            return cstt.ap[:, o:o + n]

        msk0, msk1 = sv('msk', 0), sv('msk', 1)

        def exchange(in_t, out_t):
            r_in, r_out = S.region('xin'), S.region('xout')
            S.collective(GROUPS, in_t, out_t, reads=[r_in], writes=[r_out])
            S.barrier()

        def halo_exchange(X3):
            S.dma('sync', XH_in.rearrange("(c p) -> p c", p=128), X3[:, :, NT - 1], allow_slow_non_contiguous=True)
            S.barrier()
            exchange(XH_in, XH_out)
            S.dma('sync', hx2.ap, XH_out.rearrange("r (c p) -> p r c", p=128), writes=[hx2], allow_slow_non_contiguous=True)
            S.ts('dve', hx, hx.ap, hx2, hx2.ap[:, 0, :], msk0, None, ALU.mult)
            S.stt(hx, hx.ap, hx2, hx2.ap[:, 1, :], msk1, hx, hx.ap, ALU.mult, ALU.add)
            S.barrier()

        ident_bf, ones_bf, perm_bf = cb.ap[:, 0:128], cb.ap[:, 128:256], cb.ap[:, 256:384]
        eps_rms, eps_ln = epsT.ap[:, 0:1], epsT.ap[:, 1:2]

        def mA(l, w, k, j):
            return AB.ap[:, l, w, k, j:j + 1]

        def mB(l, w, k, j):
            return modv.ap[:, l, (0 if w == 0 else 48) + k, j:j + 1]

        def mG(l, w, k, j):
            return modv.ap[:, l, (32 if w == 0 else 80) + k, j:j + 1]

        def run_stream(jobs, bufs):
            nb = len(bufs)
            n = len(jobs)

            def load(s):
                buf = bufs[s % nb]
                for (W2, kc, c0, ncols, off) in jobs[s]['loads']:
                    dst = buf.ap[:, off:off + kc * ncols].rearrange("p (c n) -> p c n", c=kc)
                    S.dma('pool', dst, W2[:, c0:c0 + ncols].rearrange("(c p) n -> p c n", p=128), writes=[buf])
            for s in range(min(nb - 1, n)):
                load(s)
            for s in range(n):
                if s + nb - 1 < n:
                    load(s + nb - 1)
                if jobs[s].get('pre') is not None:
                    jobs[s]['pre']()
                jobs[s]['fn'](bufs[s % nb])

        def wview(buf, off, kc, ncols):
            return buf.ap[:, off:off + kc * ncols].rearrange("p (c n) -> p c n", c=kc)

        def norm_mod(xt, W, l, w, j, hT, sq, sd, rs, tmp, hoff=0, bank=6):
            pst = PS[bank]
            for k in range(KC):
                q = sq[k % 2]
                S.act(q, q.ap[:, :W], xt, xt.ap[:, k, :W], AF.Square)
                S.mm(pst, pst.ap[:, :W], None, ones_bf, q, q.ap[:, :W], k == 0, k == KC - 1)
            S.act(sd, sd.ap[:, :W], pst, pst.ap[:, :W], AF.Sqrt, bias=eps_rms, scale=1.0 / D)
            S.recip(rs, rs.ap[:, :W], sd, sd.ap[:, :W])
            for k in range(KC):
                t = tmp[k % 2]
                S.tt('dve', t, t.ap[:, :W], xt, xt.ap[:, k, :W], rs, rs.ap[:, :W], ALU.mult)
                S.act(hT, hT.ap[:, k, hoff:hoff + W], t, t.ap[:, :W], AF.Identity, bias=mB(l, w, k, j), scale=mA(l, w, k, j))

        S.dma('sync', svt.ap, sv_d, writes=[svt])
        S.dma('sync', cstt.ap, cst_d, writes=[cstt])
        S.dma('sync', XA, xT_in)
        S.cp('dve', cb, cb.ap, cstt, cstt.ap[:, 0:384])
        S.ms('pool', epsT, epsT.ap[:, 0:1], RMS_EPS)
        S.ms('pool', epsT, epsT.ap[:, 1:2], LN_EPS)
        for b_ in range(NB):
            S.act(sbf, sbf.ap[:, :, b_], svt, sv('call', 16 * b_, 16), AF.Silu)
        S.act(sbf, sbf.ap[:, :, NB], svt, sv('cc', 0, 16), AF.Silu)
        S.act(lgT, lgT.ap, svt, sv('dec', 0, 32), AF.Exp)
        S.ts('dve', lgT, lgT.ap, lgT, lgT.ap, -1.0, None, ALU.mult)
        A.reset()
        wb = [A.alloc('wb%d' % i, (8192,), BF16) for i in range(3)]
        mdp = A.alloc('mdp', (nL, FPC, NCOL), F32)
        modall = A.alloc('modall', (nL, 96, NCOL), F32)
        nsl = (FPC * 128) // 512
        for li, l in enumerate(layers):
            pm = PS[li % 2]
            jobs = []
            for s_ in range(nsl):
                def fn(buf, s_=s_, pm=pm):
                    wv = wview(buf, 0, KC, 512)
                    for f4 in range(4):
                        f = s_ * 4 + f4
                        for k in range(KC):
                            S.mm(pm, pm.ap[:, NCOL * f:NCOL * (f + 1)], buf, wv[:, k, f4 * 128:(f4 + 1) * 128],
                                 sbf, sbf.ap[:, k, :], k == 0, k == KC - 1)
                jobs.append(dict(loads=[(Wt['mw', l], KC, s_ * 512, 512, 0)], fn=fn))
            run_stream(jobs, wb)
            S.cp('act', mdp, mdp.ap[:, li], pm, pm.ap[:, 0:FPC * NCOL].rearrange("p (f c) -> p f c", c=NCOL))
        S.dma('sync', MD_in, mdp.ap.rearrange("p l f c -> p (l f c)"), reads=[mdp])
        S.barrier()
        S.collective([list(range(ncores))], MD_in, MD_out)
        S.barrier()
        for li in range(nL):
            S.dma('sync', modall.ap[:, li].rearrange("p (r f) c -> p r f c", r=ncores),
                  MD_out.rearrange("r p (l f c) -> l p r f c", l=nL, f=FPC)[li], writes=[modall])
        for li, l in enumerate(layers):
            mb = sv('mod_b', l * 96, 96)
            S.tt('dve', modv, modv.ap[:, l, :, 1], modall, modall.ap[:, li, :, NB], svt, mb, ALU.add)
            S.stt(modv, modv.ap[:, l, :, 0], modall, modall.ap[:, li, :, 0], sv('bm', 0), svt, mb, ALU.mult, ALU.add)
            for b_ in range(1, NB):
                S.stt(modv, modv.ap[:, l, :, 0], modall, modall.ap[:, li, :, b_], sv('bm', b_),
                      modv, modv.ap[:, l, :, 0], ALU.mult, ALU.add)
        for l in layers:
            for w in range(2):
                base = 16 if w == 0 else 64
                nw = sv('n1w' if w == 0 else 'n2w', l * 16, 16)
                for j in range(2):
                    S.ts('dve', AB, AB.ap[:, l, w, :, j], modv, modv.ap[:, l, base:base + 16, j], 1.0, None, ALU.add)
                    S.tt('dve', AB, AB.ap[:, l, w, :, j], AB, AB.ap[:, l, w, :, j], svt, nw, ALU.mult)
            if MIXER[l] == 'cnv':
                for j in range(2):
                    S.tt('dve', b2g, b2g.ap[:, :, j], svt, sv('cb2', 0, 16), modv, modv.ap[:, l, 32:48, j], ALU.mult)
        S.barrier()

        LB = CTX + 2
        NE = LB + L + 2

        def ffn_phase(l, XI3, XO3, final):
            skip_ctx = (l == DEPTH - 1)
            wins = ([] if skip_ctx else [(0, CTX, 1, 1)]) + [(CTX + 512 * i, 512, 0, LB + 1 + 512 * i) for i in range(nlat)]
            A.reset()
            hT = A.alloc('hT', (KC, NE), BF16)
            xt = A.alloc('xt', (KC, 256), F32)
            wb = [A.alloc('wb%d' % i, (8192,), BF16) for i in range(2)]
            gseg = [A.alloc('gseg%d' % i, (NE,), F32) for i in range(2)]
            acc = A.alloc('acc', (NT,), F32)
            ast = [A.alloc('ast%d' % i, (NT,), BF16) for i in range(2)]
            sq = [A.alloc('sq%d' % i, (256,), BF16) for i in range(2)]
            tmp = [A.alloc('tmp%d' % i, (256,), F32) for i in range(2)]
            sd = A.alloc('sd', (256,), F32)
            rs = A.alloc('rs', (256,), F32)
            for zc in (0, CTX + 1, LB):
                S.ms('pool', hT, hT.ap[:, :, zc:zc + 1], 0.0)
                for g_ in gseg:
                    S.ms('pool', g_, g_.ap[:, zc:zc + 1], 0.0)
            pieces = ([] if skip_ctx else [(0, 256, 1, 1)]) + [(CTX + 256 * i, 256, 0, LB + 1 + 256 * i) for i in range(L // 256)]
            pieces.append((None, 1, 0, LB + L + 1))
            xts = [xt, A.alloc('xtb', (KC, 256), F32)]
            sds = [sd, A.alloc('sdb', (256,), F32)]
            rss = [rs, A.alloc('rsb', (256,), F32)]

            def stA(p):
                (c0, Wp, j, e0) = pieces[p]
                x_, d_, r_ = xts[p % 2], sds[p % 2], rss[p % 2]
                pst = PS[7] if p % 2 == 0 else PS[3]
                if c0 is None:
                    S.cp('pool', x_, x_.ap[:, :, 0], hx, hx.ap)
                else:
                    S.dma('sync', x_.ap[:, :, :Wp], XI3[:, :, c0:c0 + Wp], writes=[x_])
                for k in range(KC):
                    q_ = sq[k % 2]
                    S.act(q_, q_.ap[:, :Wp], x_, x_.ap[:, k, :Wp], AF.Square)
                    S.mm(pst, pst.ap[:, :Wp], None, ones_bf, q_, q_.ap[:, :Wp], k == 0, k == KC - 1)
                S.act(d_, d_.ap[:, :Wp], pst, pst.ap[:, :Wp], AF.Sqrt, bias=eps_rms, scale=1.0 / D)
                S.recip(r_, r_.ap[:, :Wp], d_, d_.ap[:, :Wp])

            def stB(p):
                (c0, Wp, j, e0) = pieces[p]
                x_, r_ = xts[p % 2], rss[p % 2]
                for k in range(KC):
                    t = tmp[k % 2]
                    S.tt('dve', t, t.ap[:, :Wp], x_, x_.ap[:, k, :Wp], r_, r_.ap[:, :Wp], ALU.mult)
                    S.act(hT, hT.ap[:, k, e0:e0 + Wp], t, t.ap[:, :Wp], AF.Identity, bias=mB(l, 1, k, j), scale=mA(l, 1, k, j))
            stA(0)
            for p in range(len(pieces)):
                if p + 1 < len(pieces):
                    stA(p + 1)
                stB(p)
            jobs = []
            cnt = [0, 0]
            for j2 in range(FC // 2):
                def fn(buf, j2=j2):
                    wgv = wview(buf, 0, KC, 256)
                    wuv = wview(buf, 4096, KC, 256)
                    for jj in range(2):
                        jh = 2 * j2 + jj
                        gs = gseg[jh % 2]
                        pus = []
                        for (c0, T, j, e0) in wins:
                            pg = PS[cnt[0] % 3]
                            cnt[0] += 1
                            for k in range(KC):
                                S.mm(pg, pg.ap[:, :T], buf, wgv[:, k, jj * 128:(jj + 1) * 128],
                                     hT, hT.ap[:, k, e0:e0 + T], k == 0, k == KC - 1)
                            S.cp('act', gs, gs.ap[:, e0:e0 + T], pg, pg.ap[:, :T])
                        pg = PS[3]
                        eh = LB + L + 1
                        for k in range(KC):
                            S.mm(pg, pg.ap[:, 0:1], buf, wgv[:, k, jj * 128:(jj + 1) * 128], hT, hT.ap[:, k, eh:eh + 1], k == 0, k == KC - 1)
                        S.cp('act', gs, gs.ap[:, eh:eh + 1], pg, pg.ap[:, 0:1])
                        segs = ([] if skip_ctx else [(0, CTX, 0)]) + [(CTX, L, LB)]
                        for (a0, n, e) in segs:
                            S.act(acc, acc.ap[:, a0:a0 + n], gs, gs.ap[:, e + 1:e + 1 + n], AF.Identity,
                                  bias=sv('fdwb', l * FC + jh), scale=sv('fdw', (l * 3 + 1) * FC + jh))
                            S.stt(acc, acc.ap[:, a0:a0 + n], gs, gs.ap[:, e:e + n], sv('fdw', (l * 3 + 0) * FC + jh),
                                  acc, acc.ap[:, a0:a0 + n], ALU.mult, ALU.add)
                            S.stt(acc, acc.ap[:, a0:a0 + n], gs, gs.ap[:, e + 2:e + 2 + n], sv('fdw', (l * 3 + 2) * FC + jh),
                                  acc, acc.ap[:, a0:a0 + n], ALU.mult, ALU.add)
                        lo = CTX if skip_ctx else 0
                        S.act(acc, acc.ap[:, lo:NT], acc, acc.ap[:, lo:NT], AF.Silu)
                        a_ = ast[jh % 2]
                        for (c0, T, j, e0) in wins:
                            pu = PS[4 + cnt[1] % 3]
                            cnt[1] += 1
                            for k in range(KC):
                                S.mm(pu, pu.ap[:, :T], buf, wuv[:, k, jj * 128:(jj + 1) * 128],
                                     hT, hT.ap[:, k, e0:e0 + T], k == 0, k == KC - 1)
                            S.tt('dve', a_, a_.ap[:, c0:c0 + T], acc, acc.ap[:, c0:c0 + T], pu, pu.ap[:, :T], ALU.mult)
                        S.dma('sync', AFF[jh, :, lo:NT], a_.ap[:, lo:NT], reads=[a_])
                jobs.append(dict(loads=[(Wt['wg', l], KC, j2 * 256, 256, 0), (Wt['wu', l], KC, j2 * 256, 256, 4096)], fn=fn))
            run_stream(jobs, wb)
            S.barrier()
            A.reset()
            wds = [A.alloc('wd%d' % i, (FC, 512), BF16) for i in range(2)]
            at = [A.alloc('at%d' % i, (FC, 512), BF16) for i in range(2)]
            xo = [A.alloc('xo%d' % i, (4, 512), F32) for i in range(1)]
            AFF3 = hm(AFF)
            it = 0
            seq = [(fs, w) for fs in range(4) for w in wins]

            def load_at(i):
                (c0, T, j, e0) = seq[i][1]
                a_ = at[i % 2]
                S.dma('sync', a_.ap[:, :, :T], AFF3[:, :, c0:c0 + T], writes=[a_])
            def load_wd(fs):
                S.dma('pool', wds[fs % 2].ap, Wt['wd', l][:, fs * 512:(fs + 1) * 512].rearrange("(c p) n -> p c n", p=128), writes=[wds[fs % 2]])
            load_at(0)
            load_wd(0)
            for i, (fs, (c0, T, j, e0)) in enumerate(seq):
                wd = wds[fs % 2]
                if i % len(wins) == 0 and fs + 1 < 4:
                    load_wd(fs + 1)
                if i + 1 < len(seq):
                    load_at(i + 1)
                a_, x_ = at[i % 2], xo[0]
                S.dma('sync', x_.ap[:, :, :T], XI3[:, 4 * fs:4 * fs + 4, c0:c0 + T], writes=[x_])
                for ff in range(4):
                    f = 4 * fs + ff
                    po = PS[(4 * i + ff) % 4]
                    for jh in range(FC):
                        S.mm(po, po.ap[:, :T], wd, wd.ap[:, jh, ff * 128:(ff + 1) * 128], a_, a_.ap[:, jh, :T], jh == 0, jh == FC - 1)
                    S.stt(x_, x_.ap[:, ff, :T], po, po.ap[:, :T], mG(l, 1, f, j), x_, x_.ap[:, ff, :T], ALU.mult, ALU.add)
                if final and j == 1:
                    pass
                elif final:
                    S.dma('sync', OUT3[:, 4 * fs:4 * fs + 4, c0 - CTX:c0 - CTX + T], x_.ap[:, :, :T], reads=[x_], final=True)
                else:
                    S.dma('sync', XO3[:, 4 * fs:4 * fs + 4, c0:c0 + T], x_.ap[:, :, :T], reads=[x_])
            S.barrier()

        def copy_phase(XI, XO):
            S.dma('sync', XO, XI)
            S.barrier()

        def cnv_phase(l, XI3, XO3):
            A.reset()
            xts = [A.alloc('xt%d' % i, (KC, 512), F32) for i in range(2)]
            hTs = [A.alloc('hT%d' % i, (KC, 512), BF16) for i in range(2)]
            gl = A.alloc('gl', (KC, 512), F32)
            wb = [A.alloc('wb%d' % i, (8192,), BF16) for i in range(2)]
            sgm = [A.alloc('sgm%d' % i, (512,), F32) for i in range(2)]
            sq = [A.alloc('sq%d' % i, (512,), BF16) for i in range(2)]
            tmp = [A.alloc('tmp%d' % i, (512,), F32) for i in range(2)]
            sd = A.alloc('sd', (512,), F32)
            rs = A.alloc('rs', (512,), F32)
            GT3 = hm(GT)

            def pl(ti):
                (t0, T, j) = mixer_tiles[ti]
                S.dma('sync', xts[ti % 2].ap[:, :, :T], XI3[:, :, t0:t0 + T], writes=[xts[ti % 2]])

            def pn(ti):
                (t0, T, j) = mixer_tiles[ti]
                norm_mod(xts[ti % 2], T, l, 0, j, hTs[ti % 2], sq, sd, rs, tmp)
            pl(0)
            pn(0)
            jobs = []
            for ti, (t0, T, j) in enumerate(mixer_tiles):
                hT = hTs[ti % 2]
                tj = []
                for c in range(KC):
                    def fn(buf, c=c, t0=t0, T=T, hT=hT):
                        wa = wview(buf, 0, KC, 128)
                        wg_ = wview(buf, 2048, KC, 128)
                        pa, pb = PS[c % 2], PS[2 + c % 2]
                        for k in range(KC):
                            S.mm(pa, pa.ap[:, :T], buf, wa[:, k, :], hT, hT.ap[:, k, :T], k == 0, k == KC - 1)
                        for k in range(KC):
                            S.mm(pb, pb.ap[:, :T], buf, wg_[:, k, :], hT, hT.ap[:, k, :T], k == 0, k == KC - 1)
                        g_ = sgm[c % 2]
                        S.act(g_, g_.ap[:, :T], pb, pb.ap[:, :T], AF.Sigmoid, bias=sv('cb1', 16 + c))
                        S.stt(gl, gl.ap[:, c, :T], pa, pa.ap[:, :T], sv('cb1', c), g_, g_.ap[:, :T], ALU.add, ALU.mult)
                        if c == KC - 1:
                            S.dma('sync', GT3[:, :, t0:t0 + T], gl.ap[:, :, :T], reads=[gl])
                    tj.append(dict(loads=[(Wt['c1', l], KC, c * 128, 128, 0), (Wt['c1', l], KC, D + c * 128, 128, 2048)], fn=fn))
                if ti + 1 < len(mixer_tiles):
                    tj[0]['pre'] = (lambda ti=ti: pl(ti + 1))
                    tj[len(tj) // 2]['pre'] = (lambda ti=ti: pn(ti + 1))
                jobs.extend(tj)
            run_stream(jobs, wb)
            S.barrier()
            S.dma('sync', CH_in.rearrange("c p d -> p c d"), GT3[:, :, NT - 15:NT])
            S.barrier()
            exchange(CH_in, CH_out)
            A.reset()
            LB2 = CTX + 30
            NE2 = LB2 + L + 30
            ch2 = A.alloc('ch2', (2, KC, 15), F32)
            hal = A.alloc('hal', (KC, 15), F32)
            gxr = [A.alloc('gxr%d' % i, (NE2,), F32) for i in range(2)]
            gxb = [A.alloc('gxb%d' % i, (NE2,), BF16) for i in range(2)]
            dgs = [A.alloc('dg%d' % i, (31, 128), BF16) for i in range(2)]
            cts = [A.alloc('ct%d' % i, (NT,), F32) for i in range(2)]
            S.dma('sync', ch2.ap, CH_out.rearrange("r c p d -> p r c d"), writes=[ch2])
            for d in range(15):
                S.ts('dve', hal, hal.ap[:, :, d], ch2, ch2.ap[:, 0, :, 14 - d], msk0, None, ALU.mult)
                S.stt(hal, hal.ap[:, :, d], ch2, ch2.ap[:, 1, :, 14 - d], msk1, hal, hal.ap[:, :, d], ALU.mult, ALU.add)
            for g_ in gxr:
                for z0 in (0, 15 + CTX, LB2):
                    S.ms('pool', g_, g_.ap[:, z0:z0 + 15], 0.0)
            wi = 0
            for k in range(KC):
                gr, gbf, dgk, ctk = gxr[k % 2], gxb[k % 2], dgs[k % 2], cts[k % 2]
                S.dma('sync', gr.ap[:, 15:15 + CTX], GT[k, :, 0:CTX], writes=[gr])
                S.dma('sync', gr.ap[:, LB2 + 15:LB2 + 15 + L], GT[k, :, CTX:NT], writes=[gr])
                S.cp('dve', gr, gr.ap[:, LB2 + 15 + L:NE2], hal, hal.ap[:, k, :])
                S.cp('act', gbf, gbf.ap, gr, gr.ap)
                for tp in range(31):
                    S.ts('dve', dgk, dgk.ap[:, tp, :], None, ident_bf, sv('cdw', tp * 16 + k), None, ALU.mult)
                for (c0, T, e0) in [(0, CTX, 0)] + [(CTX + 512 * i, 512, LB2 + 512 * i) for i in range(nlat)]:
                    ps = PS[wi % 4]
                    wi += 1
                    for tp in range(31):
                        S.mm(ps, ps.ap[:, :T], dgk, dgk.ap[:, tp, :], gbf, gbf.ap[:, e0 + tp:e0 + tp + T], tp == 0, tp == 30)
                    S.act(ctk, ctk.ap[:, c0:c0 + T], ps, ps.ap[:, :T], AF.Identity, bias=sv('cdwb', k))
                S.dma('sync', CT[k], ctk.ap, reads=[ctk])
            S.barrier()
            A.reset()
            CT3 = hm(CT)
            acs2 = [A.alloc('ac%d' % i, (KC, 512), F32) for i in range(2)]
            a2s = [A.alloc('a2%d' % i, (KC, 512), BF16) for i in range(2)]
            xts = [A.alloc('xt%d' % i, (KC, 512), F32) for i in range(2)]
            wb = [A.alloc('wb%d' % i, (2048,), BF16) for i in range(3)]
            xb = [A.alloc('xb%d' % i, (512,), BF16) for i in range(2)]
            sq = [A.alloc('sq%d' % i, (512,), BF16) for i in range(2)]
            tmp = [A.alloc('tmp%d' % i, (512,), F32) for i in range(2)]
            mean = A.alloc('mean', (512,), F32)
            msq = A.alloc('msq', (512,), F32)
            var = A.alloc('var', (512,), F32)
            rs = A.alloc('rs', (512,), F32)
            tl = [t for t in mixer_tiles if not (l == DEPTH - 1 and t[2] == 1)]

            def pl(ti):
                (t0, T, j) = tl[ti]
                S.dma('sync', acs2[ti % 2].ap[:, :, :T], CT3[:, :, t0:t0 + T], writes=[acs2[ti % 2]])
                S.dma('sync', xts[ti % 2].ap[:, :, :T], XI3[:, :, t0:t0 + T], writes=[xts[ti % 2]])

            def pn(ti):
                (t0, T, j) = tl[ti]
                ac, a2 = acs2[ti % 2], a2s[ti % 2]
                pm_, pq_ = PS[4], PS[5]
                for k in range(KC):
                    b_ = xb[k % 2]
                    q_ = sq[k % 2]
                    S.cp('act', b_, b_.ap[:, :T], ac, ac.ap[:, k, :T])
                    S.act(q_, q_.ap[:, :T], ac, ac.ap[:, k, :T], AF.Square)
                    S.mm(pm_, pm_.ap[:, :T], None, ones_bf, b_, b_.ap[:, :T], k == 0, k == KC - 1)
                    S.mm(pq_, pq_.ap[:, :T], None, ones_bf, q_, q_.ap[:, :T], k == 0, k == KC - 1)
                S.act(mean, mean.ap[:, :T], pm_, pm_.ap[:, :T], AF.Copy, scale=1.0 / D)
                S.tt('dve', msq, msq.ap[:, :T], mean, mean.ap[:, :T], mean, mean.ap[:, :T], ALU.mult)
                S.stt(var, var.ap[:, :T], pq_, pq_.ap[:, :T], 1.0 / D, msq, msq.ap[:, :T], ALU.mult, ALU.subtract)
                S.act(var, var.ap[:, :T], var, var.ap[:, :T], AF.Sqrt, bias=eps_ln, scale=1.0)
                S.recip(rs, rs.ap[:, :T], var, var.ap[:, :T])
                for k in range(KC):
                    t_ = tmp[k % 2]
                    S.tt('dve', t_, t_.ap[:, :T], ac, ac.ap[:, k, :T], mean, mean.ap[:, :T], ALU.subtract)
                    S.tt('dve', t_, t_.ap[:, :T], t_, t_.ap[:, :T], rs, rs.ap[:, :T], ALU.mult)
                    S.act(a2, a2.ap[:, k, :T], t_, t_.ap[:, :T], AF.Silu, bias=sv('clnb', k), scale=sv('clnw', k))
            pl(0)
            pn(0)
            jobs = []
            for ti, (t0, T, j) in enumerate(tl):
                a2, xt = a2s[ti % 2], xts[ti % 2]
                tj = []
                for f in range(KC):
                    def fn(buf, f=f, t0=t0, T=T, j=j, a2=a2, xt=xt):
                        wv = wview(buf, 0, KC, 128)
                        pw = PS[f % 2]
                        for k in range(KC):
                            S.mm(pw, pw.ap[:, :T], buf, wv[:, k, :], a2, a2.ap[:, k, :T], k == 0, k == KC - 1)
                        S.stt(xt, xt.ap[:, f, :T], pw, pw.ap[:, :T], mG(l, 0, f, j), xt, xt.ap[:, f, :T], ALU.mult, ALU.add)
                        S.ts('dve', xt, xt.ap[:, f, :T], xt, xt.ap[:, f, :T], b2g.ap[:, f, j:j + 1], None, ALU.add)
                        if f == KC - 1:
                            S.dma('sync', XO3[:, :, t0:t0 + T], xt.ap[:, :, :T], reads=[xt])
                    tj.append(dict(loads=[(Wt['c2', l], KC, f * 128, 128, 0)], fn=fn))
                if ti + 1 < len(tl):
                    tj[0]['pre'] = (lambda ti=ti: pl(ti + 1))
                    tj[len(tj) // 2]['pre'] = (lambda ti=ti: pn(ti + 1))
                jobs.extend(tj)
            run_stream(jobs, wb)
            S.barrier()


        def att_phase(l, XI3, XO3):
            A.reset()
            xt = A.alloc('xt', (KC, 512), F32)
            hT = A.alloc('hT', (KC, 512), BF16)
            wb = [A.alloc('wb%d' % i, (8192,), BF16) for i in range(3)]
            sq = [A.alloc('sq%d' % i, (512,), BF16) for i in range(2)]
            tmp = [A.alloc('tmp%d' % i, (512,), F32) for i in range(2)]
            sd = A.alloc('sd', (512,), F32)
            rs = A.alloc('rs', (512,), F32)
            cs = A.alloc('cs', (512,), F32)
            sn = A.alloc('sn', (512,), F32)
            qn = [A.alloc('qn%d' % i, (512,), BF16) for i in range(2)]
            qr = [A.alloc('qr%d' % i, (512,), BF16) for i in range(2)]
            hsd = [A.alloc('hsd%d' % i, (512,), F32) for i in range(2)]
            hrs = [A.alloc('hrs%d' % i, (512,), F32) for i in range(2)]
            t1 = [A.alloc('t1%d' % i, (512,), F32) for i in range(2)]
            t2 = [A.alloc('t2%d' % i, (512,), F32) for i in range(2)]
            vt = [A.alloc('vt%d' % i, (512,), BF16) for i in range(2)]
            cnt = [0]
            jobs = []
            for (t0, T, j) in mixer_tiles:
                def pre(t0=t0, T=T, j=j):
                    S.dma('sync', xt.ap[:, :, :T], XI3[:, :, t0:t0 + T], writes=[xt])
                    if j == 0:
                        S.dma('sync', cs.ap[:, :T], rcos[:, t0 - CTX:t0 - CTX + T], writes=[cs])
                        S.dma('sync', sn.ap[:, :T], rsin[:, t0 - CTX:t0 - CTX + T], writes=[sn])
                    norm_mod(xt, T, l, 0, j, hT, sq, sd, rs, tmp)

                def head(buf, wv, hh, gain, dst, t0, T, j):
                    i = cnt[0]
                    cnt[0] += 1
                    pq, pss, pr = PS[i % 2], PS[2 + i % 2], PS[4 + i % 2]
                    for k in range(KC):
                        S.mm(pq, pq.ap[:, :T], buf, wv[:, k, hh * 128:(hh + 1) * 128], hT, hT.ap[:, k, :T], k == 0, k == KC - 1)
                    q_ = sq[i % 2]
                    S.act(q_, q_.ap[:, :T], pq, pq.ap[:, :T], AF.Square)
                    S.mm(pss, pss.ap[:, :T], None, ones_bf, q_, q_.ap[:, :T], True, True)
                    d_, r_ = hsd[i % 2], hrs[i % 2]
                    S.act(d_, d_.ap[:, :T], pss, pss.ap[:, :T], AF.Sqrt, bias=eps_rms, scale=1.0 / 128)
                    S.recip(r_, r_.ap[:, :T], d_, d_.ap[:, :T])
                    n_ = qn[i % 2]
                    S.stt(n_, n_.ap[:, :T], pq, pq.ap[:, :T], gain, r_, r_.ap[:, :T], ALU.mult, ALU.mult)
                    if j == 0:
                        S.mm(pr, pr.ap[:, :T], None, perm_bf, n_, n_.ap[:, :T], True, True)
                        a_, b_, o_ = t1[i % 2], t2[i % 2], qr[i % 2]
                        S.tt('dve', a_, a_.ap[:, :T], n_, n_.ap[:, :T], cs, cs.ap[:, :T], ALU.mult)
                        S.tt('dve', b_, b_.ap[:, :T], pr, pr.ap[:, :T], sn, sn.ap[:, :T], ALU.mult)
                        S.tt('dve', o_, o_.ap[:, :T], a_, a_.ap[:, :T], b_, b_.ap[:, :T], ALU.add)
                        S.dma('sync', dst[:, t0:t0 + T], o_.ap[:, :T], reads=[o_])
                    else:
                        S.dma('sync', dst[:, t0:t0 + T], n_.ap[:, :T], reads=[n_])

                for s in range(8):
                    def fn(buf, s=s, t0=t0, T=T, j=j):
                        wv = wview(buf, 0, KC, 256)
                        for hh in range(2):
                            head(buf, wv, hh, sv('qg'), QT[2 * s + hh], t0, T, j)
                    jobs.append(dict(pre=pre if s == 0 else None, loads=[(Wt['aq', l], KC, s * 256, 256, 0)], fn=fn))

                def fnk(buf, t0=t0, T=T, j=j):
                    wv = wview(buf, 0, KC, 512)
                    for hh in range(4):
                        head(buf, wv, hh, sv('kg'), KT[hh], t0, T, j)
                jobs.append(dict(loads=[(Wt['akv', l], KC, 0, 512, 0)], fn=fnk))

                def fnv(buf, t0=t0, T=T):
                    wv = wview(buf, 0, KC, 512)
                    for c in range(T // 128):
                        pv = PS[6 + c % 2]
                        for k in range(KC):
                            S.mm(pv, pv.ap, hT, hT.ap[:, k, c * 128:(c + 1) * 128], buf, wv[:, k, :], k == 0, k == KC - 1)
                        v_ = vt[c % 2]
                        S.cp('act', v_, v_.ap, pv, pv.ap)
                        S.dma('sync', VTOK[t0 // 128 + c, :, 0:512], v_.ap, reads=[v_])
                jobs.append(dict(loads=[(Wt['akv', l], KC, 512, 512, 0)], fn=fnv))
            run_stream(jobs, wb)
            S.barrier()
            S.dma('sync', KX_in, KT[0:4, :, CTX:NT])
            S.dma('sync', VX_in, VTOK[2:NCH, :, 0:512])
            S.barrier()
            exchange(KX_in, KX_out)
            exchange(VX_in, VX_out)
            A.reset()
            NK = CTX + 2 * L
            NKC = NK // 128
            Ks = A.alloc('Ks', (4, NK), BF16)
            Vs = A.alloc('Vs', (NKC, 512), BF16)
            qs = A.alloc('qs', (KC, 512), BF16)
            at = A.alloc('at', (KC, 512), BF16)
            xt = A.alloc('xt', (KC, 512), F32)
            wb = [A.alloc('wb%d' % i, (4096,), BF16) for i in range(2)]
            Eb = [A.alloc('E%d' % i, (512,), BF16) for i in range(3)]
            rec = [A.alloc('rec%d' % i, (512,), F32) for i in range(2)]
            S.dma('sync', Ks.ap[:, :, 0:CTX], hm(KT)[:, 0:4, 0:CTX], writes=[Ks])
            S.dma('sync', Vs.ap[:, 0:2, :], hm(VTOK)[:, 0:2, 0:512], writes=[Vs])
            for r_ in range(2):
                S.dma('sync', Ks.ap[:, :, CTX + r_ * L:CTX + (r_ + 1) * L], hm(KX_out[r_]), writes=[Ks])
                S.dma('sync', Vs.ap[:, 2 + r_ * NLC:2 + (r_ + 1) * NLC, :], hm(VX_out[r_]), writes=[Vs])
            QT3 = hm(QT)
            scale = 128.0 ** -0.5
            jobs = []
            for (t0, T, j) in mixer_tiles:
                if l == DEPTH - 1 and j == 1:
                    continue
                chunks = [0, 1] if j == 1 else list(range(NKC))

                def pre(t0=t0, T=T, j=j, chunks=chunks):
                    S.dma('sync', qs.ap[:, :, :T], QT3[:, :, t0:t0 + T], writes=[qs])
                    S.dma('sync', xt.ap[:, :, :T], XI3[:, :, t0:t0 + T], writes=[xt])
                    n = len(chunks)
                    for h in range(16):
                        kvh = h // 4
                        po, pd = PS[2 + (h % 2) * 2], PS[3 + (h % 2) * 2]

                        def score(ci):
                            c = chunks[ci]
                            ps = PS[ci % 2]
                            S.mm(ps, ps.ap[:, :T], Ks, Ks.ap[:, kvh, c * 128:(c + 1) * 128], qs, qs.ap[:, h, :T], True, True)
                        score(0)
                        for ci in range(n):
                            if ci + 1 < n:
                                score(ci + 1)
                            c = chunks[ci]
                            ps = PS[ci % 2]
                            E = Eb[ci % 3]
                            S.act(E, E.ap[:, :T], ps, ps.ap[:, :T], AF.Exp, scale=scale)
                            S.mm(po, po.ap[:, :T], Vs, Vs.ap[:, c, kvh * 128:(kvh + 1) * 128], E, E.ap[:, :T], ci == 0, ci == n - 1)
                            S.mm(pd, pd.ap[:, :T], None, ones_bf, E, E.ap[:, :T], ci == 0, ci == n - 1)
                        r_ = rec[h % 2]
                        S.recip(r_, r_.ap[:, :T], pd, pd.ap[:, :T])
                        S.tt('dve', at, at.ap[:, h, :T], po, po.ap[:, :T], r_, r_.ap[:, :T], ALU.mult)

                for s in range(8):
                    def fn(buf, s=s, t0=t0, T=T, j=j):
                        wv = wview(buf, 0, KC, 256)
                        for ff in range(2):
                            f = 2 * s + ff
                            pw = PS[6 + f % 2]
                            for k in range(KC):
                                S.mm(pw, pw.ap[:, :T], buf, wv[:, k, ff * 128:(ff + 1) * 128], at, at.ap[:, k, :T], k == 0, k == KC - 1)
                            S.stt(xt, xt.ap[:, f, :T], pw, pw.ap[:, :T], mG(l, 0, f, j), xt, xt.ap[:, f, :T], ALU.mult, ALU.add)
                            if f == KC - 1:
                                S.dma('sync', XO3[:, :, t0:t0 + T], xt.ap[:, :, :T], reads=[xt])
                    jobs.append(dict(pre=pre if s == 0 else None, loads=[(Wt['ao', l], KC, s * 256, 256, 0)], fn=fn))
            run_stream(jobs, wb)
            S.barrier()


        def ret_phase(l, XI3, XO3):
            jl = l // 3
            A.reset()
            xts = [A.alloc('xt%d' % i, (KC, 512), F32) for i in range(2)]
            hTs = [A.alloc('hT%d' % i, (KC, 512), BF16) for i in range(2)]
            wb = [A.alloc('wb%d' % i, (8192,), BF16) for i in range(3)]
            sq = [A.alloc('sq%d' % i, (512,), BF16) for i in range(2)]
            tmp = [A.alloc('tmp%d' % i, (512,), F32) for i in range(2)]
            sd = A.alloc('sd', (512,), F32)
            rs = A.alloc('rs', (512,), F32)
            ob = [A.alloc('ob%d' % i, (512,), BF16) for i in range(4)]
            cnt = [0]

            def pl(ti):
                (t0, T, j) = mixer_tiles[ti]
                S.dma('sync', xts[ti % 2].ap[:, :, :T], XI3[:, :, t0:t0 + T], writes=[xts[ti % 2]])

            def pn(ti):
                (t0, T, j) = mixer_tiles[ti]
                norm_mod(xts[ti % 2], T, l, 0, j, hTs[ti % 2], sq, sd, rs, tmp)
            pl(0)
            pn(0)
            jobs = []
            for ti, (t0, T, j) in enumerate(mixer_tiles):
                hT = hTs[ti % 2]
                tj = []
                for (wname, dst, scl) in (('rq', QT, 1.0), ('rk', KT, 1.0 / 16)):
                    for s in range(8):
                        def fn(buf, s=s, t0=t0, T=T, dst=dst, scl=scl, hT=hT):
                            wv = wview(buf, 0, KC, 256)
                            for hh in range(2):
                                i = cnt[0]
                                cnt[0] += 1
                                pq = PS[i % 2]
                                for k in range(KC):
                                    S.mm(pq, pq.ap[:, :T], buf, wv[:, k, hh * 128:(hh + 1) * 128], hT, hT.ap[:, k, :T], k == 0, k == KC - 1)
                                o_ = ob[i % 4]
                                S.act(o_, o_.ap[:, :T], pq, pq.ap[:, :T], AF.Copy, scale=scl)
                                S.dma('sync', dst[2 * s + hh, :, t0:t0 + T], o_.ap[:, :T], reads=[o_])
                        tj.append(dict(loads=[(Wt[wname, l], KC, s * 256, 256, 0)], fn=fn))
                for (wname, dst, nsl, func, scl) in (('rk', KTOK, 4, AF.Copy, 1.0 / 16), ('rv', VTOK, 8, AF.Copy, 1.0),
                                                     ('rg', GTOK, 8, AF.Silu, 1.0)):
                    for s in range(nsl):
                        def fn(buf, s=s, t0=t0, T=T, dst=dst, func=func, scl=scl, hT=hT):
                            wv = wview(buf, 0, KC, 512)
                            for c in range(T // 128):
                                i = cnt[0]
                                cnt[0] += 1
                                pv = PS[2 + i % 2]
                                for k in range(KC):
                                    S.mm(pv, pv.ap, hT, hT.ap[:, k, c * 128:(c + 1) * 128], buf, wv[:, k, :], k == 0, k == KC - 1)
                                o_ = ob[i % 4]
                                S.act(o_, o_.ap, pv, pv.ap, func, scale=scl)
                                S.dma('sync', dst[t0 // 128 + c, :, s * 512:(s + 1) * 512], o_.ap, reads=[o_])
                        tj.append(dict(loads=[(Wt[wname, l], KC, s * 512, 512, 0)], fn=fn))
                if ti + 1 < len(mixer_tiles):
                    tj[0]['pre'] = (lambda ti=ti: pl(ti + 1))
                    tj[len(tj) // 2]['pre'] = (lambda ti=ti: pn(ti + 1))
                jobs.extend(tj)
            run_stream(jobs, wb)
            S.barrier()
            QT3, KT3, ATS3 = hm(QT), hm(KT), hm(ATS)
            KTOK3, VTOK3 = hm(KTOK), hm(VTOK)

            def head_consts(hc, h):
                lgf = lgT.ap[:, jl * 16 + h:jl * 16 + h + 1]
                lgb = lgT.ap[:, jl * 16 + 8 + h:jl * 16 + 8 + h + 1]
                S.act(hc, hc.ap[:, 0:1], None, cst('p1', 1), AF.Exp, scale=lgf)
                S.act(hc, hc.ap[:, 1:2], None, cst('cmp', 1), AF.Exp, scale=lgb)
                S.act(hc, hc.ap[:, 2:3], None, cst('c1p', 1), AF.Exp, scale=lgf)
                S.act(hc, hc.ap[:, 3:4], None, cst('p0', 1), AF.Exp, scale=lgb)
                S.act(hc, hc.ap[:, 4:5], None, lgf, AF.Exp, scale=128.0)
                S.act(hc, hc.ap[:, 5:6], None, lgb, AF.Exp, scale=128.0)
                return lgf, lgb

            A.reset()
            Kfs = [A.alloc('Kf%d' % i, (NCH, 256), BF16) for i in range(2)]
            Vs_ = [A.alloc('V%d' % i, (NCH, 512), BF16) for i in range(2)]
            stall = [A.alloc('stall%d' % i, (NCH, 2, 512), BF16) for i in range(2)]
            Sms = [A.alloc('Sm%d' % i, (2, 512), F32) for i in range(2)]
            hcs = [A.alloc('hc%d' % i, (8,), F32) for i in range(2)]
            for h in range(8):
                Kf, V, sta, hc = Kfs[h % 2], Vs_[h % 2], stall[h % 2], hcs[h % 2]
                S.dma('sync', Kf.ap, KTOK3[:, :, h * 256:(h + 1) * 256], writes=[Kf])
                S.dma('sync', V.ap, VTOK3[:, :, h * 512:(h + 1) * 512], writes=[V])
                head_consts(hc, h)
                kdf, cdf = hc.ap[:, 2:3], hc.ap[:, 4:5]
                S.act(Kf, Kf.ap, Kf, Kf.ap, AF.Copy, scale=kdf, rd=(hc,))
                Sm = Sms[0]
                S.ms('pool', Sm, Sm.ap, 0.0)
                for c in range(NCH):
                    Sn = Sms[(c + 1) % 2]
                    S.cp('act', sta, sta.ap[:, c], Sm, Sm.ap)
                    for kc in range(2):
                        pb_ = PS[(2 * c + kc) % 4]
                        S.mm(pb_, pb_.ap, Kf, Kf.ap[:, c, kc * 128:(kc + 1) * 128], V, V.ap[:, c, :], True, True)
                    for kc in range(2):
                        pb_ = PS[(2 * c + kc) % 4]
                        S.stt(Sn, Sn.ap[:, kc, :], Sm, Sm.ap[:, kc, :], cdf, pb_, pb_.ap, ALU.mult, ALU.add, rd=(hc,))
                    Sm = Sn
                S.dma('sync', hm(SBD[h]).rearrange("p c (a b) -> p c a b", a=2), sta.ap, reads=[sta])
                S.dma('sync', E_in[h // 4][h % 4].rearrange("p (a b) -> p a b", a=2), Sm.ap, reads=[Sm])
            S.barrier()
            exchange(E_in[0], E_out[0])
            exchange(E_in[1], E_out[1])
            A.reset()
            q = A.alloc('q', (2, NT), BF16)
            qf = A.alloc('qf', (2, NT), BF16)
            qb = A.alloc('qb', (2, NT), BF16)
            k_ = A.alloc('k', (2, NT), BF16)
            Kb = A.alloc('Kb', (NCH, 256), BF16)
            V = A.alloc('V', (NCH, 512), BF16)
            SBr = [A.alloc('SB%d' % i, (2, 512), BF16) for i in range(4)]
            oo = A.alloc('oo', (NCH, 512), F32)
            atb = A.alloc('atb', (4, NT), BF16)
            Sms = [A.alloc('Sm%d' % i, (2, 512), F32) for i in range(2)]
            Scs = [A.alloc('Sc%d' % i, (2, 512), BF16) for i in range(4)]
            e2 = A.alloc('e2', (2, 2, 512), F32)
            gb = [A.alloc('gb%d' % i, (512,), BF16) for i in range(4)]
            m1 = A.alloc('m1', (128,), F32)
            m2 = A.alloc('m2', (128,), F32)
            Mc = A.alloc('Mc', (128,), F32)
            patf = A.alloc('patf', (128,), F32)
            patb = A.alloc('patb', (128,), F32)
            hc = A.alloc('hc', (8,), F32)
            scb = [A.alloc('scb%d' % i, (128,), BF16) for i in range(3)]
            yb = [A.alloc('yb%d' % i, (512,), BF16) for i in range(4)]
            st6 = A.alloc('st6', (NCH, 6), F32)
            mv = A.alloc('mv', (NCH, 2), F32)
            rsd = A.alloc('rsd', (NCH,), F32)
            nmr = A.alloc('nmr', (NCH,), F32)
            order_b = list(range(NCH - 1, 1, -1)) + [1, 0]
            cmin = 2 if l == DEPTH - 1 else 0
            outs = [c for c in order_b if c >= cmin]

            def q4(t):
                return t.ap.rearrange("p a (c n) -> p a c n", n=128)

            def pat4(t):
                return t.ap.unsqueeze(1).unsqueeze(1).to_broadcast([128, 2, NCH, 128])
            def head_loads(h):
                S.dma('sync', q.ap, QT3[:, 2 * h:2 * h + 2, :], writes=[q])
                S.dma('sync', k_.ap, KT3[:, 2 * h:2 * h + 2, :], writes=[k_])
                S.dma('sync', Kb.ap, KTOK3[:, :, h * 256:(h + 1) * 256], writes=[Kb])
                S.dma('sync', V.ap, VTOK3[:, :, h * 512:(h + 1) * 512], writes=[V])
                S.dma('sync', e2.ap, E_out[h // 4][:, h % 4].rearrange("r p (a b) -> p r a b", a=2), writes=[e2])
            head_loads(0)
            for h in range(8):
                lgf, lgb = head_consts(hc, h)
                S.act(m1, m1.ap, None, cst('d1'), AF.Exp, scale=lgf)
                S.tt('dve', m1, m1.ap, m1, m1.ap, None, cst('u'), ALU.mult)
                S.act(m2, m2.ap, None, cst('d2'), AF.Exp, scale=lgb)
                S.tt('dve', m2, m2.ap, m2, m2.ap, None, cst('l'), ALU.mult)
                S.tt('dve', Mc, Mc.ap, m1, m1.ap, m2, m2.ap, ALU.add)
                S.act(patf, patf.ap, None, cst('r1'), AF.Exp, scale=lgf)
                S.act(patb, patb.ap, None, cst('r2'), AF.Exp, scale=lgb)
                S.tt('dve', qf, q4(qf), q, q4(q), patf, pat4(patf), ALU.mult)
                S.tt('dve', qb, q4(qb), q, q4(q), patb, pat4(patb), ALU.mult)
                qdf, qdb, kdf, kdb, cdf, cdb = (hc.ap[:, i:i + 1] for i in range(6))
                S.act(Kb, Kb.ap, Kb, Kb.ap, AF.Copy, scale=kdb, rd=(hc,))
                Sm = Sms[0]
                mi = 0
                S.ts('dve', Sm, Sm.ap, e2, e2.ap[:, 0], msk0, None, ALU.mult)
                S.stt(Sm, Sm.ap, e2, e2.ap[:, 1], msk1, Sm, Sm.ap, ALU.mult, ALU.add)
                si = 0
                S.cp('act', Scs[0], Scs[0].ap, Sm, Sm.ap)

                def emit_pb(c, po, Sc):
                    csl = slice(c * 128, (c + 1) * 128)
                    for kc in range(2):
                        S.mm(po, po.ap, qb, qb.ap[:, kc, csl], Sc, Sc.ap[:, kc, :], False, kc == 1)
                    S.cp('act', oo, oo.ap[:, c, :], po, po.ap)

                prev = None
                for i, c in enumerate(order_b):
                    if c == 1:
                        S.ms('pool', Sm, Sm.ap, 0.0)
                        si += 1
                        S.ms('pool', Scs[si % 4], Scs[si % 4].ap, 0.0)
                    Sc = Scs[si % 4]
                    csl = slice(c * 128, (c + 1) * 128)
                    want = c >= cmin
                    upd = c not in (2, 0)
                    pss = PS[0]
                    po = PS[3 + i % 4]
                    if want:
                        sb_ = SBr[i % 4]
                        S.dma('sync', sb_.ap, SBD[h, c].rearrange("p (a b) -> p a b", a=2), writes=[sb_])
                        for kc in range(2):
                            S.mm(pss, pss.ap[:, :128], k_, k_.ap[:, kc, csl], q, q.ap[:, kc, csl], kc == 0, kc == 1)
                        sc_ = scb[i % 3]
                        S.tt('dve', sc_, sc_.ap, pss, pss.ap[:, :128], Mc, Mc.ap, ALU.mult)
                    if upd:
                        for kc in range(2):
                            S.mm(PS[1 + kc], PS[1 + kc].ap, Kb, Kb.ap[:, c, kc * 128:(kc + 1) * 128], V, V.ap[:, c, :], True, True)
                    if want:
                        for kc in range(2):
                            S.mm(po, po.ap, qf, qf.ap[:, kc, csl], sb_, sb_.ap[:, kc, :], kc == 0, False)
                        S.mm(po, po.ap, sc_, sc_.ap, V, V.ap[:, c, :], False, False)
                    if prev is not None:
                        emit_pb(*prev)
                    if upd:
                        mi += 1
                        Sn = Sms[mi % 2]
                        for kc in range(2):
                            S.stt(Sn, Sn.ap[:, kc, :], Sm, Sm.ap[:, kc, :], cdb, PS[1 + kc], PS[1 + kc].ap, ALU.mult, ALU.add, rd=(hc,))
                        Sm = Sn
                        si += 1
                        S.cp('act', Scs[si % 4], Scs[si % 4].ap, Sm, Sm.ap)
                    prev = (c, po, Sc) if want else None
                if prev is not None:
                    emit_pb(*prev)
                if h + 1 < 8:
                    head_loads(h + 1)
                for c in outs:
                    S.op('dve', lambda e, a=st6.ap[:, c, :], b=oo.ap[:, c, :]: e.bn_stats(a, b), reads=[oo], writes=[st6])
                    S.op('dve', lambda e, a=mv.ap[:, c, :], b=st6.ap[:, c, :]: e.bn_aggr(a, b), reads=[st6], writes=[mv])
                S.act(rsd, rsd.ap[:, cmin:NCH], mv, mv.ap[:, cmin:NCH, 1], AF.Sqrt, bias=eps_ln, scale=1.0)
                S.recip(rsd, rsd.ap[:, cmin:NCH], rsd, rsd.ap[:, cmin:NCH])
                S.stt(nmr, nmr.ap[:, cmin:NCH], mv, mv.ap[:, cmin:NCH, 0], -1.0, rsd, rsd.ap[:, cmin:NCH], ALU.mult, ALU.mult)

                def s3(i):
                    c = outs[i]
                    g_ = gb[i % 4]
                    S.dma('sync', g_.ap, GTOK[c, :, h * 512:(h + 1) * 512], writes=[g_])
                    S.act(oo, oo.ap[:, c, :], oo, oo.ap[:, c, :], AF.Identity, bias=nmr.ap[:, c:c + 1], scale=rsd.ap[:, c:c + 1], rd=(nmr, rsd))
                    yb_ = yb[i % 4]
                    S.tt('dve', yb_, yb_.ap, oo, oo.ap[:, c, :], g_, g_.ap, ALU.mult)

                def s4(i):
                    c = outs[i]
                    yb_ = yb[i % 4]
                    pt = PS[1 + i % 2]
                    ptv = pt.ap[:, 0:256].bitcast(BF16)
                    for vc in range(4):
                        S.tr(pt, ptv[:, vc * 128:(vc + 1) * 128], yb_, yb_.ap[:, vc * 128:(vc + 1) * 128], ident_bf)
                    S.tt('dve', atb, atb.ap[:, :, c * 128:(c + 1) * 128], pt, ptv.rearrange("p (a b) -> p a b", a=4),
                         None, sv('gnw', jl * 32 + h * 4, 4).unsqueeze(2).to_broadcast([128, 4, 128]), ALU.mult)
                s3(0)
                for i in range(len(outs)):
                    if i + 1 < len(outs):
                        s3(i + 1)
                    s4(i)
                S.dma('sync', ATS3[:, 4 * h:4 * h + 4, cmin * 128:NT], atb.ap[:, :, cmin * 128:NT], reads=[atb])
            S.barrier()
            A.reset()
            tl = [t for t in mixer_tiles if not (l == DEPTH - 1 and t[2] == 1)]
            half = (len(tl) + 1) // 2
            groups = [tl[:half], tl[half:]]
            gmax = max(sum(t[1] for t in g) for g in groups)
            atg = A.alloc('atg', (32, gmax), BF16)
            wb = [A.alloc('wb%d' % i, (8192,), BF16) for i in range(3)]
            xo = [A.alloc('xo%d' % i, (2, 512), F32) for i in range(4)]
            xi = [0]
            for g in groups:
                if not g:
                    continue
                offs = []
                o = 0
                for (t0, T, j) in g:
                    S.dma('sync', atg.ap[:, :, o:o + T], ATS3[:, :, t0:t0 + T], writes=[atg])
                    offs.append(o)
                    o += T
                jobs = []
                for s_ in range(8):
                    def fn(buf, s_=s_, g=g, offs=offs):
                        wv = wview(buf, 0, 32, 256)
                        for (t0, T, j), o in zip(g, offs):
                            x_ = xo[xi[0] % 4]
                            xi[0] += 1
                            S.dma('sync', x_.ap[:, :, :T], XI3[:, 2 * s_:2 * s_ + 2, t0:t0 + T], writes=[x_])
                            for ff in range(2):
                                f = 2 * s_ + ff
                                pw = PS[(xi[0] * 2 + ff) % 4]
                                for k in range(32):
                                    S.mm(pw, pw.ap[:, :T], buf, wv[:, k, ff * 128:(ff + 1) * 128], atg, atg.ap[:, k, o:o + T], k == 0, k == 31)
                                S.stt(x_, x_.ap[:, ff, :T], pw, pw.ap[:, :T], mG(l, 0, f, j), x_, x_.ap[:, ff, :T], ALU.mult, ALU.add)
                            S.dma('sync', XO3[:, 2 * s_:2 * s_ + 2, t0:t0 + T], x_.ap[:, :, :T], reads=[x_])
                    jobs.append(dict(loads=[(Wt['ro', l], 32, s_ * 256, 256, 0)], fn=fn))
                run_stream(jobs, wb)
            S.barrier()

        PHASES = {'cnv': cnv_phase, 'att': att_phase, 'ret': ret_phase}
        cur, nxt = (XA, XA3), (XB, XB3)
        for li, l in enumerate(layers):
            if do_mix and MIXER[l] in PHASES:
                PHASES[MIXER[l]](l, cur[1], nxt[1])
            else:
                copy_phase(cur[0], nxt[0])
            cur, nxt = nxt, cur
            lastl = (li == len(layers) - 1)
            if do_ffn:
                halo_exchange(cur[1])
                ffn_phase(l, cur[1], nxt[1], lastl)
                cur, nxt = nxt, cur
            elif lastl:
                S.dma('sync', outT, cur[0][:, CTX:NT], final=True)
        S.finish()
    return nc


def _weights_for(inp, layers, do_mix=True, do_ffn=True):
    m = {}
    for l in layers:
        if do_ffn:
            m["wg%d" % l] = inp['ffn_w_gate'][l]
            m["wu%d" % l] = inp['ffn_w_up'][l]
            m["wd%d" % l] = inp['ffn_w_down'][l]
        if not do_mix:
            continue
        if MIXER[l] == 'ret':
            j = l // 3
            m["rq%d" % l] = inp['ret_wq'][j]
            m["rk%d" % l] = inp['ret_wk'][j]
            m["rv%d" % l] = inp['ret_wv'][j]
            m["rg%d" % l] = inp['ret_wg'][j]
            m["ro%d" % l] = inp['ret_wo'][j]
            m["gn%d" % l] = inp['ret_gn_w'][j].reshape(1, HV)
        elif MIXER[l] == 'att':
            m["aq%d" % l] = inp['att_wq'][0]
            m["akv%d" % l] = inp['att_wkv'][0]
            m["ao%d" % l] = inp['att_wo'][0]
        else:
            m["c1_%d" % l] = inp['cnv_w1'][0]
            m["c2_%d" % l] = inp['cnv_w2'][0]
    return {k: np.ascontiguousarray(np.asarray(v, np.float32)) for k, v in m.items()}


def run(inp, nlat=4, layers=(0, 1, 2, 3), do_mix=True, do_ffn=True, nb=4):
    inp = {k: np.asarray(v) for k, v in inp.items()}
    L = nlat * 512
    ncores = 2 * nb
    prog = build(nlat, layers, do_mix, do_ffn, ncores)
    fw = (96 // ncores) * 128
    cst = make_consts()
    wts = _weights_for(inp, layers, do_mix, do_ffn)
    in_maps = []
    for b in range(nb):
        for r in range(2):
            if r == 0:
                tpos = np.arange(L)
                xc, xl = inp['ctx'][b], inp['x'][b][:L]
            else:
                tpos = np.arange(2 * L - 1, L - 1, -1)
                xc, xl = inp['ctx'][b][::-1], inp['x'][b][L:2 * L][::-1]
            rc, rs_ = make_rope(tpos)
            xT = np.concatenate([xc.T, xl.T], axis=1)
            m = {'xT': np.ascontiguousarray(xT, dtype=np.float32), 'sv': pack_sv(inp, b, r), 'cst': cst,
                 'rope_cos': rc, 'rope_sin': rs_}
            m.update(wts)
            ci = 2 * b + r
            for l in layers:
                m["mw%d" % l] = np.ascontiguousarray(inp['mod_w'][l][:, ci * fw:(ci + 1) * fw], dtype=np.float32)
            in_maps.append(m)
    res = run_bass_kernel_spmd(prog, in_maps, core_ids=list(range(2 * nb)))
    outs = []
    for b in range(nb):
        o0 = res.results[2 * b]['outT'].T
        o1 = res.results[2 * b + 1]['outT'].T[::-1]
        outs.append(np.concatenate([o0, o1], axis=0))
    return np.ascontiguousarray(np.stack(outs))


def kernel(**inputs):
    return run(inputs).astype(np.float32)
```

```python
import contextlib
import numpy as np
import concourse.bass as bass
import concourse.mybir as mybir
from concourse.bass_utils import run_bass_kernel_spmd

F32 = mybir.dt.float32
BF16 = mybir.dt.bfloat16
AF = mybir.ActivationFunctionType
ALU = mybir.AluOpType

ENGS = ('sync', 'act', 'dve', 'pool', 'pe')
BLOCK_NAME = {'sync': 'sync', 'act': 'scalar', 'dve': 'vector', 'pool': 'gpsimd', 'pe': 'tensor'}


class Tile:
    __slots__ = ('name', 'ap', 'w', 'r')

    def __init__(self, name, ap=None):
        self.name = name
        self.ap = ap
        self.w = None
        self.r = {}


class Sched:
    def __init__(self, nc, ndma=(('sync', 30), ('pool', 30), ('act', 8))):
        self.nc = nc
        self.stack = contextlib.ExitStack()
        self.q = {e: [] for e in ENGS}
        self.cnt = {e: 0 for e in ENGS}
        self.waited = {e: {} for e in ENGS}
        self.csem = {}
        self.dsems, self.dval, self.dn = {}, {}, {}
        self.ndma = ndma
        self.finals = []
        self.n_ops = 0
        self.cc_list = []

    def __enter__(self):
        self.stack.__enter__()
        nc = self.nc
        for e in ('act', 'dve', 'pool', 'pe'):
            self.csem[e] = self.stack.enter_context(nc.semaphore("c_" + e))
        for e, n in self.ndma:
            self.dsems[e] = [self.stack.enter_context(nc.semaphore("d_%s_%d" % (e, i))) for i in range(n)]
            self.dval[e] = [0] * n
            self.dn[e] = 0
        return self

    def __exit__(self, *a):
        return self.stack.__exit__(*a)

    def sbuf(self, name, shape, dtype):
        t = self.stack.enter_context(self.nc.sbuf_tensor(name, list(shape), dtype))
        return Tile(name, t[:])

    def psum(self, name, shape, dtype):
        t = self.stack.enter_context(self.nc.psum_tensor(name, list(shape), dtype))
        return Tile(name, t[:])

    def region(self, name):
        return Tile(name, None)

    def _deps(self, reads, writes):
        evs = []
        for t in reads:
            evs.append(t.w)
        for t in writes:
            evs.append(t.w)
            evs.extend(t.r.values())
        return evs

    def _need(self, eng, evs):
        w = self.waited[eng]
        best = {}
        pe_sem = self.csem['pe'].num
        for ev in evs:
            if ev is None:
                continue
            sem, val = ev
            k = sem.num
            if eng == 'pe' and k == pe_sem:
                continue
            if w.get(k, 0) >= val:
                continue
            if k not in best or best[k][1] < val:
                best[k] = (sem, val)
        out = []
        for k, (sem, val) in best.items():
            w[k] = val
            out.append((sem, val))
        return out

    def _record(self, ev, reads, writes):
        k = ev[0].num
        for t in reads:
            t.r[k] = ev
        for t in writes:
            t.w = ev
            t.r = {}

    def op(self, eng, fn, reads=(), writes=()):
        waits = self._need(eng, self._deps(reads, writes))
        self.cnt[eng] += 1
        sem = self.csem[eng]
        ev = (sem, self.cnt[eng])
        self.q[eng].append((waits, fn, sem, 1))
        self._record(ev, reads, writes)
        self.n_ops += 1
        return ev

    def dma(self, eng, out_ap, in_ap, reads=(), writes=(), final=False, **kw):
        evs = self._deps(reads, writes)
        n = len(self.dsems[eng])
        k = self.dn[eng] % n
        self.dn[eng] += 1
        sem = self.dsems[eng][k]
        prev = self.dval[eng][k]
        if prev:
            evs.append((sem, prev))
        waits = self._need(eng, evs)
        val = prev + 16
        self.dval[eng][k] = val
        ev = (sem, val)
        self.q[eng].append((waits, lambda e: e.dma_start(out=out_ap, in_=in_ap, **kw), sem, 16))
        self._record(ev, reads, writes)
        if final:
            self.finals.append(ev)
        self.n_ops += 1
        return ev


    def collective(self, groups, in_ap, out_ap, reads=(), writes=()):
        evs = self._deps(reads, writes)
        sem = self.stack.enter_context(self.nc.semaphore("cc_sem%d" % len(self.cc_list)))
        waits = self._need('pool', evs)
        ev = (sem, 1)
        self.cc_list.append(ev)
        self.q['pool'].append((waits, lambda e: e.collective_compute(
            "AllGather", ALU.bypass, replica_groups=groups, ins=[in_ap.opt()], outs=[out_ap.opt()]), sem, 1))
        self._record(ev, reads, writes)
        return ev

    def barrier(self):
        evs = []
        for e in ('act', 'dve', 'pool', 'pe'):
            if self.cnt[e]:
                evs.append((self.csem[e], self.cnt[e]))
        for e, _ in self.ndma:
            for s, v in zip(self.dsems[e], self.dval[e]):
                if v:
                    evs.append((s, v))
        evs.extend(self.cc_list)
        for e in ENGS:
            waits = self._need(e, evs)
            if waits:
                self.q[e].append((waits, None, None, 0))

    @staticmethod
    def _t(ts):
        return [t for t in ts if t is not None]

    def mm(self, pt, pap, lt, lap, rt, rap, start=True, stop=True):
        self.op('pe', lambda e: e.matmul(pap, lap, rap, start=start, stop=stop),
                reads=self._t((lt, rt)), writes=(pt,))

    def tr(self, pt, pap, it, iap, idap):
        self.op('pe', lambda e: e.transpose(pap, iap, idap), reads=self._t((it,)), writes=(pt,))

    def act(self, ot, oap, it, iap, func, bias=None, scale=None, rd=()):
        kw = {}
        if bias is not None:
            kw['bias'] = bias
        if scale is not None:
            kw['scale'] = scale
        self.op('act', lambda e: e.activation(oap, iap, func, **kw),
                reads=self._t((it,) + tuple(rd)), writes=(ot,))

    def tt(self, eng, ot, oap, at, aap, bt, bap, op):
        self.op(eng, lambda e: e.tensor_tensor(oap, aap, bap, op), reads=self._t((at, bt)), writes=(ot,))

    def ts(self, eng, ot, oap, it, iap, s1, s2, op0, op1=None, rd=()):
        if op1 is None:
            self.op(eng, lambda e: e.tensor_scalar(oap, iap, s1, s2, op0),
                    reads=self._t((it,) + tuple(rd)), writes=(ot,))
        else:
            self.op(eng, lambda e: e.tensor_scalar(oap, iap, s1, s2, op0, op1),
                    reads=self._t((it,) + tuple(rd)), writes=(ot,))

    def stt(self, ot, oap, at, aap, sc, bt, bap, op0, op1, rd=()):
        self.op('dve', lambda e: e.scalar_tensor_tensor(oap, aap, sc, bap, op0, op1),
                reads=self._t((at, bt) + tuple(rd)), writes=(ot,))

    def cp(self, eng, ot, oap, it, iap):
        if eng == 'act':
            self.op('act', lambda e: e.copy(oap, iap), reads=self._t((it,)), writes=(ot,))
        else:
            self.op(eng, lambda e: e.tensor_copy(oap, iap), reads=self._t((it,)), writes=(ot,))

    def ms(self, eng, t, ap, val):
        self.op(eng, lambda e: e.memset(ap, val), writes=(t,))

    def recip(self, ot, oap, it, iap):
        self.op('dve', lambda e: e.reciprocal(oap, iap), reads=self._t((it,)), writes=(ot,))

    def finish(self):
        evs = list(self.finals)
        for e in ('act', 'dve', 'pool', 'pe'):
            if self.cnt[e]:
                evs.append((self.csem[e], self.cnt[e]))
        for e, _ in self.ndma:
            for s, v in zip(self.dsems[e], self.dval[e]):
                if v:
                    evs.append((s, v))
        evs.extend(self.cc_list)
        waits = self._need('sync', evs)
        q = self.q
        q['sync'].append((waits, None, None, 0))
        with self.nc.Block() as block:
            for e in ENGS:
                lst = q[e]
                if not lst:
                    continue

                def body(eng, lst=lst):
                    for waits, fn, sem, inc in lst:
                        for s, v in waits:
                            eng.wait_ge(s, v)
                        if fn is not None:
                            fn(eng).then_inc(sem, inc)
                getattr(block, BLOCK_NAME[e])(body)


D = 2048
KC = 16
CTX = 256
FFN = 5632
FC = 44
HV = 4096
DEPTH = 4
RMS_EPS = 1e-6
LN_EPS = 1e-5
MIXER = {0: 'ret', 1: 'att', 2: 'cnv', 3: 'ret'}
ARENA = 48400


def _sv_layout():
    lay, n = {}, 0
    for name, w in (('call', 64), ('bm', 4), ('cc', 16), ('mod_b', 4 * 96), ('n1w', 64), ('n2w', 64),
                    ('fdw', 4 * 3 * FC), ('fdwb', 4 * FC), ('qg', 1), ('kg', 1), ('cb1', 32),
                    ('cdw', 31 * 16), ('cdwb', 16), ('clnw', 16), ('clnb', 16), ('cb2', 16), ('dec', 32), ('msk', 2), ('gnw', 64)):
        lay[name] = n
        n += w
    return lay, n


SVL, NSV = _sv_layout()
CSTL = {'ident': 0, 'ones': 128, 'perm': 256, 'd1': 384, 'd2': 512, 'u': 640, 'l': 768,
        'p1': 896, 'cmp': 897, 'c1p': 898, 'p0': 899, 'r1': 900, 'r2': 1028}
NCST = 1156


def _chunked(v):
    v = np.asarray(v, np.float32).reshape(-1, 128)
    return np.ascontiguousarray(v.T)


def pack_sv(inp, b, r=0):
    sv = np.zeros((128, NSV), np.float32)

    def put(name, arr):
        o = SVL[name]
        sv[:, o:o + arr.shape[1]] = arr
    ft = (2, 1, 0) if r else (0, 1, 2)
    ct = tuple(range(30, -1, -1)) if r else tuple(range(31))
    put('call', np.concatenate([_chunked(inp['c'][i]) for i in range(4)], axis=1))
    bm = np.zeros((128, 4), np.float32)
    bm[:, b] = 1.0
    put('bm', bm)
    put('cc', _chunked(inp['c_ctx']))
    put('mod_b', np.concatenate([_chunked(inp['mod_b'][l]) for l in range(4)], axis=1))
    put('n1w', np.concatenate([_chunked(inp['norm1_w'][l]) for l in range(4)], axis=1))
    put('n2w', np.concatenate([_chunked(inp['norm2_w'][l]) for l in range(4)], axis=1))
    put('fdw', np.concatenate([_chunked(inp['ffn_dw'][l][t]) for l in range(4) for t in ft], axis=1))
    put('fdwb', np.concatenate([_chunked(inp['ffn_dw_b'][l]) for l in range(4)], axis=1))
    put('qg', _chunked(inp['att_q_gain'][0]))
    put('kg', _chunked(inp['att_k_gain'][0]))
    put('cb1', _chunked(inp['cnv_b1'][0]))
    put('cdw', np.concatenate([_chunked(inp['cnv_dw'][0][t]) for t in ct], axis=1))
    put('cdwb', _chunked(inp['cnv_dw_b'][0]))
    put('clnw', _chunked(inp['cnv_ln_w'][0]))
    put('clnb', _chunked(inp['cnv_ln_b'][0]))
    put('cb2', _chunked(inp['cnv_b2'][0]))
    dec = np.asarray(inp['ret_decay'], np.float32)
    if r:
        dec = dec[:, ::-1, :]
    put('dec', np.broadcast_to(np.ascontiguousarray(dec).reshape(1, 32), (128, 32)))
    put('gnw', np.concatenate([_chunked(inp['ret_gn_w'][j]) for j in range(2)], axis=1))
    msk = np.zeros((128, 2), np.float32)
    msk[:, 1 - r] = 1.0
    put('msk', msk)
    return sv


def make_consts():
    cst = np.zeros((128, NCST), np.float32)
    p = np.arange(128)
    cst[:, 0:128] = np.eye(128)
    cst[:, 128:256] = 1.0
    perm = np.where((p % 64) < 32, p + 32, p - 32)
    pm = np.zeros((128, 128), np.float32)
    pm[perm, p] = 1.0
    cst[:, 256:384] = pm
    m = p[:, None].astype(np.float32)
    n = p[None, :].astype(np.float32)
    cst[:, 384:512] = np.maximum(n - m, 0)
    cst[:, 512:640] = np.maximum(m - n, 0)
    cst[:, 640:768] = (n >= m)
    cst[:, 768:896] = (m >= n)
    cst[:, 896] = p + 1
    cst[:, 897] = 128 - p
    cst[:, 898] = 127 - p
    cst[:, 899] = p
    cst[:, 900:1028] = np.broadcast_to(np.arange(1, 129, dtype=np.float32), (128, 128))
    cst[:, 1028:1156] = np.broadcast_to(128.0 - np.arange(128, dtype=np.float32), (128, 128))
    return cst


def make_rope(tpos):
    p = np.arange(128)
    t = np.asarray(tpos)
    row, col = t // 64, t % 64
    ii = p % 64
    fi = ii % 32
    inv = (10000.0 ** (-(np.arange(32, dtype=np.float32)) / 32.0)).astype(np.float32)
    pos = np.where((p < 64)[:, None], row[None, :], col[None, :]).astype(np.float32)
    ang = pos * inv[fi][:, None]
    cos = np.cos(ang).astype(np.float32)
    sin = np.sin(ang).astype(np.float32)
    sin = np.where((ii < 32)[:, None], -sin, sin).astype(np.float32)
    return np.ascontiguousarray(cos), np.ascontiguousarray(sin)


def _split(n, m):
    k = -(-n // m)
    base, rem = divmod(n, k)
    return [base + (1 if i < rem else 0) for i in range(k)]


class Arena:
    def __init__(self, tile):
        self.t = tile
        self.off = 0

    def reset(self):
        self.off = 0

    def alloc(self, name, fshape, dtype):
        n = 1
        for s in fshape:
            n *= s
        n32 = n if dtype == F32 else (n + 1) // 2
        assert self.off + n32 <= ARENA, (name, self.off, n32)
        ap = self.t.ap[:, self.off:self.off + n32]
        self.off += n32
        if dtype != F32:
            ap = ap.bitcast(dtype)[:, 0:n]
        if len(fshape) == 2:
            ap = ap.rearrange("p (a b) -> p a b", a=fshape[0])
        elif len(fshape) == 3:
            ap = ap.rearrange("p (a b c) -> p a b c", a=fshape[0], b=fshape[1])
        return Tile(name, ap)


def build(nlat=4, layers=(0, 1, 2, 3), do_mix=True, do_ffn=True, ncores=8):
    L = nlat * 512
    NT = CTX + L
    NCH = NT // 128
    nc = bass.Bass("TRN2", target_bir_lowering=False)

    def din(name, shape):
        return nc.dram_tensor(name, list(shape), F32, kind="ExternalInput").ap()

    def dscr(name, shape, dt):
        return nc.dram_tensor(name, list(shape), dt).ap()

    xT_in = din("xT", [D, NT])
    sv_d = din("sv", [128, NSV])
    cst_d = din("cst", [128, NCST])
    rcos = din("rope_cos", [128, L])
    rsin = din("rope_sin", [128, L])
    Wt = {}
    for l in layers:
        Wt['mw', l] = din("mw%d" % l, [D, (96 // ncores) * 128])
        if do_ffn:
            Wt['wg', l] = din("wg%d" % l, [D, FFN])
            Wt['wu', l] = din("wu%d" % l, [D, FFN])
            Wt['wd', l] = din("wd%d" % l, [FFN, D])
        if not do_mix:
            continue
        if MIXER[l] == 'ret':
            Wt['rq', l] = din("rq%d" % l, [D, D])
            Wt['rk', l] = din("rk%d" % l, [D, D])
            Wt['rv', l] = din("rv%d" % l, [D, HV])
            Wt['rg', l] = din("rg%d" % l, [D, HV])
            Wt['ro', l] = din("ro%d" % l, [HV, D])
            Wt['gn', l] = din("gn%d" % l, [1, HV])
        elif MIXER[l] == 'att':
            Wt['aq', l] = din("aq%d" % l, [D, D])
            Wt['akv', l] = din("akv%d" % l, [D, 1024])
            Wt['ao', l] = din("ao%d" % l, [D, D])
        else:
            Wt['c1', l] = din("c1_%d" % l, [D, 2 * D])
            Wt['c2', l] = din("c2_%d" % l, [D, D])
    outT = nc.dram_tensor("outT", [D, L], F32, kind="ExternalOutput").ap()
    XA = dscr("XA", [D, NT], F32)
    XB = dscr("XB", [D, NT], F32)
    QT = dscr("QT", [16, 128, NT], BF16)
    KT = dscr("KT", [16, 128, NT], BF16)
    KTOK = dscr("KTOK", [NCH, 128, D], BF16)
    VTOK = dscr("VTOK", [NCH, 128, HV], BF16)
    GTOK = dscr("GTOK", [NCH, 128, HV], BF16)
    ATS = dscr("ATS", [32, 128, NT], BF16)
    AFF = dscr("AFF", [FC, 128, NT], BF16)
    GT = dscr("GT", [16, 128, NT], F32)
    CT = dscr("CT", [16, 128, NT], F32)
    SBD = dscr("SBD", [8, NCH, 128, 1024], BF16)
    NLC = L // 128
    GROUPS = [[2 * i, 2 * i + 1] for i in range(ncores // 2)]
    NB = ncores // 2
    FPC = 96 // ncores
    NCOL = NB + 1
    nL = len(layers)
    MD_in = dscr("MD_in", [128, nL * FPC * NCOL], F32)
    MD_out = dscr("MD_out", [ncores, 128, nL * FPC * NCOL], F32)
    XH_in = dscr("XH_in", [D], F32)
    XH_out = dscr("XH_out", [2, D], F32)
    CH_in = dscr("CH_in", [16, 128, 15], F32)
    CH_out = dscr("CH_out", [2, 16, 128, 15], F32)
    KX_in = dscr("KX_in", [4, 128, L], BF16)
    KX_out = dscr("KX_out", [2, 4, 128, L], BF16)
    VX_in = dscr("VX_in", [NLC, 128, 512], BF16)
    VX_out = dscr("VX_out", [2, NLC, 128, 512], BF16)
    E_in = [dscr("E_in%d" % i, [4, 128, 1024], F32) for i in range(2)]
    E_out = [dscr("E_out%d" % i, [2, 4, 128, 1024], F32) for i in range(2)]

    def fm(ap2d):
        return ap2d.rearrange("(c p) n -> p c n", p=128)

    def hm(ap3d):
        return ap3d.rearrange("r p n -> p r n")

    XA3, XB3, OUT3 = fm(XA), fm(XB), fm(outT)

    mixer_tiles = [(0, 256, 1)] + [(CTX + 512 * i, 512, 0) for i in range(nlat)]
    ffn_tiles = [(0, 256, True, True, 1)]
    t0 = CTX
    sizes = _split(L, 510)
    for i, T in enumerate(sizes):
        ffn_tiles.append((t0, T, i == 0, 'partner' if i == len(sizes) - 1 else False, 0))
        t0 += T

    S = Sched(nc)
    with S:
        svt = S.sbuf("svt", [128, NSV], F32)
        cstt = S.sbuf("cstt", [128, NCST], F32)
        cb = S.sbuf("cb", [128, 384], BF16)
        modv = S.sbuf("modv", [128, 4, 96, 2], F32)
        AB = S.sbuf("AB", [128, 4, 2, 16, 2], F32)
        b2g = S.sbuf("b2g", [128, 16, 2], F32)
        sbf = S.sbuf("sbf", [128, 16, NCOL], BF16)
        epsT = S.sbuf("epsT", [128, 2], F32)
        lgT = S.sbuf("lgT", [128, 32], F32)
        hx = S.sbuf("hx", [128, 16], F32)
        hx2 = S.sbuf("hx2", [128, 2, 16], F32)
        arena_t = S.sbuf("arena", [128, ARENA], F32)
        A = Arena(arena_t)
        PS = [S.psum("ps%d" % i, [128, 512], F32) for i in range(8)]

        def sv(name, idx=0, n=1):
            o = SVL[name] + idx
            return svt.ap[:, o:o + n]

        def cst(name, n=128):
            o = CSTL[name]
            return cstt.ap[:, o:o + n]

        msk0, msk1 = sv('msk', 0), sv('msk', 1)

        def exchange(in_t, out_t):
            r_in, r_out = S.region('xin'), S.region('xout')
            S.collective(GROUPS, in_t, out_t, reads=[r_in], writes=[r_out])
            S.barrier()

        def halo_exchange(X3):
            S.dma('sync', XH_in.rearrange("(c p) -> p c", p=128), X3[:, :, NT - 1], allow_slow_non_contiguous=True)
            S.barrier()
            exchange(XH_in, XH_out)
            S.dma('sync', hx2.ap, XH_out.rearrange("r (c p) -> p r c", p=128), writes=[hx2], allow_slow_non_contiguous=True)
            S.ts('dve', hx, hx.ap, hx2, hx2.ap[:, 0, :], msk0, None, ALU.mult)
            S.stt(hx, hx.ap, hx2, hx2.ap[:, 1, :], msk1, hx, hx.ap, ALU.mult, ALU.add)
            S.barrier()

        ident_bf, ones_bf, perm_bf = cb.ap[:, 0:128], cb.ap[:, 128:256], cb.ap[:, 256:384]
        eps_rms, eps_ln = epsT.ap[:, 0:1], epsT.ap[:, 1:2]

        def mA(l, w, k, j):
            return AB.ap[:, l, w, k, j:j + 1]

        def mB(l, w, k, j):
            return modv.ap[:, l, (0 if w == 0 else 48) + k, j:j + 1]

        def mG(l, w, k, j):
            return modv.ap[:, l, (32 if w == 0 else 80) + k, j:j + 1]

        def run_stream(jobs, bufs):
            nb = len(bufs)
            n = len(jobs)

            def load(s):
                buf = bufs[s % nb]
                for (W2, kc, c0, ncols, off) in jobs[s]['loads']:
                    dst = buf.ap[:, off:off + kc * ncols].rearrange("p (c n) -> p c n", c=kc)
                    S.dma('pool', dst, W2[:, c0:c0 + ncols].rearrange("(c p) n -> p c n", p=128), writes=[buf])
            for s in range(min(nb - 1, n)):
                load(s)
            for s in range(n):
                if s + nb - 1 < n:
                    load(s + nb - 1)
                if jobs[s].get('pre') is not None:
                    jobs[s]['pre']()
                jobs[s]['fn'](bufs[s % nb])

        def wview(buf, off, kc, ncols):
            return buf.ap[:, off:off + kc * ncols].rearrange("p (c n) -> p c n", c=kc)

        def norm_mod(xt, W, l, w, j, hT, sq, sd, rs, tmp, hoff=0, bank=6):
            pst = PS[bank]
            for k in range(KC):
                q = sq[k % 2]
                S.act(q, q.ap[:, :W], xt, xt.ap[:, k, :W], AF.Square)
                S.mm(pst, pst.ap[:, :W], None, ones_bf, q, q.ap[:, :W], k == 0, k == KC - 1)
            S.act(sd, sd.ap[:, :W], pst, pst.ap[:, :W], AF.Sqrt, bias=eps_rms, scale=1.0 / D)
            S.recip(rs, rs.ap[:, :W], sd, sd.ap[:, :W])
            for k in range(KC):
                t = tmp[k % 2]
                S.tt('dve', t, t.ap[:, :W], xt, xt.ap[:, k, :W], rs, rs.ap[:, :W], ALU.mult)
                S.act(hT, hT.ap[:, k, hoff:hoff + W], t, t.ap[:, :W], AF.Identity, bias=mB(l, w, k, j), scale=mA(l, w, k, j))

        S.dma('sync', svt.ap, sv_d, writes=[svt])
        S.dma('sync', cstt.ap, cst_d, writes=[cstt])
        S.dma('sync', XA, xT_in)
        S.cp('dve', cb, cb.ap, cstt, cstt.ap[:, 0:384])
        S.ms('pool', epsT, epsT.ap[:, 0:1], RMS_EPS)
        S.ms('pool', epsT, epsT.ap[:, 1:2], LN_EPS)
        for b_ in range(NB):
            S.act(sbf, sbf.ap[:, :, b_], svt, sv('call', 16 * b_, 16), AF.Silu)
        S.act(sbf, sbf.ap[:, :, NB], svt, sv('cc', 0, 16), AF.Silu)
        S.act(lgT, lgT.ap, svt, sv('dec', 0, 32), AF.Exp)
        S.ts('dve', lgT, lgT.ap, lgT, lgT.ap, -1.0, None, ALU.mult)
        A.reset()
        wb = [A.alloc('wb%d' % i, (8192,), BF16) for i in range(3)]
        mdp = A.alloc('mdp', (nL, FPC, NCOL), F32)
        modall = A.alloc('modall', (nL, 96, NCOL), F32)
        nsl = (FPC * 128) // 512
        for li, l in enumerate(layers):
            pm = PS[li % 2]
            jobs = []
            for s_ in range(nsl):
                def fn(buf, s_=s_, pm=pm):
                    wv = wview(buf, 0, KC, 512)
                    for f4 in range(4):
                        f = s_ * 4 + f4
                        for k in range(KC):
                            S.mm(pm, pm.ap[:, NCOL * f:NCOL * (f + 1)], buf, wv[:, k, f4 * 128:(f4 + 1) * 128],
                                 sbf, sbf.ap[:, k, :], k == 0, k == KC - 1)
                jobs.append(dict(loads=[(Wt['mw', l], KC, s_ * 512, 512, 0)], fn=fn))
            run_stream(jobs, wb)
            S.cp('act', mdp, mdp.ap[:, li], pm, pm.ap[:, 0:FPC * NCOL].rearrange("p (f c) -> p f c", c=NCOL))
        S.dma('sync', MD_in, mdp.ap.rearrange("p l f c -> p (l f c)"), reads=[mdp])
        S.barrier()
        S.collective([list(range(ncores))], MD_in, MD_out)
        S.barrier()
        for li in range(nL):
            S.dma('sync', modall.ap[:, li].rearrange("p (r f) c -> p r f c", r=ncores),
                  MD_out.rearrange("r p (l f c) -> l p r f c", l=nL, f=FPC)[li], writes=[modall])
        for li, l in enumerate(layers):
            mb = sv('mod_b', l * 96, 96)
            S.tt('dve', modv, modv.ap[:, l, :, 1], modall, modall.ap[:, li, :, NB], svt, mb, ALU.add)
            S.stt(modv, modv.ap[:, l, :, 0], modall, modall.ap[:, li, :, 0], sv('bm', 0), svt, mb, ALU.mult, ALU.add)
            for b_ in range(1, NB):
                S.stt(modv, modv.ap[:, l, :, 0], modall, modall.ap[:, li, :, b_], sv('bm', b_),
                      modv, modv.ap[:, l, :, 0], ALU.mult, ALU.add)
        for l in layers:
            for w in range(2):
                base = 16 if w == 0 else 64
                nw = sv('n1w' if w == 0 else 'n2w', l * 16, 16)
                for j in range(2):
                    S.ts('dve', AB, AB.ap[:, l, w, :, j], modv, modv.ap[:, l, base:base + 16, j], 1.0, None, ALU.add)
                    S.tt('dve', AB, AB.ap[:, l, w, :, j], AB, AB.ap[:, l, w, :, j], svt, nw, ALU.mult)
            if MIXER[l] == 'cnv':
                for j in range(2):
                    S.tt('dve', b2g, b2g.ap[:, :, j], svt, sv('cb2', 0, 16), modv, modv.ap[:, l, 32:48, j], ALU.mult)
        S.barrier()

        LB = CTX + 2
        NE = LB + L + 2

        def ffn_phase(l, XI3, XO3, final):
            skip_ctx = (l == DEPTH - 1)
            wins = ([] if skip_ctx else [(0, CTX, 1, 1)]) + [(CTX + 512 * i, 512, 0, LB + 1 + 512 * i) for i in range(nlat)]
            A.reset()
            hT = A.alloc('hT', (KC, NE), BF16)
            xt = A.alloc('xt', (KC, 256), F32)
            wb = [A.alloc('wb%d' % i, (8192,), BF16) for i in range(2)]
            gseg = [A.alloc('gseg%d' % i, (NE,), F32) for i in range(2)]
            acc = A.alloc('acc', (NT,), F32)
            ast = [A.alloc('ast%d' % i, (NT,), BF16) for i in range(2)]
            sq = [A.alloc('sq%d' % i, (256,), BF16) for i in range(2)]
            tmp = [A.alloc('tmp%d' % i, (256,), F32) for i in range(2)]
            sd = A.alloc('sd', (256,), F32)
            rs = A.alloc('rs', (256,), F32)
            for zc in (0, CTX + 1, LB):
                S.ms('pool', hT, hT.ap[:, :, zc:zc + 1], 0.0)
                for g_ in gseg:
                    S.ms('pool', g_, g_.ap[:, zc:zc + 1], 0.0)
            pieces = ([] if skip_ctx else [(0, 256, 1, 1)]) + [(CTX + 256 * i, 256, 0, LB + 1 + 256 * i) for i in range(L // 256)]
            pieces.append((None, 1, 0, LB + L + 1))
            xts = [xt, A.alloc('xtb', (KC, 256), F32)]
            sds = [sd, A.alloc('sdb', (256,), F32)]
            rss = [rs, A.alloc('rsb', (256,), F32)]

            def stA(p):
                (c0, Wp, j, e0) = pieces[p]
                x_, d_, r_ = xts[p % 2], sds[p % 2], rss[p % 2]
                pst = PS[7] if p % 2 == 0 else PS[3]
                if c0 is None:
                    S.cp('pool', x_, x_.ap[:, :, 0], hx, hx.ap)
                else:
                    S.dma('sync', x_.ap[:, :, :Wp], XI3[:, :, c0:c0 + Wp], writes=[x_])
                for k in range(KC):
                    q_ = sq[k % 2]
                    S.act(q_, q_.ap[:, :Wp], x_, x_.ap[:, k, :Wp], AF.Square)
                    S.mm(pst, pst.ap[:, :Wp], None, ones_bf, q_, q_.ap[:, :Wp], k == 0, k == KC - 1)
                S.act(d_, d_.ap[:, :Wp], pst, pst.ap[:, :Wp], AF.Sqrt, bias=eps_rms, scale=1.0 / D)
                S.recip(r_, r_.ap[:, :Wp], d_, d_.ap[:, :Wp])

            def stB(p):
                (c0, Wp, j, e0) = pieces[p]
                x_, r_ = xts[p % 2], rss[p % 2]
                for k in range(KC):
                    t = tmp[k % 2]
                    S.tt('dve', t, t.ap[:, :Wp], x_, x_.ap[:, k, :Wp], r_, r_.ap[:, :Wp], ALU.mult)
                    S.act(hT, hT.ap[:, k, e0:e0 + Wp], t, t.ap[:, :Wp], AF.Identity, bias=mB(l, 1, k, j), scale=mA(l, 1, k, j))
            stA(0)
            for p in range(len(pieces)):
                if p + 1 < len(pieces):
                    stA(p + 1)
                stB(p)
            jobs = []
            cnt = [0, 0]
            for j2 in range(FC // 2):
                def fn(buf, j2=j2):
                    wgv = wview(buf, 0, KC, 256)
                    wuv = wview(buf, 4096, KC, 256)
                    for jj in range(2):
                        jh = 2 * j2 + jj
                        gs = gseg[jh % 2]
                        pus = []
                        for (c0, T, j, e0) in wins:
                            pg = PS[cnt[0] % 3]
                            cnt[0] += 1
                            for k in range(KC):
                                S.mm(pg, pg.ap[:, :T], buf, wgv[:, k, jj * 128:(jj + 1) * 128],
                                     hT, hT.ap[:, k, e0:e0 + T], k == 0, k == KC - 1)
                            S.cp('act', gs, gs.ap[:, e0:e0 + T], pg, pg.ap[:, :T])
                        pg = PS[3]
                        eh = LB + L + 1
                        for k in range(KC):
                            S.mm(pg, pg.ap[:, 0:1], buf, wgv[:, k, jj * 128:(jj + 1) * 128], hT, hT.ap[:, k, eh:eh + 1], k == 0, k == KC - 1)
                        S.cp('act', gs, gs.ap[:, eh:eh + 1], pg, pg.ap[:, 0:1])
                        segs = ([] if skip_ctx else [(0, CTX, 0)]) + [(CTX, L, LB)]
                        for (a0, n, e) in segs:
                            S.act(acc, acc.ap[:, a0:a0 + n], gs, gs.ap[:, e + 1:e + 1 + n], AF.Identity,
                                  bias=sv('fdwb', l * FC + jh), scale=sv('fdw', (l * 3 + 1) * FC + jh))
                            S.stt(acc, acc.ap[:, a0:a0 + n], gs, gs.ap[:, e:e + n], sv('fdw', (l * 3 + 0) * FC + jh),
                                  acc, acc.ap[:, a0:a0 + n], ALU.mult, ALU.add)
                            S.stt(acc, acc.ap[:, a0:a0 + n], gs, gs.ap[:, e + 2:e + 2 + n], sv('fdw', (l * 3 + 2) * FC + jh),
                                  acc, acc.ap[:, a0:a0 + n], ALU.mult, ALU.add)
                        lo = CTX if skip_ctx else 0
                        S.act(acc, acc.ap[:, lo:NT], acc, acc.ap[:, lo:NT], AF.Silu)
                        a_ = ast[jh % 2]
                        for (c0, T, j, e0) in wins:
                            pu = PS[4 + cnt[1] % 3]
                            cnt[1] += 1
                            for k in range(KC):
                                S.mm(pu, pu.ap[:, :T], buf, wuv[:, k, jj * 128:(jj + 1) * 128],
                                     hT, hT.ap[:, k, e0:e0 + T], k == 0, k == KC - 1)
                            S.tt('dve', a_, a_.ap[:, c0:c0 + T], acc, acc.ap[:, c0:c0 + T], pu, pu.ap[:, :T], ALU.mult)
                        S.dma('sync', AFF[jh, :, lo:NT], a_.ap[:, lo:NT], reads=[a_])
                jobs.append(dict(loads=[(Wt['wg', l], KC, j2 * 256, 256, 0), (Wt['wu', l], KC, j2 * 256, 256, 4096)], fn=fn))
            run_stream(jobs, wb)
            S.barrier()
            A.reset()
            wds = [A.alloc('wd%d' % i, (FC, 512), BF16) for i in range(2)]
            at = [A.alloc('at%d' % i, (FC, 512), BF16) for i in range(2)]
            xo = [A.alloc('xo%d' % i, (4, 512), F32) for i in range(1)]
            AFF3 = hm(AFF)
            it = 0
            seq = [(fs, w) for fs in range(4) for w in wins]

            def load_at(i):
                (c0, T, j, e0) = seq[i][1]
                a_ = at[i % 2]
                S.dma('sync', a_.ap[:, :, :T], AFF3[:, :, c0:c0 + T], writes=[a_])
            def load_wd(fs):
                S.dma('pool', wds[fs % 2].ap, Wt['wd', l][:, fs * 512:(fs + 1) * 512].rearrange("(c p) n -> p c n", p=128), writes=[wds[fs % 2]])
            load_at(0)
            load_wd(0)
            for i, (fs, (c0, T, j, e0)) in enumerate(seq):
                wd = wds[fs % 2]
                if i % len(wins) == 0 and fs + 1 < 4:
                    load_wd(fs + 1)
                if i + 1 < len(seq):
                    load_at(i + 1)
                a_, x_ = at[i % 2], xo[0]
                S.dma('sync', x_.ap[:, :, :T], XI3[:, 4 * fs:4 * fs + 4, c0:c0 + T], writes=[x_])
                for ff in range(4):
                    f = 4 * fs + ff
                    po = PS[(4 * i + ff) % 4]
                    for jh in range(FC):
                        S.mm(po, po.ap[:, :T], wd, wd.ap[:, jh, ff * 128:(ff + 1) * 128], a_, a_.ap[:, jh, :T], jh == 0, jh == FC - 1)
                    S.stt(x_, x_.ap[:, ff, :T], po, po.ap[:, :T], mG(l, 1, f, j), x_, x_.ap[:, ff, :T], ALU.mult, ALU.add)
                if final and j == 1:
                    pass
                elif final:
                    S.dma('sync', OUT3[:, 4 * fs:4 * fs + 4, c0 - CTX:c0 - CTX + T], x_.ap[:, :, :T], reads=[x_], final=True)
                else:
                    S.dma('sync', XO3[:, 4 * fs:4 * fs + 4, c0:c0 + T], x_.ap[:, :, :T], reads=[x_])
            S.barrier()

        def copy_phase(XI, XO):
            S.dma('sync', XO, XI)
            S.barrier()

        def cnv_phase(l, XI3, XO3):
            A.reset()
            xts = [A.alloc('xt%d' % i, (KC, 512), F32) for i in range(2)]
            hTs = [A.alloc('hT%d' % i, (KC, 512), BF16) for i in range(2)]
            gl = A.alloc('gl', (KC, 512), F32)
            wb = [A.alloc('wb%d' % i, (8192,), BF16) for i in range(2)]
            sgm = [A.alloc('sgm%d' % i, (512,), F32) for i in range(2)]
            sq = [A.alloc('sq%d' % i, (512,), BF16) for i in range(2)]
            tmp = [A.alloc('tmp%d' % i, (512,), F32) for i in range(2)]
            sd = A.alloc('sd', (512,), F32)
            rs = A.alloc('rs', (512,), F32)
            GT3 = hm(GT)

            def pl(ti):
                (t0, T, j) = mixer_tiles[ti]
                S.dma('sync', xts[ti % 2].ap[:, :, :T], XI3[:, :, t0:t0 + T], writes=[xts[ti % 2]])

            def pn(ti):
                (t0, T, j) = mixer_tiles[ti]
                norm_mod(xts[ti % 2], T, l, 0, j, hTs[ti % 2], sq, sd, rs, tmp)
            pl(0)
            pn(0)
            jobs = []
            for ti, (t0, T, j) in enumerate(mixer_tiles):
                hT = hTs[ti % 2]
                tj = []
                for c in range(KC):
                    def fn(buf, c=c, t0=t0, T=T, hT=hT):
                        wa = wview(buf, 0, KC, 128)
                        wg_ = wview(buf, 2048, KC, 128)
                        pa, pb = PS[c % 2], PS[2 + c % 2]
                        for k in range(KC):
                            S.mm(pa, pa.ap[:, :T], buf, wa[:, k, :], hT, hT.ap[:, k, :T], k == 0, k == KC - 1)
                        for k in range(KC):
                            S.mm(pb, pb.ap[:, :T], buf, wg_[:, k, :], hT, hT.ap[:, k, :T], k == 0, k == KC - 1)
                        g_ = sgm[c % 2]
                        S.act(g_, g_.ap[:, :T], pb, pb.ap[:, :T], AF.Sigmoid, bias=sv('cb1', 16 + c))
                        S.stt(gl, gl.ap[:, c, :T], pa, pa.ap[:, :T], sv('cb1', c), g_, g_.ap[:, :T], ALU.add, ALU.mult)
                        if c == KC - 1:
                            S.dma('sync', GT3[:, :, t0:t0 + T], gl.ap[:, :, :T], reads=[gl])
                    tj.append(dict(loads=[(Wt['c1', l], KC, c * 128, 128, 0), (Wt['c1', l], KC, D + c * 128, 128, 2048)], fn=fn))
                if ti + 1 < len(mixer_tiles):
                    tj[0]['pre'] = (lambda ti=ti: pl(ti + 1))
                    tj[len(tj) // 2]['pre'] = (lambda ti=ti: pn(ti + 1))
                jobs.extend(tj)
            run_stream(jobs, wb)
            S.barrier()
            S.dma('sync', CH_in.rearrange("c p d -> p c d"), GT3[:, :, NT - 15:NT])
            S.barrier()
            exchange(CH_in, CH_out)
            A.reset()
            LB2 = CTX + 30
            NE2 = LB2 + L + 30
            ch2 = A.alloc('ch2', (2, KC, 15), F32)
            hal = A.alloc('hal', (KC, 15), F32)
            gxr = [A.alloc('gxr%d' % i, (NE2,), F32) for i in range(2)]
            gxb = [A.alloc('gxb%d' % i, (NE2,), BF16) for i in range(2)]
            dgs = [A.alloc('dg%d' % i, (31, 128), BF16) for i in range(2)]
            cts = [A.alloc('ct%d' % i, (NT,), F32) for i in range(2)]
            S.dma('sync', ch2.ap, CH_out.rearrange("r c p d -> p r c d"), writes=[ch2])
            for d in range(15):
                S.ts('dve', hal, hal.ap[:, :, d], ch2, ch2.ap[:, 0, :, 14 - d], msk0, None, ALU.mult)
                S.stt(hal, hal.ap[:, :, d], ch2, ch2.ap[:, 1, :, 14 - d], msk1, hal, hal.ap[:, :, d], ALU.mult, ALU.add)
            for g_ in gxr:
                for z0 in (0, 15 + CTX, LB2):
                    S.ms('pool', g_, g_.ap[:, z0:z0 + 15], 0.0)
            wi = 0
            for k in range(KC):
                gr, gbf, dgk, ctk = gxr[k % 2], gxb[k % 2], dgs[k % 2], cts[k % 2]
                S.dma('sync', gr.ap[:, 15:15 + CTX], GT[k, :, 0:CTX], writes=[gr])
                S.dma('sync', gr.ap[:, LB2 + 15:LB2 + 15 + L], GT[k, :, CTX:NT], writes=[gr])
                S.cp('dve', gr, gr.ap[:, LB2 + 15 + L:NE2], hal, hal.ap[:, k, :])
                S.cp('act', gbf, gbf.ap, gr, gr.ap)
                for tp in range(31):
                    S.ts('dve', dgk, dgk.ap[:, tp, :], None, ident_bf, sv('cdw', tp * 16 + k), None, ALU.mult)
                for (c0, T, e0) in [(0, CTX, 0)] + [(CTX + 512 * i, 512, LB2 + 512 * i) for i in range(nlat)]:
                    ps = PS[wi % 4]
                    wi += 1
                    for tp in range(31):
                        S.mm(ps, ps.ap[:, :T], dgk, dgk.ap[:, tp, :], gbf, gbf.ap[:, e0 + tp:e0 + tp + T], tp == 0, tp == 30)
                    S.act(ctk, ctk.ap[:, c0:c0 + T], ps, ps.ap[:, :T], AF.Identity, bias=sv('cdwb', k))
                S.dma('sync', CT[k], ctk.ap, reads=[ctk])
            S.barrier()
            A.reset()
            CT3 = hm(CT)
            ac = A.alloc('ac', (KC, 512), F32)
            a2 = A.alloc('a2', (KC, 512), BF16)
            xt = A.alloc('xt', (KC, 512), F32)
            wb = [A.alloc('wb%d' % i, (2048,), BF16) for i in range(3)]
            xb = [A.alloc('xb%d' % i, (512,), BF16) for i in range(2)]
            sq = [A.alloc('sq%d' % i, (512,), BF16) for i in range(2)]
            tmp = [A.alloc('tmp%d' % i, (512,), F32) for i in range(2)]
            mean = A.alloc('mean', (512,), F32)
            msq = A.alloc('msq', (512,), F32)
            var = A.alloc('var', (512,), F32)
            rs = A.alloc('rs', (512,), F32)
            jobs = []
            for (t0, T, j) in mixer_tiles:
                if l == DEPTH - 1 and j == 1:
                    continue

                def pre(t0=t0, T=T, j=j):
                    S.dma('sync', ac.ap[:, :, :T], CT3[:, :, t0:t0 + T], writes=[ac])
                    S.dma('sync', xt.ap[:, :, :T], XI3[:, :, t0:t0 + T], writes=[xt])
                    pm_, pq_ = PS[4], PS[5]
                    for k in range(KC):
                        b_ = xb[k % 2]
                        q_ = sq[k % 2]
                        S.cp('act', b_, b_.ap[:, :T], ac, ac.ap[:, k, :T])
                        S.act(q_, q_.ap[:, :T], ac, ac.ap[:, k, :T], AF.Square)
                        S.mm(pm_, pm_.ap[:, :T], None, ones_bf, b_, b_.ap[:, :T], k == 0, k == KC - 1)
                        S.mm(pq_, pq_.ap[:, :T], None, ones_bf, q_, q_.ap[:, :T], k == 0, k == KC - 1)
                    S.act(mean, mean.ap[:, :T], pm_, pm_.ap[:, :T], AF.Copy, scale=1.0 / D)
                    S.tt('dve', msq, msq.ap[:, :T], mean, mean.ap[:, :T], mean, mean.ap[:, :T], ALU.mult)
                    S.stt(var, var.ap[:, :T], pq_, pq_.ap[:, :T], 1.0 / D, msq, msq.ap[:, :T], ALU.mult, ALU.subtract)
                    S.act(var, var.ap[:, :T], var, var.ap[:, :T], AF.Sqrt, bias=eps_ln, scale=1.0)
                    S.recip(rs, rs.ap[:, :T], var, var.ap[:, :T])
                    for k in range(KC):
                        t_ = tmp[k % 2]
                        S.tt('dve', t_, t_.ap[:, :T], ac, ac.ap[:, k, :T], mean, mean.ap[:, :T], ALU.subtract)
                        S.tt('dve', t_, t_.ap[:, :T], t_, t_.ap[:, :T], rs, rs.ap[:, :T], ALU.mult)
                        S.act(a2, a2.ap[:, k, :T], t_, t_.ap[:, :T], AF.Silu, bias=sv('clnb', k), scale=sv('clnw', k))
                for f in range(KC):
                    def fn(buf, f=f, t0=t0, T=T, j=j):
                        wv = wview(buf, 0, KC, 128)
                        pw = PS[f % 2]
                        for k in range(KC):
                            S.mm(pw, pw.ap[:, :T], buf, wv[:, k, :], a2, a2.ap[:, k, :T], k == 0, k == KC - 1)
                        S.stt(xt, xt.ap[:, f, :T], pw, pw.ap[:, :T], mG(l, 0, f, j), xt, xt.ap[:, f, :T], ALU.mult, ALU.add)
                        S.ts('dve', xt, xt.ap[:, f, :T], xt, xt.ap[:, f, :T], b2g.ap[:, f, j:j + 1], None, ALU.add)
                        if f == KC - 1:
                            S.dma('sync', XO3[:, :, t0:t0 + T], xt.ap[:, :, :T], reads=[xt])
                    jobs.append(dict(pre=pre if f == 0 else None, loads=[(Wt['c2', l], KC, f * 128, 128, 0)], fn=fn))
            run_stream(jobs, wb)
            S.barrier()


        def att_phase(l, XI3, XO3):
            A.reset()
            xt = A.alloc('xt', (KC, 512), F32)
            hT = A.alloc('hT', (KC, 512), BF16)
            wb = [A.alloc('wb%d' % i, (8192,), BF16) for i in range(3)]
            sq = [A.alloc('sq%d' % i, (512,), BF16) for i in range(2)]
            tmp = [A.alloc('tmp%d' % i, (512,), F32) for i in range(2)]
            sd = A.alloc('sd', (512,), F32)
            rs = A.alloc('rs', (512,), F32)
            cs = A.alloc('cs', (512,), F32)
            sn = A.alloc('sn', (512,), F32)
            qn = [A.alloc('qn%d' % i, (512,), BF16) for i in range(2)]
            qr = [A.alloc('qr%d' % i, (512,), BF16) for i in range(2)]
            hsd = [A.alloc('hsd%d' % i, (512,), F32) for i in range(2)]
            hrs = [A.alloc('hrs%d' % i, (512,), F32) for i in range(2)]
            t1 = [A.alloc('t1%d' % i, (512,), F32) for i in range(2)]
            t2 = [A.alloc('t2%d' % i, (512,), F32) for i in range(2)]
            vt = [A.alloc('vt%d' % i, (512,), BF16) for i in range(2)]
            cnt = [0]
            jobs = []
            for (t0, T, j) in mixer_tiles:
                def pre(t0=t0, T=T, j=j):
                    S.dma('sync', xt.ap[:, :, :T], XI3[:, :, t0:t0 + T], writes=[xt])
                    if j == 0:
                        S.dma('sync', cs.ap[:, :T], rcos[:, t0 - CTX:t0 - CTX + T], writes=[cs])
                        S.dma('sync', sn.ap[:, :T], rsin[:, t0 - CTX:t0 - CTX + T], writes=[sn])
                    norm_mod(xt, T, l, 0, j, hT, sq, sd, rs, tmp)

                def head(buf, wv, hh, gain, dst, t0, T, j):
                    i = cnt[0]
                    cnt[0] += 1
                    pq, pss, pr = PS[i % 2], PS[2 + i % 2], PS[4 + i % 2]
                    for k in range(KC):
                        S.mm(pq, pq.ap[:, :T], buf, wv[:, k, hh * 128:(hh + 1) * 128], hT, hT.ap[:, k, :T], k == 0, k == KC - 1)
                    q_ = sq[i % 2]
                    S.act(q_, q_.ap[:, :T], pq, pq.ap[:, :T], AF.Square)
                    S.mm(pss, pss.ap[:, :T], None, ones_bf, q_, q_.ap[:, :T], True, True)
                    d_, r_ = hsd[i % 2], hrs[i % 2]
                    S.act(d_, d_.ap[:, :T], pss, pss.ap[:, :T], AF.Sqrt, bias=eps_rms, scale=1.0 / 128)
                    S.recip(r_, r_.ap[:, :T], d_, d_.ap[:, :T])
                    n_ = qn[i % 2]
                    S.stt(n_, n_.ap[:, :T], pq, pq.ap[:, :T], gain, r_, r_.ap[:, :T], ALU.mult, ALU.mult)
                    if j == 0:
                        S.mm(pr, pr.ap[:, :T], None, perm_bf, n_, n_.ap[:, :T], True, True)
                        a_, b_, o_ = t1[i % 2], t2[i % 2], qr[i % 2]
                        S.tt('dve', a_, a_.ap[:, :T], n_, n_.ap[:, :T], cs, cs.ap[:, :T], ALU.mult)
                        S.tt('dve', b_, b_.ap[:, :T], pr, pr.ap[:, :T], sn, sn.ap[:, :T], ALU.mult)
                        S.tt('dve', o_, o_.ap[:, :T], a_, a_.ap[:, :T], b_, b_.ap[:, :T], ALU.add)
                        S.dma('sync', dst[:, t0:t0 + T], o_.ap[:, :T], reads=[o_])
                    else:
                        S.dma('sync', dst[:, t0:t0 + T], n_.ap[:, :T], reads=[n_])

                for s in range(8):
                    def fn(buf, s=s, t0=t0, T=T, j=j):
                        wv = wview(buf, 0, KC, 256)
                        for hh in range(2):
                            head(buf, wv, hh, sv('qg'), QT[2 * s + hh], t0, T, j)
                    jobs.append(dict(pre=pre if s == 0 else None, loads=[(Wt['aq', l], KC, s * 256, 256, 0)], fn=fn))

                def fnk(buf, t0=t0, T=T, j=j):
                    wv = wview(buf, 0, KC, 512)
                    for hh in range(4):
                        head(buf, wv, hh, sv('kg'), KT[hh], t0, T, j)
                jobs.append(dict(loads=[(Wt['akv', l], KC, 0, 512, 0)], fn=fnk))

                def fnv(buf, t0=t0, T=T):
                    wv = wview(buf, 0, KC, 512)
                    for c in range(T // 128):
                        pv = PS[6 + c % 2]
                        for k in range(KC):
                            S.mm(pv, pv.ap, hT, hT.ap[:, k, c * 128:(c + 1) * 128], buf, wv[:, k, :], k == 0, k == KC - 1)
                        v_ = vt[c % 2]
                        S.cp('act', v_, v_.ap, pv, pv.ap)
                        S.dma('sync', VTOK[t0 // 128 + c, :, 0:512], v_.ap, reads=[v_])
                jobs.append(dict(loads=[(Wt['akv', l], KC, 512, 512, 0)], fn=fnv))
            run_stream(jobs, wb)
            S.barrier()
            S.dma('sync', KX_in, KT[0:4, :, CTX:NT])
            S.dma('sync', VX_in, VTOK[2:NCH, :, 0:512])
            S.barrier()
            exchange(KX_in, KX_out)
            exchange(VX_in, VX_out)
            A.reset()
            NK = CTX + 2 * L
            NKC = NK // 128
            Ks = A.alloc('Ks', (4, NK), BF16)
            Vs = A.alloc('Vs', (NKC, 512), BF16)
            qs = A.alloc('qs', (KC, 512), BF16)
            at = A.alloc('at', (KC, 512), BF16)
            xt = A.alloc('xt', (KC, 512), F32)
            wb = [A.alloc('wb%d' % i, (4096,), BF16) for i in range(2)]
            Eb = [A.alloc('E%d' % i, (512,), BF16) for i in range(3)]
            rec = [A.alloc('rec%d' % i, (512,), F32) for i in range(2)]
            S.dma('sync', Ks.ap[:, :, 0:CTX], hm(KT)[:, 0:4, 0:CTX], writes=[Ks])
            S.dma('sync', Vs.ap[:, 0:2, :], hm(VTOK)[:, 0:2, 0:512], writes=[Vs])
            for r_ in range(2):
                S.dma('sync', Ks.ap[:, :, CTX + r_ * L:CTX + (r_ + 1) * L], hm(KX_out[r_]), writes=[Ks])
                S.dma('sync', Vs.ap[:, 2 + r_ * NLC:2 + (r_ + 1) * NLC, :], hm(VX_out[r_]), writes=[Vs])
            QT3 = hm(QT)
            scale = 128.0 ** -0.5
            jobs = []
            for (t0, T, j) in mixer_tiles:
                if l == DEPTH - 1 and j == 1:
                    continue
                chunks = [0, 1] if j == 1 else list(range(NKC))

                def pre(t0=t0, T=T, j=j, chunks=chunks):
                    S.dma('sync', qs.ap[:, :, :T], QT3[:, :, t0:t0 + T], writes=[qs])
                    S.dma('sync', xt.ap[:, :, :T], XI3[:, :, t0:t0 + T], writes=[xt])
                    n = len(chunks)
                    for h in range(16):
                        kvh = h // 4
                        po, pd = PS[2 + (h % 2) * 2], PS[3 + (h % 2) * 2]

                        def score(ci):
                            c = chunks[ci]
                            ps = PS[ci % 2]
                            S.mm(ps, ps.ap[:, :T], Ks, Ks.ap[:, kvh, c * 128:(c + 1) * 128], qs, qs.ap[:, h, :T], True, True)
                        score(0)
                        for ci in range(n):
                            if ci + 1 < n:
                                score(ci + 1)
                            c = chunks[ci]
                            ps = PS[ci % 2]
                            E = Eb[ci % 3]
                            S.act(E, E.ap[:, :T], ps, ps.ap[:, :T], AF.Exp, scale=scale)
                            S.mm(po, po.ap[:, :T], Vs, Vs.ap[:, c, kvh * 128:(kvh + 1) * 128], E, E.ap[:, :T], ci == 0, ci == n - 1)
                            S.mm(pd, pd.ap[:, :T], None, ones_bf, E, E.ap[:, :T], ci == 0, ci == n - 1)
                        r_ = rec[h % 2]
                        S.recip(r_, r_.ap[:, :T], pd, pd.ap[:, :T])
                        S.tt('dve', at, at.ap[:, h, :T], po, po.ap[:, :T], r_, r_.ap[:, :T], ALU.mult)

                for s in range(8):
                    def fn(buf, s=s, t0=t0, T=T, j=j):
                        wv = wview(buf, 0, KC, 256)
                        for ff in range(2):
                            f = 2 * s + ff
                            pw = PS[6 + f % 2]
                            for k in range(KC):
                                S.mm(pw, pw.ap[:, :T], buf, wv[:, k, ff * 128:(ff + 1) * 128], at, at.ap[:, k, :T], k == 0, k == KC - 1)
                            S.stt(xt, xt.ap[:, f, :T], pw, pw.ap[:, :T], mG(l, 0, f, j), xt, xt.ap[:, f, :T], ALU.mult, ALU.add)
                            if f == KC - 1:
                                S.dma('sync', XO3[:, :, t0:t0 + T], xt.ap[:, :, :T], reads=[xt])
                    jobs.append(dict(pre=pre if s == 0 else None, loads=[(Wt['ao', l], KC, s * 256, 256, 0)], fn=fn))
            run_stream(jobs, wb)
            S.barrier()


        def ret_phase(l, XI3, XO3):
            jl = l // 3
            A.reset()
            xts = [A.alloc('xt%d' % i, (KC, 512), F32) for i in range(2)]
            hTs = [A.alloc('hT%d' % i, (KC, 512), BF16) for i in range(2)]
            wb = [A.alloc('wb%d' % i, (8192,), BF16) for i in range(3)]
            sq = [A.alloc('sq%d' % i, (512,), BF16) for i in range(2)]
            tmp = [A.alloc('tmp%d' % i, (512,), F32) for i in range(2)]
            sd = A.alloc('sd', (512,), F32)
            rs = A.alloc('rs', (512,), F32)
            ob = [A.alloc('ob%d' % i, (512,), BF16) for i in range(4)]
            cnt = [0]

            def pl(ti):
                (t0, T, j) = mixer_tiles[ti]
                S.dma('sync', xts[ti % 2].ap[:, :, :T], XI3[:, :, t0:t0 + T], writes=[xts[ti % 2]])

            def pn(ti):
                (t0, T, j) = mixer_tiles[ti]
                norm_mod(xts[ti % 2], T, l, 0, j, hTs[ti % 2], sq, sd, rs, tmp)
            pl(0)
            pn(0)
            jobs = []
            for ti, (t0, T, j) in enumerate(mixer_tiles):
                hT = hTs[ti % 2]
                tj = []
                for (wname, dst, scl) in (('rq', QT, 1.0), ('rk', KT, 1.0 / 16)):
                    for s in range(8):
                        def fn(buf, s=s, t0=t0, T=T, dst=dst, scl=scl, hT=hT):
                            wv = wview(buf, 0, KC, 256)
                            for hh in range(2):
                                i = cnt[0]
                                cnt[0] += 1
                                pq = PS[i % 2]
                                for k in range(KC):
                                    S.mm(pq, pq.ap[:, :T], buf, wv[:, k, hh * 128:(hh + 1) * 128], hT, hT.ap[:, k, :T], k == 0, k == KC - 1)
                                o_ = ob[i % 4]
                                S.act(o_, o_.ap[:, :T], pq, pq.ap[:, :T], AF.Copy, scale=scl)
                                S.dma('sync', dst[2 * s + hh, :, t0:t0 + T], o_.ap[:, :T], reads=[o_])
                        tj.append(dict(loads=[(Wt[wname, l], KC, s * 256, 256, 0)], fn=fn))
                for (wname, dst, nsl, func, scl) in (('rk', KTOK, 4, AF.Copy, 1.0 / 16), ('rv', VTOK, 8, AF.Copy, 1.0),
                                                     ('rg', GTOK, 8, AF.Silu, 1.0)):
                    for s in range(nsl):
                        def fn(buf, s=s, t0=t0, T=T, dst=dst, func=func, scl=scl, hT=hT):
                            wv = wview(buf, 0, KC, 512)
                            for c in range(T // 128):
                                i = cnt[0]
                                cnt[0] += 1
                                pv = PS[2 + i % 2]
                                for k in range(KC):
                                    S.mm(pv, pv.ap, hT, hT.ap[:, k, c * 128:(c + 1) * 128], buf, wv[:, k, :], k == 0, k == KC - 1)
                                o_ = ob[i % 4]
                                S.act(o_, o_.ap, pv, pv.ap, func, scale=scl)
                                S.dma('sync', dst[t0 // 128 + c, :, s * 512:(s + 1) * 512], o_.ap, reads=[o_])
                        tj.append(dict(loads=[(Wt[wname, l], KC, s * 512, 512, 0)], fn=fn))
                if ti + 1 < len(mixer_tiles):
                    tj[0]['pre'] = (lambda ti=ti: pl(ti + 1))
                    tj[len(tj) // 2]['pre'] = (lambda ti=ti: pn(ti + 1))
                jobs.extend(tj)
            run_stream(jobs, wb)
            S.barrier()
            QT3, KT3, ATS3 = hm(QT), hm(KT), hm(ATS)
            KTOK3, VTOK3 = hm(KTOK), hm(VTOK)

            def head_consts(hc, h):
                lgf = lgT.ap[:, jl * 16 + h:jl * 16 + h + 1]
                lgb = lgT.ap[:, jl * 16 + 8 + h:jl * 16 + 8 + h + 1]
                S.act(hc, hc.ap[:, 0:1], None, cst('p1', 1), AF.Exp, scale=lgf)
                S.act(hc, hc.ap[:, 1:2], None, cst('cmp', 1), AF.Exp, scale=lgb)
                S.act(hc, hc.ap[:, 2:3], None, cst('c1p', 1), AF.Exp, scale=lgf)
                S.act(hc, hc.ap[:, 3:4], None, cst('p0', 1), AF.Exp, scale=lgb)
                S.act(hc, hc.ap[:, 4:5], None, lgf, AF.Exp, scale=128.0)
                S.act(hc, hc.ap[:, 5:6], None, lgb, AF.Exp, scale=128.0)
                return lgf, lgb

            A.reset()
            Kfs = [A.alloc('Kf%d' % i, (NCH, 256), BF16) for i in range(2)]
            Vs_ = [A.alloc('V%d' % i, (NCH, 512), BF16) for i in range(2)]
            stall = [A.alloc('stall%d' % i, (NCH, 2, 512), BF16) for i in range(2)]
            Sms = [A.alloc('Sm%d' % i, (2, 512), F32) for i in range(2)]
            hcs = [A.alloc('hc%d' % i, (8,), F32) for i in range(2)]
            for h in range(8):
                Kf, V, sta, hc = Kfs[h % 2], Vs_[h % 2], stall[h % 2], hcs[h % 2]
                S.dma('sync', Kf.ap, KTOK3[:, :, h * 256:(h + 1) * 256], writes=[Kf])
                S.dma('sync', V.ap, VTOK3[:, :, h * 512:(h + 1) * 512], writes=[V])
                head_consts(hc, h)
                kdf, cdf = hc.ap[:, 2:3], hc.ap[:, 4:5]
                S.act(Kf, Kf.ap, Kf, Kf.ap, AF.Copy, scale=kdf, rd=(hc,))
                Sm = Sms[0]
                S.ms('pool', Sm, Sm.ap, 0.0)
                for c in range(NCH):
                    Sn = Sms[(c + 1) % 2]
                    S.cp('act', sta, sta.ap[:, c], Sm, Sm.ap)
                    for kc in range(2):
                        pb_ = PS[(2 * c + kc) % 4]
                        S.mm(pb_, pb_.ap, Kf, Kf.ap[:, c, kc * 128:(kc + 1) * 128], V, V.ap[:, c, :], True, True)
                    for kc in range(2):
                        pb_ = PS[(2 * c + kc) % 4]
                        S.stt(Sn, Sn.ap[:, kc, :], Sm, Sm.ap[:, kc, :], cdf, pb_, pb_.ap, ALU.mult, ALU.add, rd=(hc,))
                    Sm = Sn
                S.dma('sync', hm(SBD[h]).rearrange("p c (a b) -> p c a b", a=2), sta.ap, reads=[sta])
                S.dma('sync', E_in[h // 4][h % 4].rearrange("p (a b) -> p a b", a=2), Sm.ap, reads=[Sm])
            S.barrier()
            exchange(E_in[0], E_out[0])
            exchange(E_in[1], E_out[1])
            A.reset()
            q = A.alloc('q', (2, NT), BF16)
            qf = A.alloc('qf', (2, NT), BF16)
            qb = A.alloc('qb', (2, NT), BF16)
            k_ = A.alloc('k', (2, NT), BF16)
            Kb = A.alloc('Kb', (NCH, 256), BF16)
            V = A.alloc('V', (NCH, 512), BF16)
            SBr = [A.alloc('SB%d' % i, (2, 512), BF16) for i in range(4)]
            oo = A.alloc('oo', (NCH, 512), F32)
            atb = A.alloc('atb', (4, NT), BF16)
            Sms = [A.alloc('Sm%d' % i, (2, 512), F32) for i in range(2)]
            Scs = [A.alloc('Sc%d' % i, (2, 512), BF16) for i in range(4)]
            e2 = A.alloc('e2', (2, 2, 512), F32)
            gb = [A.alloc('gb%d' % i, (512,), BF16) for i in range(4)]
            m1 = A.alloc('m1', (128,), F32)
            m2 = A.alloc('m2', (128,), F32)
            Mc = A.alloc('Mc', (128,), F32)
            patf = A.alloc('patf', (128,), F32)
            patb = A.alloc('patb', (128,), F32)
            hc = A.alloc('hc', (8,), F32)
            scb = [A.alloc('scb%d' % i, (128,), BF16) for i in range(3)]
            yb = [A.alloc('yb%d' % i, (512,), BF16) for i in range(4)]
            st6 = A.alloc('st6', (NCH, 6), F32)
            mv = A.alloc('mv', (NCH, 2), F32)
            rsd = A.alloc('rsd', (NCH,), F32)
            nmr = A.alloc('nmr', (NCH,), F32)
            order_b = list(range(NCH - 1, 1, -1)) + [1, 0]
            cmin = 2 if l == DEPTH - 1 else 0
            outs = [c for c in order_b if c >= cmin]

            def q4(t):
                return t.ap.rearrange("p a (c n) -> p a c n", n=128)

            def pat4(t):
                return t.ap.unsqueeze(1).unsqueeze(1).to_broadcast([128, 2, NCH, 128])
            def head_loads(h):
                S.dma('sync', q.ap, QT3[:, 2 * h:2 * h + 2, :], writes=[q])
                S.dma('sync', k_.ap, KT3[:, 2 * h:2 * h + 2, :], writes=[k_])
                S.dma('sync', Kb.ap, KTOK3[:, :, h * 256:(h + 1) * 256], writes=[Kb])
                S.dma('sync', V.ap, VTOK3[:, :, h * 512:(h + 1) * 512], writes=[V])
                S.dma('sync', e2.ap, E_out[h // 4][:, h % 4].rearrange("r p (a b) -> p r a b", a=2), writes=[e2])
            head_loads(0)
            for h in range(8):
                lgf, lgb = head_consts(hc, h)
                S.act(m1, m1.ap, None, cst('d1'), AF.Exp, scale=lgf)
                S.tt('dve', m1, m1.ap, m1, m1.ap, None, cst('u'), ALU.mult)
                S.act(m2, m2.ap, None, cst('d2'), AF.Exp, scale=lgb)
                S.tt('dve', m2, m2.ap, m2, m2.ap, None, cst('l'), ALU.mult)
                S.tt('dve', Mc, Mc.ap, m1, m1.ap, m2, m2.ap, ALU.add)
                S.act(patf, patf.ap, None, cst('r1'), AF.Exp, scale=lgf)
                S.act(patb, patb.ap, None, cst('r2'), AF.Exp, scale=lgb)
                S.tt('dve', qf, q4(qf), q, q4(q), patf, pat4(patf), ALU.mult)
                S.tt('dve', qb, q4(qb), q, q4(q), patb, pat4(patb), ALU.mult)
                qdf, qdb, kdf, kdb, cdf, cdb = (hc.ap[:, i:i + 1] for i in range(6))
                S.act(Kb, Kb.ap, Kb, Kb.ap, AF.Copy, scale=kdb, rd=(hc,))
                Sm = Sms[0]
                mi = 0
                S.ts('dve', Sm, Sm.ap, e2, e2.ap[:, 0], msk0, None, ALU.mult)
                S.stt(Sm, Sm.ap, e2, e2.ap[:, 1], msk1, Sm, Sm.ap, ALU.mult, ALU.add)
                si = 0
                S.cp('act', Scs[0], Scs[0].ap, Sm, Sm.ap)

                def emit_pb(c, po, Sc):
                    csl = slice(c * 128, (c + 1) * 128)
                    for kc in range(2):
                        S.mm(po, po.ap, qb, qb.ap[:, kc, csl], Sc, Sc.ap[:, kc, :], False, kc == 1)
                    S.cp('act', oo, oo.ap[:, c, :], po, po.ap)

                prev = None
                for i, c in enumerate(order_b):
                    if c == 1:
                        S.ms('pool', Sm, Sm.ap, 0.0)
                        si += 1
                        S.ms('pool', Scs[si % 4], Scs[si % 4].ap, 0.0)
                    Sc = Scs[si % 4]
                    csl = slice(c * 128, (c + 1) * 128)
                    want = c >= cmin
                    upd = c not in (2, 0)
                    pss = PS[0]
                    po = PS[3 + i % 4]
                    if want:
                        sb_ = SBr[i % 4]
                        S.dma('sync', sb_.ap, SBD[h, c].rearrange("p (a b) -> p a b", a=2), writes=[sb_])
                        for kc in range(2):
                            S.mm(pss, pss.ap[:, :128], k_, k_.ap[:, kc, csl], q, q.ap[:, kc, csl], kc == 0, kc == 1)
                        sc_ = scb[i % 3]
                        S.tt('dve', sc_, sc_.ap, pss, pss.ap[:, :128], Mc, Mc.ap, ALU.mult)
                    if upd:
                        for kc in range(2):
                            S.mm(PS[1 + kc], PS[1 + kc].ap, Kb, Kb.ap[:, c, kc * 128:(kc + 1) * 128], V, V.ap[:, c, :], True, True)
                    if want:
                        for kc in range(2):
                            S.mm(po, po.ap, qf, qf.ap[:, kc, csl], sb_, sb_.ap[:, kc, :], kc == 0, False)
                        S.mm(po, po.ap, sc_, sc_.ap, V, V.ap[:, c, :], False, False)
                    if prev is not None:
                        emit_pb(*prev)
                    if upd:
                        mi += 1
                        Sn = Sms[mi % 2]
                        for kc in range(2):
                            S.stt(Sn, Sn.ap[:, kc, :], Sm, Sm.ap[:, kc, :], cdb, PS[1 + kc], PS[1 + kc].ap, ALU.mult, ALU.add, rd=(hc,))
                        Sm = Sn
                        si += 1
                        S.cp('act', Scs[si % 4], Scs[si % 4].ap, Sm, Sm.ap)
                    prev = (c, po, Sc) if want else None
                if prev is not None:
                    emit_pb(*prev)
                if h + 1 < 8:
                    head_loads(h + 1)
                for c in outs:
                    S.op('dve', lambda e, a=st6.ap[:, c, :], b=oo.ap[:, c, :]: e.bn_stats(a, b), reads=[oo], writes=[st6])
                    S.op('dve', lambda e, a=mv.ap[:, c, :], b=st6.ap[:, c, :]: e.bn_aggr(a, b), reads=[st6], writes=[mv])
                S.act(rsd, rsd.ap[:, cmin:NCH], mv, mv.ap[:, cmin:NCH, 1], AF.Sqrt, bias=eps_ln, scale=1.0)
                S.recip(rsd, rsd.ap[:, cmin:NCH], rsd, rsd.ap[:, cmin:NCH])
                S.stt(nmr, nmr.ap[:, cmin:NCH], mv, mv.ap[:, cmin:NCH, 0], -1.0, rsd, rsd.ap[:, cmin:NCH], ALU.mult, ALU.mult)

                def s3(i):
                    c = outs[i]
                    g_ = gb[i % 4]
                    S.dma('sync', g_.ap, GTOK[c, :, h * 512:(h + 1) * 512], writes=[g_])
                    S.act(oo, oo.ap[:, c, :], oo, oo.ap[:, c, :], AF.Identity, bias=nmr.ap[:, c:c + 1], scale=rsd.ap[:, c:c + 1], rd=(nmr, rsd))
                    yb_ = yb[i % 4]
                    S.tt('dve', yb_, yb_.ap, oo, oo.ap[:, c, :], g_, g_.ap, ALU.mult)

                def s4(i):
                    c = outs[i]
                    yb_ = yb[i % 4]
                    pt = PS[1 + i % 2]
                    ptv = pt.ap[:, 0:256].bitcast(BF16)
                    for vc in range(4):
                        S.tr(pt, ptv[:, vc * 128:(vc + 1) * 128], yb_, yb_.ap[:, vc * 128:(vc + 1) * 128], ident_bf)
                    S.tt('dve', atb, atb.ap[:, :, c * 128:(c + 1) * 128], pt, ptv.rearrange("p (a b) -> p a b", a=4),
                         None, sv('gnw', jl * 32 + h * 4, 4).unsqueeze(2).to_broadcast([128, 4, 128]), ALU.mult)
                s3(0)
                for i in range(len(outs)):
                    if i + 1 < len(outs):
                        s3(i + 1)
                    s4(i)
                S.dma('sync', ATS3[:, 4 * h:4 * h + 4, cmin * 128:NT], atb.ap[:, :, cmin * 128:NT], reads=[atb])
            S.barrier()
            A.reset()
            tl = [t for t in mixer_tiles if not (l == DEPTH - 1 and t[2] == 1)]
            half = (len(tl) + 1) // 2
            groups = [tl[:half], tl[half:]]
            gmax = max(sum(t[1] for t in g) for g in groups)
            atg = A.alloc('atg', (32, gmax), BF16)
            wb = [A.alloc('wb%d' % i, (8192,), BF16) for i in range(3)]
            xo = [A.alloc('xo%d' % i, (2, 512), F32) for i in range(4)]
            xi = [0]
            for g in groups:
                if not g:
                    continue
                offs = []
                o = 0
                for (t0, T, j) in g:
                    S.dma('sync', atg.ap[:, :, o:o + T], ATS3[:, :, t0:t0 + T], writes=[atg])
                    offs.append(o)
                    o += T
                jobs = []
                for s_ in range(8):
                    def fn(buf, s_=s_, g=g, offs=offs):
                        wv = wview(buf, 0, 32, 256)
                        for (t0, T, j), o in zip(g, offs):
                            x_ = xo[xi[0] % 4]
                            xi[0] += 1
                            S.dma('sync', x_.ap[:, :, :T], XI3[:, 2 * s_:2 * s_ + 2, t0:t0 + T], writes=[x_])
                            for ff in range(2):
                                f = 2 * s_ + ff
                                pw = PS[(xi[0] * 2 + ff) % 4]
                                for k in range(32):
                                    S.mm(pw, pw.ap[:, :T], buf, wv[:, k, ff * 128:(ff + 1) * 128], atg, atg.ap[:, k, o:o + T], k == 0, k == 31)
                                S.stt(x_, x_.ap[:, ff, :T], pw, pw.ap[:, :T], mG(l, 0, f, j), x_, x_.ap[:, ff, :T], ALU.mult, ALU.add)
                            S.dma('sync', XO3[:, 2 * s_:2 * s_ + 2, t0:t0 + T], x_.ap[:, :, :T], reads=[x_])
                    jobs.append(dict(loads=[(Wt['ro', l], 32, s_ * 256, 256, 0)], fn=fn))
                run_stream(jobs, wb)
            S.barrier()

        PHASES = {'cnv': cnv_phase, 'att': att_phase, 'ret': ret_phase}
        cur, nxt = (XA, XA3), (XB, XB3)
        for li, l in enumerate(layers):
            if do_mix and MIXER[l] in PHASES:
                PHASES[MIXER[l]](l, cur[1], nxt[1])
            else:
                copy_phase(cur[0], nxt[0])
            cur, nxt = nxt, cur
            lastl = (li == len(layers) - 1)
            if do_ffn:
                halo_exchange(cur[1])
                ffn_phase(l, cur[1], nxt[1], lastl)
                cur, nxt = nxt, cur
            elif lastl:
                S.dma('sync', outT, cur[0][:, CTX:NT], final=True)
        S.finish()
    return nc


def _weights_for(inp, layers, do_mix=True, do_ffn=True):
    m = {}
    for l in layers:
        if do_ffn:
            m["wg%d" % l] = inp['ffn_w_gate'][l]
            m["wu%d" % l] = inp['ffn_w_up'][l]
            m["wd%d" % l] = inp['ffn_w_down'][l]
        if not do_mix:
            continue
        if MIXER[l] == 'ret':
            j = l // 3
            m["rq%d" % l] = inp['ret_wq'][j]
            m["rk%d" % l] = inp['ret_wk'][j]
            m["rv%d" % l] = inp['ret_wv'][j]
            m["rg%d" % l] = inp['ret_wg'][j]
            m["ro%d" % l] = inp['ret_wo'][j]
            m["gn%d" % l] = inp['ret_gn_w'][j].reshape(1, HV)
        elif MIXER[l] == 'att':
            m["aq%d" % l] = inp['att_wq'][0]
            m["akv%d" % l] = inp['att_wkv'][0]
            m["ao%d" % l] = inp['att_wo'][0]
        else:
            m["c1_%d" % l] = inp['cnv_w1'][0]
            m["c2_%d" % l] = inp['cnv_w2'][0]
    return {k: np.ascontiguousarray(np.asarray(v, np.float32)) for k, v in m.items()}


def run(inp, nlat=4, layers=(0, 1, 2, 3), do_mix=True, do_ffn=True, nb=4):
    inp = {k: np.asarray(v) for k, v in inp.items()}
    L = nlat * 512
    ncores = 2 * nb
    prog = build(nlat, layers, do_mix, do_ffn, ncores)
    fw = (96 // ncores) * 128
    cst = make_consts()
    wts = _weights_for(inp, layers, do_mix, do_ffn)
    in_maps = []
    for b in range(nb):
        for r in range(2):
            if r == 0:
                tpos = np.arange(L)
                xc, xl = inp['ctx'][b], inp['x'][b][:L]
            else:
                tpos = np.arange(2 * L - 1, L - 1, -1)
                xc, xl = inp['ctx'][b][::-1], inp['x'][b][L:2 * L][::-1]
            rc, rs_ = make_rope(tpos)
            xT = np.concatenate([xc.T, xl.T], axis=1)
            m = {'xT': np.ascontiguousarray(xT, dtype=np.float32), 'sv': pack_sv(inp, b, r), 'cst': cst,
                 'rope_cos': rc, 'rope_sin': rs_}
            m.update(wts)
            ci = 2 * b + r
            for l in layers:
                m["mw%d" % l] = np.ascontiguousarray(inp['mod_w'][l][:, ci * fw:(ci + 1) * fw], dtype=np.float32)
            in_maps.append(m)
    res = run_bass_kernel_spmd(prog, in_maps, core_ids=list(range(2 * nb)))
    outs = []
    for b in range(nb):
        o0 = res.results[2 * b]['outT'].T
        o1 = res.results[2 * b + 1]['outT'].T[::-1]
        outs.append(np.concatenate([o0, o1], axis=0))
    return np.ascontiguousarray(np.stack(outs))


def kernel(**inputs):
    return run(inputs).astype(np.float32)
```
